# Optimizing a Trainium2 kernel written in Bass

```python
import math
import jax, jax.numpy as jnp
from jax import lax
import numpy as np

D_MODEL = 1024
BATCH = 8
SEQ = 2048
DEPTH = 1
DEC_BATCH = 128
DEC_SEQ = 4
PAST_LEN = 16384
PAGE_SIZE = 128

N_META = 16
HG_HEADS = 8
HG_KDIM = 128
HG_VDIM = 128
HG_QK = HG_HEADS * HG_KDIM
HG_V = HG_HEADS * HG_VDIM
HG_CHUNK = 16
SSM_W = 1024
SSM_GROUP = 16
SSM_G = SSM_W // SSM_GROUP
SSM_P = 64
SSM_MIN_RE = -1e-4
STEP_MIN = 0.001
STEP_MAX = 0.1
IN_COLS = 2 * HG_QK + 2 * HG_V + SSM_W + 2 * D_MODEL
PEER_HEADS = 8
PEER_NKEYS = 128
PEER_EXPERTS = PEER_NKEYS * PEER_NKEYS
PEER_DK = 256
PEER_TOPK = 16
PEER_BLOCK = 128
EPS = 1e-6

kernel_name = "hgrn2_s5_peer_hybrid_step"


def rmsnorm(x, w):
    xf = x.astype(jnp.float32)
    y = xf * lax.rsqrt(jnp.mean(xf * xf, axis=-1, keepdims=True) + EPS)
    return (y * w.astype(jnp.float32)).astype(x.dtype)


def hgrn2_scan(q, k, v, lg, s0):
    n, l, h, kd = q.shape
    vd = v.shape[-1]
    c = math.gcd(l, HG_CHUNK)
    nc = l // c

    def to_chunks(a):
        return jnp.moveaxis(a.reshape(n, nc, c, *a.shape[2:]), 1, 0)

    causal = jnp.tril(jnp.ones((c, c), dtype=bool))[None, :, :, None, None]

    def step(s, blk):
        qc, kc, vc, lgc = blk
        b = jnp.cumsum(lgc, axis=1)
        blast = b[:, -1]
        o_inter = jnp.einsum('nthk,nhkv->nthv', qc * jnp.exp(b), s)
        decay = jnp.exp(jnp.where(causal, b[:, :, None] - b[:, None, :], -jnp.inf))
        att = jnp.einsum('nthk,nshk,ntshk->nhts', qc, kc, decay)
        o_intra = jnp.einsum('nhts,nshv->nthv', att, vc)
        s_new = jnp.exp(blast)[..., None] * s + jnp.einsum(
            'nshk,nshv->nhkv', kc * jnp.exp(blast[:, None] - b), vc)
        return s_new, o_inter + o_intra

    s_fin, o = lax.scan(step, s0, (to_chunks(q), to_chunks(k), to_chunks(v), to_chunks(lg)))
    o = jnp.moveaxis(o, 0, 1).reshape(n, l, h, vd)
    return o, s_fin


def s5_scan(u, x0_re, x0_im, a_re, a_im, log_step, b_re, b_im, c_re, c_im, d_skip):
    n, l, _ = u.shape
    f32 = jnp.float32
    ug = u.astype(f32).reshape(n, l, SSM_G, SSM_GROUP)
    lam = lax.complex(jnp.minimum(a_re.astype(f32), SSM_MIN_RE), a_im.astype(f32))
    delta = jnp.exp(log_step.astype(f32))[:, None]
    a_bar = jnp.exp(lam * delta)
    b_bar = ((a_bar - 1.0) / lam)[..., None] * lax.complex(b_re.astype(f32), b_im.astype(f32))
    bu = jnp.einsum('gpc,nlgc->nlgp', b_bar, ug.astype(jnp.complex64))
    x0 = lax.complex(x0_re.astype(f32), x0_im.astype(f32))
    bu = bu.at[:, 0].add(a_bar * x0)
    a_seq = jnp.broadcast_to(a_bar, bu.shape)

    def combine(e1, e2):
        a1, b1 = e1
        a2, b2 = e2
        return a1 * a2, a2 * b1 + b2

    _, xs = lax.associative_scan(combine, (a_seq, bu), axis=1)
    cc = lax.complex(c_re.astype(f32), c_im.astype(f32))
    y = jnp.real(jnp.einsum('gcp,nlgp->nlgc', cc, xs)) + d_skip.astype(f32).reshape(SSM_G, SSM_GROUP) * ug
    x_last = xs[:, -1]
    return y.reshape(n, l, SSM_W), jnp.real(x_last), jnp.imag(x_last)


def peer(h, wq, k1, k2, u_tab, v_tab):
    n, l, d = h.shape
    f32 = jnp.float32
    t = n * l
    nb = -(-t // PEER_BLOCK)
    hf = jnp.pad(h.reshape(t, d), ((0, nb * PEER_BLOCK - t), (0, 0))).reshape(nb, PEER_BLOCK, d)

    def block(xb):
        qry = (xb @ wq).astype(f32).reshape(PEER_BLOCK, PEER_HEADS, 2, PEER_DK // 2)
        s1 = jnp.einsum('thd,hnd->thn', qry[:, :, 0], k1.astype(f32))
        s2 = jnp.einsum('thd,hnd->thn', qry[:, :, 1], k2.astype(f32))
        v1, i1 = lax.top_k(s1, PEER_TOPK)
        v2, i2 = lax.top_k(s2, PEER_TOPK)
        cand = (v1[..., :, None] + v2[..., None, :]).reshape(PEER_BLOCK, PEER_HEADS, PEER_TOPK * PEER_TOPK)
        sc, pos = lax.top_k(cand, PEER_TOPK)
        e1 = jnp.take_along_axis(i1, pos // PEER_TOPK, axis=-1)
        e2 = jnp.take_along_axis(i2, pos % PEER_TOPK, axis=-1)
        expert = e1 * PEER_NKEYS + e2
        gate = jax.nn.softmax(sc, axis=-1)
        act = jax.nn.gelu(jnp.einsum('thkd,td->thk', u_tab[expert].astype(f32), xb.astype(f32)),
                          approximate=False)
        return jnp.einsum('thk,thkd->td', gate * act, v_tab[expert].astype(f32)).astype(h.dtype)

    out = lax.map(block, hf).reshape(nb * PEER_BLOCK, d)[:t]
    return out.reshape(n, l, d)


def layer(x, s_hg, ssm_re, ssm_im, lb, norm1_w, w_in, g_norm_w, ssm_a_re, ssm_a_im, ssm_log_step,
          ssm_b_re, ssm_b_im, ssm_c_re, ssm_c_im, ssm_d, w_glu, b_glu, w_branch_a, w_branch_b, w_out,
          norm2_w, peer_wq, peer_k1, peer_k2, peer_u, peer_v):
    dt = x.dtype
    f32 = jnp.float32
    n, l, _ = x.shape
    h = rmsnorm(x, norm1_w)
    proj = h @ w_in
    sizes = (HG_QK, HG_QK, HG_V, HG_V, SSM_W, D_MODEL, D_MODEL)
    offs = np.cumsum(sizes)[:-1].tolist()
    q, f, i, g, u, ga, gb = jnp.split(proj, offs, axis=-1)
    qh = jax.nn.silu(q.astype(f32)).reshape(n, l, HG_HEADS, HG_KDIM)
    fg = (lb + (1.0 - lb) * jax.nn.sigmoid(f.astype(f32))).reshape(n, l, HG_HEADS, HG_KDIM)
    kh = 1.0 - fg
    lg = jnp.log(fg)
    vh = i.astype(f32).reshape(n, l, HG_HEADS, HG_VDIM)
    o, s_new = hgrn2_scan(qh, kh, vh, lg, s_hg.astype(f32))
    o = rmsnorm(o, g_norm_w) * jax.nn.silu(g.astype(f32)).reshape(n, l, HG_HEADS, HG_VDIM)
    br_a = o.reshape(n, l, HG_V).astype(dt) @ w_branch_a
    y, x_re, x_im = s5_scan(u, ssm_re, ssm_im, ssm_a_re, ssm_a_im, ssm_log_step,
                            ssm_b_re, ssm_b_im, ssm_c_re, ssm_c_im, ssm_d)
    y = jax.nn.gelu(y, approximate=False)
    y = y * jax.nn.sigmoid(y @ w_glu.astype(f32) + b_glu.astype(f32))
    br_b = y.astype(dt) @ w_branch_b
    mixed = jax.nn.sigmoid(ga) * br_a + jax.nn.sigmoid(gb) * br_b
    x = x + mixed @ w_out
    x = x + peer(rmsnorm(x, norm2_w), peer_wq, peer_k1, peer_k2, peer_u, peer_v)
    sd = s_hg.dtype
    return x, s_new.astype(sd), x_re.astype(ssm_re.dtype), x_im.astype(ssm_im.dtype)


def setup_inputs(seed: int = 0) -> dict:
    key = jax.random.key(seed)
    ks = jax.random.split(key, 32)
    f32 = jnp.float32
    nrm = lambda k, shape, s: jax.random.normal(k, shape, f32) * s
    inp = {}
    inp['x_prompt'] = nrm(ks[0], (BATCH, SEQ, D_MODEL), 1.0)
    inp['x_sample'] = nrm(ks[1], (DEC_BATCH, DEC_SEQ, D_MODEL), 1.0)
    inp['state_hgrn'] = nrm(ks[2], (DEPTH, DEC_BATCH, HG_HEADS, HG_KDIM, HG_VDIM), 0.5)
    inp['state_ssm_re'] = nrm(ks[3], (DEPTH, DEC_BATCH, SSM_G, SSM_P), 0.3)
    inp['state_ssm_im'] = nrm(ks[4], (DEPTH, DEC_BATCH, SSM_G, SSM_P), 0.3)
    inp['meta_tokens'] = nrm(ks[5], (N_META, D_MODEL), 1.0)
    inp['lower_bounds'] = nrm(ks[6], (DEPTH + 1, HG_QK), 0.1)
    inp['norm1_w'] = 1.0 + nrm(ks[7], (DEPTH, D_MODEL), 0.01)
    inp['w_in'] = nrm(ks[8], (DEPTH, D_MODEL, IN_COLS), D_MODEL ** -0.5)
    inp['g_norm_w'] = 1.0 + nrm(ks[9], (DEPTH, HG_VDIM), 0.01)
    inp['ssm_a_re'] = -0.5 + nrm(ks[10], (DEPTH, SSM_G, SSM_P), 0.01)
    inp['ssm_a_im'] = jnp.pi * jnp.arange(SSM_P, dtype=f32) + nrm(ks[11], (DEPTH, SSM_G, SSM_P), 0.01)
    inp['ssm_log_step'] = jax.random.uniform(ks[12], (DEPTH, SSM_G), f32,
                                             minval=math.log(STEP_MIN), maxval=math.log(STEP_MAX))
    inp['ssm_b_re'] = nrm(ks[13], (DEPTH, SSM_G, SSM_P, SSM_GROUP), (2.0 * SSM_GROUP) ** -0.5)
    inp['ssm_b_im'] = nrm(ks[14], (DEPTH, SSM_G, SSM_P, SSM_GROUP), (2.0 * SSM_GROUP) ** -0.5)
    inp['ssm_c_re'] = nrm(ks[15], (DEPTH, SSM_G, SSM_GROUP, SSM_P), 0.5)
    inp['ssm_c_im'] = nrm(ks[16], (DEPTH, SSM_G, SSM_GROUP, SSM_P), 0.5)
    inp['ssm_d'] = nrm(ks[17], (DEPTH, SSM_W), 1.0)
    inp['w_glu'] = nrm(ks[18], (DEPTH, SSM_W, SSM_W), SSM_W ** -0.5)
    inp['b_glu'] = nrm(ks[19], (DEPTH, SSM_W), 0.01)
    inp['w_branch_a'] = nrm(ks[20], (DEPTH, HG_V, D_MODEL), HG_V ** -0.5)
    inp['w_branch_b'] = nrm(ks[21], (DEPTH, SSM_W, D_MODEL), SSM_W ** -0.5)
    inp['w_out'] = nrm(ks[22], (DEPTH, D_MODEL, D_MODEL), D_MODEL ** -0.5)
    inp['norm2_w'] = 1.0 + nrm(ks[23], (DEPTH, D_MODEL), 0.01)
    inp['peer_wq'] = nrm(ks[24], (DEPTH, D_MODEL, PEER_HEADS * PEER_DK), D_MODEL ** -0.5)
    inp['peer_k1'] = nrm(ks[25], (DEPTH, PEER_HEADS, PEER_NKEYS, PEER_DK // 2), (PEER_DK // 2) ** -0.5)
    inp['peer_k2'] = nrm(ks[26], (DEPTH, PEER_HEADS, PEER_NKEYS, PEER_DK // 2), (PEER_DK // 2) ** -0.5)
    inp['peer_u'] = nrm(ks[27], (DEPTH, PEER_EXPERTS, D_MODEL), D_MODEL ** -0.5)
    inp['peer_v'] = nrm(ks[28], (DEPTH, PEER_EXPERTS, D_MODEL), 0.5 * PEER_HEADS ** -0.5)
    inp['final_norm_w'] = 1.0 + nrm(ks[29], (D_MODEL,), 0.01)
    return inp


def reference(x_prompt, x_sample, state_hgrn, state_ssm_re, state_ssm_im, meta_tokens, lower_bounds,
              norm1_w, w_in, g_norm_w, ssm_a_re, ssm_a_im, ssm_log_step, ssm_b_re, ssm_b_im,
              ssm_c_re, ssm_c_im, ssm_d, w_glu, b_glu, w_branch_a, w_branch_b, w_out, norm2_w,
              peer_wq, peer_k1, peer_k2, peer_u, peer_v, final_norm_w):
    lb_all = jnp.cumsum(jax.nn.softmax(lower_bounds.astype(jnp.float32), axis=0), axis=0)
    nb = x_prompt.shape[0]
    meta = jnp.broadcast_to(meta_tokens[None].astype(x_prompt.dtype), (nb, N_META, D_MODEL))
    hp = jnp.concatenate([meta, x_prompt], axis=1)
    hs = x_sample
    hg0 = jnp.zeros((nb, HG_HEADS, HG_KDIM, HG_VDIM), state_hgrn.dtype)
    re0 = jnp.zeros((nb, SSM_G, SSM_P), state_ssm_re.dtype)
    im0 = jnp.zeros((nb, SSM_G, SSM_P), state_ssm_im.dtype)
    p_hg, p_re, p_im, s_hg, s_re, s_im = [], [], [], [], [], []
    for li in range(DEPTH):
        params = (norm1_w[li], w_in[li], g_norm_w[li], ssm_a_re[li], ssm_a_im[li], ssm_log_step[li],
                  ssm_b_re[li], ssm_b_im[li], ssm_c_re[li], ssm_c_im[li], ssm_d[li], w_glu[li], b_glu[li],
                  w_branch_a[li], w_branch_b[li], w_out[li], norm2_w[li], peer_wq[li], peer_k1[li],
                  peer_k2[li], peer_u[li], peer_v[li])
        hp, a1, a2, a3 = layer(hp, hg0, re0, im0, lb_all[li], *params)
        hs, b1, b2, b3 = layer(hs, state_hgrn[li], state_ssm_re[li], state_ssm_im[li], lb_all[li], *params)
        p_hg.append(a1); p_re.append(a2); p_im.append(a3)
        s_hg.append(b1); s_re.append(b2); s_im.append(b3)
    y_prompt = rmsnorm(hp[:, N_META:], final_norm_w)
    y_sample = rmsnorm(hs, final_norm_w)
    return (y_prompt, y_sample, jnp.stack(p_hg), jnp.stack(p_re), jnp.stack(p_im),
            jnp.stack(s_hg), jnp.stack(s_re), jnp.stack(s_im))
```

```python
import os
import numpy as np
import concourse.bass as bass
import concourse.mybir as mybir
from concourse.bass_utils import run_bass_kernel_spmd

F32 = mybir.dt.float32
F32R = mybir.dt.float32r
I32 = mybir.dt.int32
U32 = mybir.dt.uint32
BF16 = mybir.dt.bfloat16
ALU = mybir.AluOpType
AF = mybir.ActivationFunctionType
AX = mybir.AxisListType


class Res:
    __slots__ = ("name", "w", "r")

    def __init__(self, name):
        self.name = name
        self.w = None
        self.r = {}


import types


def _freeze(fn):
    if fn.__closure__ is None:
        return fn
    cells = tuple(types.CellType(c.cell_contents) for c in fn.__closure__)
    return types.FunctionType(fn.__code__, fn.__globals__, fn.__name__, fn.__defaults__, cells)


class Prog:
    CE = ("pe", "act", "dve", "pool")

    def __init__(self, nc, n_dma_sems=12):
        self.nc = nc
        self.q = {e: [] for e in ("pe", "act", "dve", "pool", "sp")}
        self.cnt = {e: 0 for e in self.CE}
        self.waited = {e: {} for e in self.q}
        self.sems = {}
        self.dma_pool = {}
        self.dma_next = {}
        self.dma_tgt = {}
        self.n_dma_sems = n_dma_sems
        self.out_events = []
        self.n_instr = 0

    def alloc_sems(self, stack):
        nc = self.nc
        for e in self.CE:
            self.sems[("e", e)] = stack.enter_context(nc.semaphore("sem_" + e))
        for qn in ("sp", "pool", "act"):
            self.dma_pool[qn] = []
            for i in range(self.n_dma_sems):
                k = ("d", qn, i)
                self.sems[k] = stack.enter_context(nc.semaphore("dsem_%s_%d" % (qn, i)))
                self.dma_pool[qn].append(k)
                self.dma_tgt[k] = 0
            self.dma_next[qn] = 0

    def _need(self, eng, ev):
        if ev is None:
            return
        key, val = ev
        if key == ("e", "pe") and eng == "pe":
            return
        if self.waited[eng].get(key, 0) >= val:
            return
        self.waited[eng][key] = val
        sem = self.sems[key]
        self.q[eng].append(lambda e, s=sem, v=val: e.wait_ge(s, v))

    def _deps(self, eng, r, w):
        for x in r:
            self._need(eng, x.w)
        for x in w:
            self._need(eng, x.w)
            for k_, v_ in x.r.items():
                self._need(eng, (k_, v_))

    def _commit(self, ev, r, w):
        for x in w:
            x.w = ev
            x.r = {}
        for x in r:
            if x not in w:
                if x.r.get(ev[0], 0) < ev[1]:
                    x.r[ev[0]] = ev[1]

    def op(self, eng, fn, r=(), w=()):
        fn = _freeze(fn)
        self._deps(eng, r, w)
        self.cnt[eng] += 1
        n = self.cnt[eng]
        key = ("e", eng)
        sem = self.sems[key]
        self.q[eng].append(lambda e, f=fn, s=sem: f(e).then_inc(s, 1))
        self.waited[eng][key] = max(self.waited[eng].get(key, 0), 0)
        self._commit((key, n), r, w)
        self.n_instr += 1

    def dma(self, qn, fn, r=(), w=(), is_output=False):
        fn = _freeze(fn)
        self._deps(qn, r, w)
        i = self.dma_next[qn]
        self.dma_next[qn] = (i + 1) % self.n_dma_sems
        k = self.dma_pool[qn][i]
        if self.dma_tgt[k] > 0:
            self._need(qn, (k, self.dma_tgt[k]))
        self.dma_tgt[k] += 16
        tgt = self.dma_tgt[k]
        sem = self.sems[k]
        self.q[qn].append(lambda e, f=fn, s=sem: f(e).then_inc(s, 16))
        ev = (k, tgt)
        self._commit(ev, r, w)
        if is_output:
            self.out_events.append(ev)
        self.n_instr += 1

    def pe_drain(self):
        n = self.cnt["pe"]
        if n:
            sem = self.sems[("e", "pe")]
            self.q["pe"].append(lambda e, s=sem, v=n: e.wait_ge(s, v))

    def finish(self):
        for ev in self.out_events:
            self._need("sp", ev)
        for e in self.CE:
            if self.cnt[e]:
                self._need("sp", (("e", e), self.cnt[e]))

    def emit(self, block):
        q = self.q

        @block.sync
        def _(eng):
            for f in q["sp"]:
                f(eng)

        @block.tensor
        def _(eng):
            for f in q["pe"]:
                f(eng)

        @block.scalar
        def _(eng):
            for f in q["act"]:
                f(eng)

        @block.vector
        def _(eng):
            for f in q["dve"]:
                f(eng)

        @block.gpsimd
        def _(eng):
            for f in q["pool"]:
                f(eng)


D = 1024
NCORES = 8
SEQ = 2048
NMETA = 16
NSAMP = 16
LS = 4
H = 8
G = 64
PS = 64
TOPK = 16
EPS = 1e-6
TWO_PI = 2.0 * np.pi

C_ID, C_TRI, C_ONE, C_MS, C_TS, C_M2, C_IO, C_MQ, C_END = 0, 128, 256, 384, 448, 512, 528, 544, 544 + 1024


def make_consts():
    c = np.zeros((128, C_END), np.float32)
    c[:, C_ID:C_ID + 128] = np.eye(128)
    c[:, C_TRI:C_TRI + 128] = np.triu(np.ones((128, 128)))
    c[:, C_ONE:C_ONE + 128] = 1.0
    seq = np.arange(64) // LS
    same = (seq[:, None] == seq[None, :])
    c[:64, C_MS:C_MS + 64] = same & (np.arange(64)[:, None] <= np.arange(64)[None, :])
    c[:64, C_TS:C_TS + 64] = same
    c[:64, C_M2:C_M2 + 16] = (seq[:, None] == np.arange(16)[None, :])
    c[:, C_IO:C_IO + 16] = np.arange(16)[None, :]
    mq = (np.arange(16)[:, None] == seq[None, :]).astype(np.float32).reshape(1, 1024)
    c[:, C_MQ:C_MQ + 1024] = mq
    return c


class Buf:
    def __init__(self, t, name):
        self.t = t
        self.r = Res(name)

    def __getitem__(self, k):
        return self.t[k]


def build_program(NPT=16, DO_SAMPLE=True, DO_PEER=True, STAGE=9):
    from contextlib import ExitStack
    nc = bass.Bass("TRN2", target_bir_lowering=False)

    def din(name, shape, dt=F32):
        return nc.dram_tensor(name, list(shape), dt, kind="ExternalInput").ap()

    def dout(name, shape, dt=F32):
        return nc.dram_tensor(name, list(shape), dt, kind="ExternalOutput").ap()

    xp = din("xp", [SEQ, D]); xs = din("xs", [NSAMP * LS, D]); meta = din("meta", [NMETA, D])
    st_hg = din("st_hg", [NSAMP, H, 128, 128]); st_re = din("st_re", [NSAMP * G, PS]); st_im = din("st_im", [NSAMP * G, PS])
    consts = din("consts", [128, C_END])
    lower_bounds = din("lower_bounds", [2, D]); norm1_w = din("norm1_w", [1, D]); w_in = din("w_in", [1, D, 7168])
    g_norm_w = din("g_norm_w", [1, 128]); a_re = din("ssm_a_re", [1, G, PS]); a_im = din("ssm_a_im", [1, G, PS])
    log_step = din("ssm_log_step", [1, G]); b_re = din("ssm_b_re", [1, G, PS, 16]); b_im = din("ssm_b_im", [1, G, PS, 16])
    c_re = din("ssm_c_re", [1, G, 16, PS]); c_im = din("ssm_c_im", [1, G, 16, PS]); ssm_d = din("ssm_d", [1, D])
    w_glu = din("w_glu", [1, D, D]); b_glu = din("b_glu", [1, D]); w_a = din("w_branch_a", [1, D, D])
    w_b = din("w_branch_b", [1, D, D]); w_out = din("w_out", [1, D, D]); norm2_w = din("norm2_w", [1, D])
    peer_wq = din("peer_wq", [1, D, 2048]); peer_k1 = din("peer_k1", [1, H, 128, 128]); peer_k2 = din("peer_k2", [1, H, 128, 128])
    peer_u = din("peer_u", [1, 16384, D]); peer_v = din("peer_v", [1, 16384, D]); final_norm_w = din("final_norm_w", [D])

    y_p = dout("y_p", [SEQ, D]); y_s = dout("y_s", [NSAMP * LS, D])
    hg_p = dout("hg_p", [H, 128, 128]); re_p = dout("re_p", [G, PS]); im_p = dout("im_p", [G, PS])
    hg_s = dout("hg_s", [NSAMP, H, 128, 128]); re_s = dout("re_s", [NSAMP * G, PS]); im_s = dout("im_s", [NSAMP * G, PS])

    with ExitStack() as st:
        P = Prog(nc)
        P.alloc_sems(st)

        def SB(name, shape, dt=F32):
            return Buf(st.enter_context(nc.sbuf_tensor(name, list(shape), dt)), name)

        ps2 = [Buf(st.enter_context(nc.psum_tensor("ps%d" % i, [128, 1024], F32)), "ps%d" % i) for i in range(4)]
        psc = [0]

        def PSN():
            psc[0] = psc[0] % 2 + 2
            return ps2[psc[0]]

        def R(bufs):
            return [b.r for b in bufs]

        def V(fn, r=(), w=()):
            P.op("dve", fn, r=R(r), w=R(w))

        def A(fn, r=(), w=()):
            P.op("act", fn, r=R(r), w=R(w))

        def GP(fn, r=(), w=()):
            P.op("pool", fn, r=R(r), w=R(w))

        def PE(fn, r=(), w=()):
            P.op("pe", fn, r=R(r), w=R(w))

        def DMA(fn, r=(), w=(), q="sp", out=False):
            P.dma(q, fn, r=R(r), w=R(w), is_output=out)

        def ld(dst_ap, src_ap, wbuf, q="sp", slow=False):
            if slow:
                DMA(lambda e: e.dma_start(out=dst_ap, in_=src_ap, allow_slow_non_contiguous=True), w=[wbuf], q=q)
            else:
                DMA(lambda e: e.dma_start(out=dst_ap, in_=src_ap), w=[wbuf], q=q)

        def stq(dst_ap, src_ap, rbuf, q="pool"):
            DMA(lambda e: e.dma_start(out=dst_ap, in_=src_ap), r=[rbuf], q=q, out=True)

        cst = SB("cst", [128, C_MQ])
        ident = cst[:, C_ID:C_ID + 128]
        n1w = SB("n1w", [128, D], BF16); n2w = SB("n2w", [128, D], BF16); fnw = SB("fnw", [128, D], BF16)
        lbt = SB("lbt", [128, D]); gnw8 = SB("gnw8", [128, D], BF16); dvec = SB("dvec", [128, D], BF16)
        bglu = SB("bglu", [128, 8])
        wb = [SB("wb%d" % i, [128, 8, 256], BF16) for i in range(3)]
        fb = [SB("fb%d" % i, [128, 8, 128], BF16) for i in range(3)]
        qT0, qT1 = fb[1], fb[2]
        wbc = [0]
        NTM = 13
        tm = [SB("tm%d" % i, [128, D]) for i in range(NTM)]
        fm = [None, SB("fm1", [128, 8, 128]), SB("fm2", [128, 8, 128]), None]

        Shg = SB("Shg", [128, H, 128])
        attm = SB("attm", [128, 128])
        gdec = SB("gdec", [128, H, NSAMP])
        sm = SB("sm", [128, 64])
        Bmat = SB("Bmat", [128, 11, 2, 2, 128], BF16)
        ub16 = SB("ub16", [128, 11, 128], BF16)
        KT1 = SB("KT1", [128, 8, 128], BF16); KT2 = SB("KT2", [128, 8, 128], BF16)
        Cmat = SB("Cmat", [128, G, 16])
        ArAr = SB("ArAr", [128, 2, G]); AiPN = SB("AiPN", [128, 2, G])
        SBLK = 64
        SHb = SB("SHb", [128, SBLK, 2, G])

        class View:
            def __init__(self, ap, res):
                self.ap = ap
                self.r = res

            def __getitem__(self, k):
                return self.ap[k]

        _shf = SHb[:].rearrange("p t v g -> p (t v g)")
        Ssm = View(_shf[:, 0:2048].rearrange("p (s v) -> p s v", v=128), SHb.r)
        Zq = View(_shf[:, 2048:3072].rearrange("p (s t) -> p s t", t=64), SHb.r)
        Zv = View(_shf[0:64, 3072:4096].rearrange("p (s v) -> p s v", v=128), SHb.r)
        Scar = SB("Scar", [128, 1, 2, G])
        sT1 = SB("sT1", [128, 1, 2, G]); sT2 = SB("sT2", [128, 1, 2, G])
        sT2a = sT2; sT2b = Buf(sT2.t, "sT2b")

        block = st.enter_context(nc.Block())

        ld(cst[:], consts[:, 0:C_MQ], cst)
        ld(n1w[:], norm1_w[0, :].partition_broadcast(128), n1w, q="pool")
        ld(n2w[:], norm2_w[0, :].partition_broadcast(128), n2w, q="pool")
        ld(fnw[:], final_norm_w.partition_broadcast(128), fnw, q="pool")
        ld(dvec[:], ssm_d[0, :].partition_broadcast(128), dvec, q="pool")
        ld(lbt[:], lower_bounds[0, :].partition_broadcast(128), lbt)
        oml = tm[12]
        ld(oml[:], lower_bounds[1, :].partition_broadcast(128), oml)
        for h in range(H):
            ld(gnw8[:, h * 128:(h + 1) * 128], g_norm_w[0, :].partition_broadcast(128), gnw8, q="pool")
        ld(bglu[:], b_glu[0].rearrange("(j p) -> p j", p=128), bglu, slow=True)
        V(lambda e: e.tensor_tensor(out=lbt[:], in0=lbt[:], in1=oml[:], op=ALU.subtract), r=[lbt, oml], w=[lbt])
        A(lambda e: e.activation(out=lbt[:], in_=lbt[:], func=AF.Sigmoid), r=[lbt], w=[lbt])

        are, aim, lst, mag, ang, cs, sn, wk1, wk2, wk3 = [tm[i] for i in range(10)]
        for half in range(2):
            ld(are[half * 64:(half + 1) * 64, 0:G], a_re[0].rearrange("g p -> p g"), are, slow=True)
            ld(aim[half * 64:(half + 1) * 64, 0:G], a_im[0].rearrange("g p -> p g"), aim, slow=True)
        ld(lst[:, 0:G], log_step[0, :].partition_broadcast(128), lst)
        g64 = slice(0, G)
        A(lambda e: e.activation(out=lst[:, g64], in_=lst[:, g64], func=AF.Exp), r=[lst], w=[lst])
        V(lambda e: e.tensor_scalar_min(out=are[:, g64], in0=are[:, g64], scalar1=-1e-4), r=[are], w=[are])
        V(lambda e: e.tensor_tensor(out=mag[:, g64], in0=are[:, g64], in1=lst[:, g64], op=ALU.mult), r=[are, lst], w=[mag])
        A(lambda e: e.activation(out=mag[:, g64], in_=mag[:, g64], func=AF.Exp), r=[mag], w=[mag])
        V(lambda e: e.tensor_tensor(out=ang[:, g64], in0=aim[:, g64], in1=lst[:, g64], op=ALU.mult), r=[aim, lst], w=[ang])

        wki = SB("wki", [128, G], I32)

        def sin_of(dst, src, shift):
            V(lambda e: e.tensor_scalar(out=wk1[:, g64], in0=src[:, g64], scalar1=1.0 / TWO_PI, scalar2=shift / TWO_PI,
                                        op0=ALU.mult, op1=ALU.add), r=[src], w=[wk1])
            V(lambda e: e.tensor_copy(out=wki[:], in_=wk1[:, g64]), r=[wk1], w=[wki])
            V(lambda e: e.tensor_copy(out=wk2[:, g64], in_=wki[:]), r=[wki], w=[wk2])
            V(lambda e: e.tensor_tensor(out=wk1[:, g64], in0=wk1[:, g64], in1=wk2[:, g64], op=ALU.subtract), r=[wk1, wk2], w=[wk1])
            V(lambda e: e.tensor_scalar(out=wk2[:, g64], in0=wk1[:, g64], scalar1=0.5, scalar2=None, op0=ALU.is_gt), r=[wk1], w=[wk2])
            V(lambda e: e.tensor_tensor(out=wk1[:, g64], in0=wk1[:, g64], in1=wk2[:, g64], op=ALU.subtract), r=[wk1, wk2], w=[wk1])
            V(lambda e: e.tensor_scalar(out=wk2[:, g64], in0=wk1[:, g64], scalar1=-0.5, scalar2=None, op0=ALU.is_lt), r=[wk1], w=[wk2])
            V(lambda e: e.tensor_tensor(out=wk1[:, g64], in0=wk1[:, g64], in1=wk2[:, g64], op=ALU.add), r=[wk1, wk2], w=[wk1])
            V(lambda e: e.tensor_scalar(out=wk1[:, g64], in0=wk1[:, g64], scalar1=TWO_PI, scalar2=3.14159,
                                        op0=ALU.mult, op1=ALU.min), r=[wk1], w=[wk1])
            V(lambda e: e.tensor_scalar_max(out=wk1[:, g64], in0=wk1[:, g64], scalar1=-3.14159), r=[wk1], w=[wk1])
            A(lambda e: e.activation(out=dst[:, g64], in_=wk1[:, g64], func=AF.Sin), r=[wk1], w=[dst])

        sin_of(sn, ang, 0.0)
        sin_of(cs, ang, np.pi / 2)
        V(lambda e: e.tensor_tensor(out=cs[:, g64], in0=cs[:, g64], in1=mag[:, g64], op=ALU.mult), r=[cs, mag], w=[cs])
        V(lambda e: e.tensor_tensor(out=sn[:, g64], in0=sn[:, g64], in1=mag[:, g64], op=ALU.mult), r=[sn, mag], w=[sn])
        V(lambda e: e.tensor_copy(out=ArAr[:, 0, :], in_=cs[:, g64]), r=[cs], w=[ArAr])
        V(lambda e: e.tensor_copy(out=ArAr[:, 1, :], in_=cs[:, g64]), r=[cs], w=[ArAr])
        V(lambda e: e.tensor_copy(out=AiPN[:, 0, :], in_=sn[:, g64]), r=[sn], w=[AiPN])
        V(lambda e: e.tensor_scalar_mul(out=AiPN[:, 1, :], in0=sn[:, g64], scalar1=-1.0), r=[sn], w=[AiPN])
        V(lambda e: e.tensor_scalar_add(out=wk1[:, g64], in0=cs[:, g64], scalar1=-1.0), r=[cs], w=[wk1])
        V(lambda e: e.tensor_tensor(out=wk2[:, g64], in0=are[:, g64], in1=are[:, g64], op=ALU.mult), r=[are], w=[wk2])
        V(lambda e: e.tensor_tensor(out=wk3[:, g64], in0=aim[:, g64], in1=aim[:, g64], op=ALU.mult), r=[aim], w=[wk3])
        V(lambda e: e.tensor_tensor(out=wk2[:, g64], in0=wk2[:, g64], in1=wk3[:, g64], op=ALU.add), r=[wk2, wk3], w=[wk2])
        V(lambda e: e.reciprocal(out=wk2[:, g64], in_=wk2[:, g64]), r=[wk2], w=[wk2])
        cr, ci = mag, ang
        V(lambda e: e.tensor_tensor(out=cr[:, g64], in0=wk1[:, g64], in1=are[:, g64], op=ALU.mult), r=[wk1, are], w=[cr])
        V(lambda e: e.tensor_tensor(out=wk3[:, g64], in0=sn[:, g64], in1=aim[:, g64], op=ALU.mult), r=[sn, aim], w=[wk3])
        V(lambda e: e.tensor_tensor(out=cr[:, g64], in0=cr[:, g64], in1=wk3[:, g64], op=ALU.add), r=[cr, wk3], w=[cr])
        V(lambda e: e.tensor_tensor(out=cr[:, g64], in0=cr[:, g64], in1=wk2[:, g64], op=ALU.mult), r=[cr, wk2], w=[cr])
        V(lambda e: e.tensor_tensor(out=ci[:, g64], in0=sn[:, g64], in1=are[:, g64], op=ALU.mult), r=[sn, are], w=[ci])
        V(lambda e: e.tensor_tensor(out=wk3[:, g64], in0=wk1[:, g64], in1=aim[:, g64], op=ALU.mult), r=[wk1, aim], w=[wk3])
        V(lambda e: e.tensor_tensor(out=ci[:, g64], in0=ci[:, g64], in1=wk3[:, g64], op=ALU.subtract), r=[ci, wk3], w=[ci])
        V(lambda e: e.tensor_tensor(out=ci[:, g64], in0=ci[:, g64], in1=wk2[:, g64], op=ALU.mult), r=[ci, wk2], w=[ci])
        V(lambda e: e.tensor_copy(out=wk1[0:64, g64], in_=cr[0:64, g64]), r=[cr], w=[wk1])
        V(lambda e: e.tensor_copy(out=wk1[64:128, g64], in_=ci[64:128, g64]), r=[ci], w=[wk1])
        V(lambda e: e.tensor_scalar_mul(out=wk2[0:64, g64], in0=ci[0:64, g64], scalar1=-1.0), r=[ci], w=[wk2])
        V(lambda e: e.tensor_copy(out=wk2[64:128, g64], in_=cr[64:128, g64]), r=[cr], w=[wk2])
        BR2, BI2, V0, V1, VM = tm[10], tm[11], tm[12], tm[0], tm[1]
        for half in range(2):
            for gh in range(2):
                gs_ = slice(gh * 32, (gh + 1) * 32)
                ld(BR2[half * 64:(half + 1) * 64, gh * 512:(gh + 1) * 512].rearrange("p (g c) -> p g c", c=16),
                   b_re[0][gs_].rearrange("g p c -> p g c"), BR2, slow=True)
                ld(BI2[half * 64:(half + 1) * 64, gh * 512:(gh + 1) * 512].rearrange("p (g c) -> p g c", c=16),
                   b_im[0][gs_].rearrange("g p c -> p g c"), BI2, slow=True)

        def g3(buf):
            return buf[:].rearrange("p (g c) -> p g c", c=16)

        def bc16(buf):
            return buf[:, g64].unsqueeze(2).to_broadcast([128, G, 16])

        V(lambda e: e.tensor_tensor(out=g3(V0), in0=g3(BR2), in1=bc16(wk1), op=ALU.mult), r=[BR2, wk1], w=[V0])
        V(lambda e: e.tensor_tensor(out=g3(V1), in0=g3(BI2), in1=bc16(wk2), op=ALU.mult), r=[BI2, wk2], w=[V1])
        V(lambda e: e.tensor_tensor(out=V0[:], in0=V0[:], in1=V1[:], op=ALU.add), r=[V0, V1], w=[V0])
        V(lambda e: e.tensor_tensor(out=g3(V1), in0=g3(BR2), in1=bc16(wk2), op=ALU.mult), r=[BR2, wk2], w=[V1])
        V(lambda e: e.tensor_tensor(out=g3(BR2), in0=g3(BI2), in1=bc16(wk1), op=ALU.mult), r=[BI2, wk1], w=[BR2])
        V(lambda e: e.tensor_tensor(out=V1[:], in0=V1[:], in1=BR2[:], op=ALU.subtract), r=[V1, BR2], w=[V1])
        for v, Vv in enumerate((V0, V1)):
            for gl in range(2):
                V(lambda e: e.memset(VM[:], 0.0), w=[VM])
                V(lambda e, Vv=Vv, gl=gl: e.tensor_copy(out=g3(VM)[:, gl::2, :], in_=g3(Vv)[:, gl::2, :]), r=[Vv], w=[VM])
                for j in range(11):
                    wd = min(96, 1024 - j * 96)
                    pp = PSN()
                    PE(lambda e, pp=pp, j=j, wd=wd: e.transpose(out=pp[:wd, 0:128], in_=VM[:, j * 96:j * 96 + wd], identity=ident),
                       r=[VM, cst], w=[pp])
                    A(lambda e, pp=pp, j=j, gl=gl, v=v, wd=wd: e.copy(out=Bmat[:wd, j, gl, v, :], in_=pp[:wd, 0:128]), r=[pp], w=[Bmat])
        Cn = tm[2]
        ld(Cn[:].rearrange("p (j q) -> p j q", q=128)[:, :, 0:64], c_re[0].rearrange("(j g) c p -> (g c) j p", j=8), Cn)
        ld(Cn[:].rearrange("p (j q) -> p j q", q=128)[:, :, 64:128], c_im[0].rearrange("(j g) c p -> (g c) j p", j=8), Cn)
        V(lambda e: e.tensor_scalar_mul(out=Cn[:].rearrange("p (j q) -> p j q", q=128)[:, :, 64:128],
                                        in0=Cn[:].rearrange("p (j q) -> p j q", q=128)[:, :, 64:128], scalar1=-1.0), r=[Cn], w=[Cn])
        for j in range(8):
            pp = PSN()
            PE(lambda e, pp=pp, j=j: e.transpose(out=pp[:, 0:128], in_=Cn[:, j * 128:(j + 1) * 128], identity=ident),
               r=[Cn, cst], w=[pp])
            A(lambda e, pp=pp, j=j: e.copy(out=Cmat[:, j * 8:(j + 1) * 8, :],
                                           in_=pp[:, 0:128].rearrange("p (g c) -> p g c", c=16)), r=[pp], w=[Cmat])

        if os.environ.get('KDUMP', '') == 'abar':
            dbg = dout("dbg", [128, 1024])
            stq(dbg[:, 0:64], ArAr[:, 0, :], ArAr)
            stq(dbg[:, 64:128], AiPN[:, 0, :], AiPN)
            stq(dbg[:, 128:192], wk1[:, g64], wk1)
            stq(dbg[:, 192:256], wk2[:, g64], wk2)
            stq(dbg[:, 512:768], Cmat[:, 0:16, :].rearrange("p g c -> p (g c)"), Cmat)
        KN1, KN2 = tm[8], tm[9]
        ld(KN1[:].rearrange("p (h d) -> p h d", d=128), peer_k1[0].rearrange("h n d -> n h d"), KN1)
        ld(KN2[:].rearrange("p (h d) -> p h d", d=128), peer_k2[0].rearrange("h n d -> n h d"), KN2)
        for KNx, KTx in ((KN1, KT1), (KN2, KT2)):
            ppk = PSN()
            for j in range(8):
                PE(lambda e, ppk=ppk, j=j, KNx=KNx: e.transpose(out=ppk[:, j * 128:(j + 1) * 128], in_=KNx[:, j * 128:(j + 1) * 128],
                                                              identity=ident), r=[KNx, cst], w=[ppk])
            A(lambda e, ppk=ppk, KTx=KTx: e.copy(out=KTx[:], in_=ppk[:, :].rearrange("p (j t) -> p j t", t=128)), r=[ppk], w=[KTx])
        WBASE = {}
        wlist = (("w_in", w_in, 28), ("w_a", w_a, 4), ("w_glu", w_glu, 4), ("w_b", w_b, 4), ("w_out", w_out, 4), ("wq", peer_wq, 8))
        ncg_tot = sum(x[2] for x in wlist)
        wsc = nc.dram_tensor("w_scratch", [ncg_tot, 128, 2048], BF16, kind="Internal").ap()
        wsc_res = [Buf(None, "wsc%d" % i_) for i_ in range(ncg_tot)]
        WSRC = {}
        gi_ = 0
        for nm_, Wd_, ncg_ in wlist:
            WBASE[nm_] = gi_
            WSRC[nm_] = Wd_
            gi_ += ncg_
        wsc_conv = set()
        wsc_seen = set()
        uvb = nc.dram_tensor("uvb_scratch", [16384, 2048], BF16, kind="Internal").ap()
        uvb_res = [Buf(None, "uvb%d" % c) for c in range(128)]

        def uvb_gen():
            for c in range(128):
                stg = PG[c % 6]
                sv = stg[:].bitcast(BF16)
                DMA(lambda e, sv=sv, c=c: e.dma_start(out=sv[:, 0:1024], in_=peer_u[0][c * 128:(c + 1) * 128, :]), w=[stg], q="pool")
                DMA(lambda e, sv=sv, c=c: e.dma_start(out=sv[:, 1024:2048], in_=peer_v[0][c * 128:(c + 1) * 128, :]), w=[stg], q="pool")
                DMA(lambda e, sv=sv, c=c: e.dma_start(out=uvb[c * 128:(c + 1) * 128, :], in_=sv), r=[stg], w=[uvb_res[c]], q="sp")
                yield "u"
        uvb_waited = [False]
        def transpose8(dst_fm, src_tm, n, nblk=8, evac="act"):
            pp = PSN()
            for j in range(nblk):
                PE(lambda e, pp=pp, j=j: e.transpose(out=pp[:, j * n:(j + 1) * n], in_=src_tm[:n, j * 128:(j + 1) * 128],
                                                     identity=cst[:n, C_ID:C_ID + n]), r=[src_tm, cst], w=[pp])
            src = pp[:, 0:nblk * n].rearrange("p (j t) -> p j t", t=n)
            if evac == "act":
                A(lambda e: e.copy(out=dst_fm[:, 0:nblk, :n], in_=src), r=[pp], w=[dst_fm])
            else:
                V(lambda e: e.tensor_copy(out=dst_fm[:, 0:nblk, :n], in_=src), r=[pp], w=[dst_fm])

        def load_w(wname, cg):
            i = wbc[0]
            wbc[0] = (i + 1) % 3
            w = wb[i]
            gidx = WBASE[wname] + cg
            if gidx not in wsc_conv:
                wsc_conv.add(gidx)
                Wd_ = WSRC[wname]
                DMA(lambda e, w=w, Wd_=Wd_, cg=cg: e.dma_start(out=w[:], in_=Wd_[0][:, cg * 256:(cg + 1) * 256].rearrange("(k p) c -> p k c", p=128)),
                    w=[w], q="pool")
                DMA(lambda e, w=w, gidx=gidx: e.dma_start(out=wsc[gidx], in_=w[:].rearrange("p k c -> p (k c)")), r=[w], w=[wsc_res[gidx]], q="sp")
                return w
            extra = [] if gidx in wsc_seen else [wsc_res[gidx]]
            wsc_seen.add(gidx)
            DMA(lambda e, w=w, gidx=gidx: e.dma_start(out=w[:].rearrange("p k c -> p (k c)"), in_=wsc[gidx]), r=extra, w=[w], q="sp")
            return w

        def proj_tm(lhsT, n, Wd, cgs, evac):
            for cg in cgs:
                w = load_w(Wd, cg)
                pp = PSN()
                for k in range(8):
                    PE(lambda e, pp=pp, w=w, k=k: e.matmul(pp[:n, 0:256], lhsT=lhsT[:, k, :n], rhs=w[:, k, :],
                                                           start=(k == 0), stop=(k == 7)), r=[lhsT, w], w=[pp])
                evac(cg, pp)
                yield "y"

        def rmsnorm_tm(dst, src, n, wbuf_):
            A(lambda e: e.activation(out=tm[NTM - 1][:n, :], in_=src[:n, :], func=AF.Square, accum_out=sm[:n, 0:1]),
              r=[src], w=[tm[NTM - 1], sm])
            A(lambda e: e.activation(out=sm[:n, 1:2], in_=sm[:n, 0:1], func=AF.Sqrt, scale=1.0 / D, bias=EPS), r=[sm], w=[sm])
            V(lambda e: e.reciprocal(out=sm[:n, 2:3], in_=sm[:n, 1:2]), r=[sm], w=[sm])
            V(lambda e: e.scalar_tensor_tensor(out=dst[:n, :], in0=src[:n, :], scalar=sm[:n, 2:3], in1=wbuf_[:n, :],
                                               op0=ALU.mult, op1=ALU.mult), r=[src, sm, wbuf_], w=[dst])

        PO = PY = ps2[0]
        PA = ps2[1]
        PX = SB("PX", [128, D]); PXN = SB("PXN", [128, D]); PS1 = SB("PS1", [128, D]); POH = SB("POH", [128, D])
        PG = [SB("PG%d" % i, [128, D]) for i in range(6)]
        pfb = SB("pfb", [128, 8, 128], BF16)
        tv = SB("tv", [128, 256]); ti = SB("ti", [128, 256], U32); tif = tv
        tv2 = SB("tv2", [128, 128]); ti2 = SB("ti2", [128, 128], U32)
        sc2 = SB("sc2", [128, 5, 128])
        eidf = SB("eidf", [128, 128]); eidi = SB("eidi", [128, 128], I32)
        gate = SB("gate", [128, 128]); dots = SB("dots", [128, 128]); coef = SB("coef", [128, 128])
        wkm = SB("wkm", [128, 256])
        DG = [SB("dg%d" % i, [128, 128], BF16) for i in range(2)]
        V(lambda e: e.memset(eidi[:], 0), w=[eidi])
        V(lambda e: e.memset(Shg[:], 0.0), w=[Shg])
        V(lambda e: e.memset(Scar[:], 0.0), w=[Scar])
        iota16 = cst[:, C_IO:C_IO + 16]

        def layer_gen(kind, idx):
            if kind == "meta":
                n, nseq, xsrc = NMETA, 1, meta
            elif kind == "prompt":
                n, nseq, xsrc = 128, 1, xp[idx * 128:(idx + 1) * 128, :]
            else:
                n, nseq, xsrc = NSAMP * LS, NSAMP, xs
            full = kind != "meta"
            if nseq == 1:
                Matt = cst[:n, C_TRI:C_TRI + n]; Tot = cst[:n, C_ONE:C_ONE + n]; M2 = cst[:n, C_ONE:C_ONE + 1]
            else:
                Matt = cst[:n, C_MS:C_MS + n]; Tot = cst[:n, C_TS:C_TS + n]; M2 = cst[:n, C_M2:C_M2 + 16]
            X, Hn, Q, KK, LG, VV, GS, U, SGA, SGB, BS, KT_, SCR = tm
            ld(X[:n, :], xsrc, X)
            rmsnorm_tm(Hn, X, n, n1w)
            transpose8(fb[0], Hn, n)
            segs = [(Q, AF.Silu), (KK, AF.Sigmoid), (VV, AF.Copy), (GS, AF.Silu), (U, AF.Copy), (SGA, AF.Sigmoid), (SGB, AF.Sigmoid)]

            def evac_in(cg, pp):
                dst, fn_ = segs[cg // 4]
                c0 = (cg % 4) * 256
                A(lambda e: e.activation(out=dst[:n, c0:c0 + 256], in_=pp[:n, 0:256], func=fn_), r=[pp], w=[dst])

            yield from proj_tm(fb[0], n, "w_in", list(range(28)) if full else list(range(12)) + list(range(16, 20)), evac_in)
            yield "A"
            V(lambda e: e.tensor_scalar(out=LG[:n, :], in0=KK[:n, :], scalar1=-1.0, scalar2=1.0, op0=ALU.mult, op1=ALU.add), r=[KK], w=[LG])
            V(lambda e: e.tensor_tensor(out=LG[:n, :], in0=LG[:n, :], in1=lbt[:n, :], op=ALU.mult), r=[LG, lbt], w=[LG])
            V(lambda e: e.tensor_tensor(out=KK[:n, :], in0=KK[:n, :], in1=LG[:n, :], op=ALU.add), r=[KK, LG], w=[KK])
            A(lambda e: e.activation(out=LG[:n, :], in_=KK[:n, :], func=AF.Ln), r=[KK], w=[LG])
            V(lambda e: e.tensor_scalar(out=KK[:n, :], in0=KK[:n, :], scalar1=-1.0, scalar2=1.0, op0=ALU.mult, op1=ALU.add),
              r=[KK], w=[KK])
            pb = PSN()
            for hf in range(2):
                PE(lambda e, hf=hf: e.matmul(pb[:n, hf * 512:(hf + 1) * 512], lhsT=Matt, rhs=LG[:n, hf * 512:(hf + 1) * 512],
                                             start=True, stop=True), r=[cst, LG], w=[pb])
            A(lambda e: e.copy(out=BS[:n, :], in_=pb[:n, :]), r=[pb], w=[BS])
            pl = PSN()
            for hf in range(2):
                PE(lambda e, hf=hf: e.matmul(pl[:n, hf * 512:(hf + 1) * 512], lhsT=Tot, rhs=LG[:n, hf * 512:(hf + 1) * 512],
                                             start=True, stop=True), r=[cst, LG], w=[pl])
            A(lambda e: e.activation(out=KT_[:n, :], in_=BS[:n, :], func=AF.Exp, scale=-1.0), r=[BS], w=[KT_])
            V(lambda e: e.tensor_tensor(out=KT_[:n, :], in0=KT_[:n, :], in1=KK[:n, :], op=ALU.mult), r=[KT_, KK], w=[KT_])
            V(lambda e: e.tensor_tensor(out=SCR[:n, :], in0=pl[:n, :], in1=BS[:n, :], op=ALU.subtract), r=[pl, BS], w=[SCR])
            A(lambda e: e.activation(out=SCR[:n, :], in_=SCR[:n, :], func=AF.Exp), r=[SCR], w=[SCR])
            V(lambda e: e.tensor_tensor(out=KK[:n, :], in0=KK[:n, :], in1=SCR[:n, :], op=ALU.mult), r=[KK, SCR], w=[KK])
            A(lambda e: e.activation(out=BS[:n, :], in_=BS[:n, :], func=AF.Exp), r=[BS], w=[BS])
            V(lambda e: e.tensor_tensor(out=Q[:n, :], in0=Q[:n, :], in1=BS[:n, :], op=ALU.mult), r=[Q, BS], w=[Q])
            pg = PSN()
            for h in range(H):
                PE(lambda e, h=h: e.matmul(pg[:, h * 16:h * 16 + nseq], lhsT=LG[:n, h * 128:(h + 1) * 128], rhs=M2,
                                           start=True, stop=True), r=[LG, cst], w=[pg])
            A(lambda e: e.activation(out=gdec[:, :, 0:nseq], in_=pg[:, 0:128].rearrange("p (h s) -> p h s", s=16)[:, :, 0:nseq],
                                     func=AF.Exp), r=[pg], w=[gdec])
            transpose8(fm[1], Q, n)
            transpose8(fm[2], KT_, n, evac="dve")
            if nseq > 1:
                ld(Hn[:, :], consts[:, C_MQ:C_MQ + 1024], Hn)
            for h in range(H):
                yield "y"
                hs = slice(h * 128, (h + 1) * 128)
                pa = PSN()
                PE(lambda e, h=h, pa=pa: e.matmul(pa[:n, 0:n], lhsT=fm[2][:, h, :n], rhs=fm[1][:, h, :n], start=True, stop=True),
                   r=[fm[1], fm[2]], w=[pa])
                V(lambda e, pa=pa: e.tensor_tensor(out=attm[:n, :n], in0=pa[:n, 0:n], in1=Matt, op=ALU.mult), r=[pa, cst], w=[attm])
                if nseq == 1:
                    if full:
                        PE(lambda e, h=h, hs=hs: e.matmul(PO[:n, hs], lhsT=fm[1][:, h, :n], rhs=Shg[:, h, :], start=True, stop=False),
                           r=[fm[1], Shg], w=[PO])
                        PE(lambda e, hs=hs: e.matmul(PO[:n, hs], lhsT=attm[:n, :n], rhs=VV[:n, hs], start=False, stop=True),
                           r=[attm, VV], w=[PO])
                    pd = PSN()
                    PE(lambda e, hs=hs, pd=pd: e.matmul(pd[:, 0:128], lhsT=KK[:n, hs], rhs=VV[:n, hs], start=True, stop=True),
                       r=[KK, VV], w=[pd])
                    V(lambda e, h=h, pd=pd: e.scalar_tensor_tensor(out=Shg[:, h, :], in0=Shg[:, h, :], scalar=gdec[:, h, 0:1],
                                                                   in1=pd[:, 0:128], op0=ALU.mult, op1=ALU.add),
                      r=[Shg, gdec, pd], w=[Shg])
                else:
                    ld(Ssm[:], st_hg[:, h].rearrange("s k v -> k s v"), Ssm)
                    V(lambda e, h=h: e.tensor_tensor(out=Zq[:], in0=fm[1][:, h, :n].unsqueeze(1).to_broadcast([128, NSAMP, n]),
                                                     in1=Hn[:, :].rearrange("p (s t) -> p s t", t=64), op=ALU.mult),
                      r=[fm[1], Hn], w=[Zq])
                    for s_ in range(NSAMP):
                        PE(lambda e, s_=s_, hs=hs: e.matmul(PO[:n, hs], lhsT=Zq[:, s_, :], rhs=Ssm[:, s_, :], start=(s_ == 0), stop=False),
                           r=[Zq, Ssm], w=[PO])
                    PE(lambda e, hs=hs: e.matmul(PO[:n, hs], lhsT=attm[:n, :n], rhs=VV[:n, hs], start=False, stop=True),
                       r=[attm, VV], w=[PO])
                    for half in range(2):
                        V(lambda e, hs=hs, half=half: e.tensor_tensor(
                            out=Zv[:], in0=VV[:n, hs].unsqueeze(1).to_broadcast([n, 8, 128]),
                            in1=cst[:n, C_M2 + half * 8:C_M2 + half * 8 + 8].unsqueeze(2).to_broadcast([n, 8, 128]), op=ALU.mult),
                          r=[VV, cst], w=[Zv])
                        pd = PSN()
                        for qq in range(2):
                            PE(lambda e, hs=hs, pd=pd, qq=qq: e.matmul(
                                pd[:, qq * 512:(qq + 1) * 512], lhsT=KK[:n, hs],
                                rhs=Zv[:, qq * 4:(qq + 1) * 4, :].rearrange("p s v -> p (s v)"), start=True, stop=True),
                               r=[KK, Zv], w=[pd])
                        for sl in range(8):
                            s_ = half * 8 + sl
                            V(lambda e, h=h, s_=s_, sl=sl, pd=pd: e.scalar_tensor_tensor(
                                out=Ssm[:, s_, :], in0=Ssm[:, s_, :], scalar=gdec[:, h, s_:s_ + 1],
                                in1=pd[:, sl * 128:(sl + 1) * 128], op0=ALU.mult, op1=ALU.add), r=[Ssm, gdec, pd], w=[Ssm])
                    stq(hg_s[:, h].rearrange("s k v -> k s v"), Ssm[:], Ssm)
            if kind == "prompt" and idx == NPT - 1:
                stq(hg_p.rearrange("h k v -> k h v"), Shg[:], Shg)
            if STAGE <= 1 and full:
                return
            if full:
                OG = BS
                A(lambda e: e.activation(out=SCR[:n, :], in_=PO[:n, :], func=AF.Square), r=[PO], w=[SCR])
                V(lambda e: e.tensor_reduce(out=sm[:n, 8:16], in_=SCR[:n, :].rearrange("p (h v) -> p h v", v=128), axis=AX.X, op=ALU.add),
                  r=[SCR], w=[sm])
                A(lambda e: e.activation(out=sm[:n, 16:24], in_=sm[:n, 8:16], func=AF.Sqrt, scale=1.0 / 128, bias=EPS), r=[sm], w=[sm])
                V(lambda e: e.reciprocal(out=sm[:n, 24:32], in_=sm[:n, 16:24]), r=[sm], w=[sm])
                V(lambda e: e.tensor_tensor(out=OG[:n, :].rearrange("p (h v) -> p h v", v=128),
                                            in0=PO[:n, :].rearrange("p (h v) -> p h v", v=128),
                                            in1=sm[:n, 24:32].unsqueeze(2).to_broadcast([n, H, 128]), op=ALU.mult), r=[PO, sm], w=[OG])
                V(lambda e: e.tensor_tensor(out=OG[:n, :], in0=OG[:n, :], in1=gnw8[:n, :], op=ALU.mult), r=[OG, gnw8], w=[OG])
                V(lambda e: e.tensor_tensor(out=OG[:n, :], in0=OG[:n, :], in1=GS[:n, :], op=ALU.mult), r=[OG, GS], w=[OG])
                transpose8(fb[0], OG, n)
                BRA = Hn

                def evac_a(cg, pp):
                    A(lambda e: e.copy(out=BRA[:n, cg * 256:(cg + 1) * 256], in_=pp[:n, 0:256]), r=[pp], w=[BRA])
                yield from proj_tm(fb[0], n, "w_a", [0, 1, 2, 3], evac_a)

            if STAGE <= 2 and full:
                return
            for jr in (range(0, 8), range(8, 11)):
                pp = PSN()
                for j in jr:
                    wd = min(96, 1024 - j * 96)
                    PE(lambda e, pp=pp, j=j, wd=wd: e.transpose(out=pp[:wd, (j % 8) * n:(j % 8 + 1) * n], in_=U[:n, j * 96:j * 96 + wd],
                                                             identity=cst[:n, C_ID:C_ID + n]), r=[U, cst], w=[pp])
                nbk_ = len(jr)
                j0_ = jr[0]
                A(lambda e, pp=pp, nbk_=nbk_, j0_=j0_: e.copy(out=ub16[:96, j0_:j0_ + nbk_, :n],
                                                             in_=pp[:96, 0:nbk_ * n].rearrange("p (j t) -> p j t", t=n)), r=[pp], w=[ub16])
            T1, T2 = LG, VV
            DBG = os.environ.get('KDBG', '')
            if DBG == 'a' and full:
                return
            if nseq > 1:
                AIN, BIN = Q, KK
                S0s = [KT_, SCR]
                a3 = AIN[:].rearrange("p (j q) -> p j q", q=128)
                b3 = BIN[:].rearrange("p (j q) -> p j q", q=128)
                ld(a3[:, :, 0:64], st_re.rearrange("(j r) p -> r j p", r=128), AIN)
                ld(a3[:, :, 64:128], st_im.rearrange("(j r) p -> r j p", r=128), AIN)
                V(lambda e: e.tensor_scalar_mul(out=b3[:, :, 0:64], in0=a3[:, :, 64:128], scalar1=-1.0), r=[AIN], w=[BIN])
                V(lambda e: e.tensor_copy(out=b3[:, :, 64:128], in_=a3[:, :, 0:64]), r=[AIN], w=[BIN])
                for jj in range(8):
                    S0 = S0s[jj // 4]
                    for v, src in enumerate((AIN, BIN)):
                        pp = PSN()
                        PE(lambda e, pp=pp, src=src, jj=jj: e.transpose(out=pp[:, 0:128], in_=src[:, jj * 128:(jj + 1) * 128],
                                                                        identity=ident), r=[src, cst], w=[pp])
                        A(lambda e, pp=pp, jj=jj, v=v, S0=S0: e.copy(
                            out=S0[:].rearrange("p (s v g) -> p s v g", v=2, g=G)[:, 2 * (jj % 4):2 * (jj % 4) + 2, v, :],
                            in_=pp[:, 0:128].rearrange("p (s g) -> p s g", g=G)), r=[pp], w=[S0])
            SCB = int(os.environ.get('KNB', SBLK)) if nseq == 1 else SBLK
            yield "B"
            nblk = (n + SCB - 1) // SCB
            for bi in range(nblk):
                b0 = bi * SCB
                nb = min(SCB, n - b0)
                for q8 in range(8):
                    pp = PSN()
                    last_rr = -1
                    for gi in sorted(range(8), key=lambda gi_: ((q8 * 8 + gi_) // 2) % 3):
                        g = q8 * 8 + gi
                        pq_ = g // 2
                        gl = g % 2
                        jb, rr = divmod(pq_, 3)
                        srcf = ub16
                        if rr != last_rr or os.environ.get('KDRAIN', ''):
                            P.pe_drain()
                            last_rr = rr
                        for v in range(2):
                            slot = v * 8 + gi
                            PE(lambda e, pp=pp, slot=slot, rr=rr, jb=jb, gl=gl, v=v, srcf=srcf: e.matmul(
                                pp[:, slot * nb:(slot + 1) * nb], lhsT=Bmat[32 * rr:32 * rr + 32, jb, gl, v, :],
                                rhs=srcf[32 * rr:32 * rr + 32, jb, b0:b0 + nb], start=True, stop=True), r=[Bmat, srcf], w=[pp])
                    A(lambda e, pp=pp, q8=q8: e.copy(
                        out=SHb[:, 0:nb, :, q8 * 8:(q8 + 1) * 8].rearrange("p t v g -> p v g t"),
                        in_=pp[:, 0:16 * nb].rearrange("p (v g t) -> p v g t", v=2, g=8)), r=[pp], w=[SHb])
                if nseq == 1:
                    steps = [(None if t == 0 else SHb[:, t - 1:t, :, :], SHb[:, t:t + 1, :, :], 1, Scar[:, 0:1, :, :], Scar)
                             for t in range(nb)]
                else:
                    shv = SHb[:, 0:nb, :, :].rearrange("p (s l) v g -> p s l v g", l=LS)
                    steps = []
                    for sbt in range(2):
                        S0 = S0s[sbt]
                        for l in range(LS):
                            steps.append((None if l == 0 else shv[:, 8 * sbt:8 * sbt + 8, l - 1, :, :], shv[:, 8 * sbt:8 * sbt + 8, l, :, :], 8,
                                          S0[:].rearrange("p (s v g) -> p s v g", v=2, g=G), S0))
                if DBG == 'b' and full:
                    steps = []
                if DBG == 'c' and full:
                    steps = steps[:2]
                for (prev, cur, nbat, prev0, prev0_buf) in steps:
                    pbuf = SHb
                    if prev is None:
                        prev, pbuf = prev0, prev0_buf
                    if nbat == 1:
                        t1v = sT1[:, 0:1, :, :]; t2v = sT2[:, 0:1, :, :]
                        rT1, rT2a, rT2b = sT1, sT2a, sT2b
                    else:
                        t1v = T1[:, 0:nbat * 128].rearrange("p (s v g) -> p s v g", v=2, g=G)
                        t2v = T2[:, 0:nbat * 128].rearrange("p (s v g) -> p s v g", v=2, g=G)
                        rT1, rT2a, rT2b = T1, T2, T2
                    V(lambda e, prev=prev, t1v=t1v, nbat=nbat: e.tensor_tensor(
                        out=t1v, in0=prev, in1=ArAr[:].unsqueeze(1).to_broadcast([128, nbat, 2, G]), op=ALU.mult),
                      r=[pbuf, ArAr], w=[rT1])
                    V(lambda e, prev=prev, t2v=t2v, nbat=nbat: e.tensor_tensor(
                        out=t2v, in0=prev[:, :, ::-1, :], in1=AiPN[:].unsqueeze(1).to_broadcast([128, nbat, 2, G]), op=ALU.mult),
                      r=[pbuf, AiPN], w=[rT2a, rT2b])
                    V(lambda e, cur=cur, t1v=t1v: e.tensor_tensor(out=cur, in0=cur, in1=t1v, op=ALU.add), r=[SHb, rT1], w=[SHb])
                    yield "S1"
                    V(lambda e, cur=cur, t2v=t2v: e.tensor_tensor(out=cur, in0=cur, in1=t2v, op=ALU.add), r=[SHb, rT2a, rT2b], w=[SHb])
                    yield "S"
                if nseq == 1:
                    V(lambda e, nb=nb: e.tensor_copy(out=Scar[:, 0:1, :, :], in_=SHb[:, nb - 1:nb, :, :]), r=[SHb], w=[Scar])
                    if os.environ.get('KDUMP', '') == 'blk' and kind == 'prompt':
                        if bi == 0:
                            dU = dout("dbgU", [128, 1024]); stq(dU, U[:, :], U)
                        dbgb = dout("dbg%d" % bi, [128, 64 * 128])
                        stq(dbgb, SHb[:].rearrange("p t v g -> p (t v g)"), SHb)
                else:
                    FS = GS
                    FSX = Q
                    V(lambda e: e.tensor_copy(out=FSX[:, :].rearrange("p (s g) -> p s g", g=G),
                                              in_=SHb[:, 0:nb, 0, :].rearrange("p (s l) g -> p s l g", l=LS)[:, :, LS - 1, :]),
                      r=[SHb], w=[FSX])
                    for pr in range(8):
                        pp = PSN()
                        PE(lambda e, pp=pp, pr=pr: e.transpose(out=pp[:, 0:128], in_=FSX[:, pr * 128:(pr + 1) * 128],
                                                               identity=ident), r=[FSX, cst], w=[pp])
                        A(lambda e, pp=pp, pr=pr: e.copy(out=FS[:, pr * 128:(pr + 1) * 128], in_=pp[:, 0:128]), r=[pp], w=[FS])
                    fs3 = FS[:, :].rearrange("p (j q) -> p j q", q=128)
                    stq(re_s.rearrange("(j r) p -> r j p", r=128), fs3[:, :, 0:64], FS)
                    stq(im_s.rearrange("(j r) p -> r j p", r=128), fs3[:, :, 64:128], FS)
                if full and STAGE != 3:
                    for g in range(G):
                        PE(lambda e, g=g, b0=b0, nb=nb: e.matmul(PY[b0:b0 + nb, g * 16:(g + 1) * 16], lhsT=SHb[:, 0:nb, 0, g],
                                                                 rhs=Cmat[:, g, :], start=True, stop=True), r=[SHb, Cmat], w=[PY])
            if kind == "prompt" and idx == NPT - 1:
                pp = PSN()
                PE(lambda e, pp=pp: e.transpose(out=pp[:G, 0:128], in_=Scar[:, 0, 0, :], identity=ident), r=[Scar, cst], w=[pp])
                A(lambda e, pp=pp: e.copy(out=SCR[:G, 0:128], in_=pp[:G, 0:128]), r=[pp], w=[SCR])
                stq(re_p, SCR[:G, 0:64], SCR)
                stq(im_p, SCR[:G, 64:128], SCR)
            if not full:
                return
            if STAGE <= 3:
                return
            Y = BS
            V(lambda e: e.tensor_tensor(out=Y[:n, :], in0=U[:n, :], in1=dvec[:n, :], op=ALU.mult), r=[U, dvec], w=[Y])
            V(lambda e: e.tensor_tensor(out=Y[:n, :], in0=Y[:n, :], in1=PY[:n, :], op=ALU.add), r=[Y, PY], w=[Y])
            A(lambda e: e.activation(out=Y[:n, :], in_=Y[:n, :], func=AF.Gelu), r=[Y], w=[Y])
            transpose8(fb[1], Y, n)
            for cg in range(4):
                yield "y"
                w = load_w("w_glu", cg)
                for m in range(2):
                    pz = PSN()
                    for k in range(8):
                        PE(lambda e, pz=pz, w=w, k=k, m=m: e.matmul(pz[:, 0:n], lhsT=w[:, k, m * 128:(m + 1) * 128], rhs=fb[1][:, k, :n],
                                                                    start=(k == 0), stop=(k == 7)), r=[w, fb[1]], w=[pz])
                    A(lambda e, pz=pz, cg=cg, m=m: e.activation(out=fb[2][:, cg * 2 + m, :n], in_=pz[:, 0:n], func=AF.Sigmoid,
                                                                bias=bglu[:, cg * 2 + m:cg * 2 + m + 1]), r=[pz, bglu], w=[fb[2]])
            V(lambda e: e.tensor_tensor(out=fb[2][:, :, :n], in0=fb[2][:, :, :n], in1=fb[1][:, :, :n], op=ALU.mult), r=[fb[2], fb[1]], w=[fb[2]])
            MIX = KT_

            def evac_b(cg, pp):
                V(lambda e: e.tensor_tensor(out=MIX[:n, cg * 256:(cg + 1) * 256], in0=pp[:n, 0:256], in1=SGB[:n, cg * 256:(cg + 1) * 256],
                                            op=ALU.mult), r=[pp, SGB], w=[MIX])
            yield from proj_tm(fb[2], n, "w_b", [0, 1, 2, 3], evac_b)
            V(lambda e: e.tensor_tensor(out=BRA[:n, :], in0=BRA[:n, :], in1=SGA[:n, :], op=ALU.mult), r=[BRA, SGA], w=[BRA])
            V(lambda e: e.tensor_tensor(out=MIX[:n, :], in0=MIX[:n, :], in1=BRA[:n, :], op=ALU.add), r=[MIX, BRA], w=[MIX])
            transpose8(fb[0], MIX, n)

            def evac_o(cg, pp):
                V(lambda e: e.tensor_tensor(out=X[:n, cg * 256:(cg + 1) * 256], in0=X[:n, cg * 256:(cg + 1) * 256], in1=pp[:n, 0:256],
                                            op=ALU.add), r=[pp, X], w=[X])
            yield from proj_tm(fb[0], n, "w_out", [0, 1, 2, 3], evac_o)

            if STAGE <= 4:
                return
            yield "C"
            V(lambda e: e.tensor_copy(out=PX[:n, :], in_=X[:n, :]), r=[X], w=[PX])
            return

        def peer_gen(kind, idx, n):
            X = PX; XN = PXN; Hn = PS1; LG = PS1; VV = POH; KT_ = POH; SCR = POH
            rmsnorm_tm(XN, X, n, n2w)
            transpose8(pfb, XN, n)
            for cg in range(8):
                w = load_w("wq", cg)
                for m in range(2):
                    un = cg * 2 + m
                    pq = PSN()
                    for k in range(8):
                        PE(lambda e, pq=pq, w=w, k=k, m=m: e.matmul(pq[:, 0:n], lhsT=w[:, k, m * 128:(m + 1) * 128], rhs=pfb[:, k, :n],
                                                                    start=(k == 0), stop=(k == 7)), r=[w, pfb], w=[pq])
                    dstq = qT0 if un < 8 else qT1
                    A(lambda e, pq=pq, dstq=dstq, un=un: e.copy(out=dstq[:, un % 8, :n], in_=pq[:, 0:n]), r=[pq], w=[dstq])
            S1, S2 = LG, VV
            for half, (Sd, KTb) in enumerate(((S1, KT1), (S2, KT2))):
                for hh in range(2):
                    pp = PSN()
                    for h4 in range(4):
                        h = hh * 4 + h4
                        un = h * 2 + half
                        srcq = qT0 if un < 8 else qT1
                        PE(lambda e, pp=pp, h4=h4, h=h, srcq=srcq, un=un, KTb=KTb: e.matmul(
                            pp[:n, h4 * 128:(h4 + 1) * 128], lhsT=srcq[:, un % 8, :n], rhs=KTb[:, h, :], start=True, stop=True),
                           r=[srcq, KTb], w=[pp])
                    A(lambda e, pp=pp, hh=hh, Sd=Sd: e.copy(out=Sd[:n, hh * 512:(hh + 1) * 512], in_=pp[:n, 0:512]), r=[pp], w=[Sd])

            yield "a"

            def top16(src_ap, srcbuf, width, vals, valbuf, idxs, idxbuf):
                V(lambda e: e.max(out=vals[:, 0:8], in_=src_ap), r=[srcbuf], w=[valbuf])
                V(lambda e: e.match_replace(out=wkm[:n, 0:width], in_to_replace=vals[:, 0:8], in_values=src_ap, imm_value=-1e30),
                  r=[srcbuf, valbuf], w=[wkm])
                V(lambda e: e.max(out=vals[:, 8:16], in_=wkm[:n, 0:width]), r=[wkm], w=[valbuf])
                V(lambda e: e.max_index(out=idxs[:, 0:8], in_max=vals[:, 0:8], in_values=src_ap), r=[srcbuf, valbuf], w=[idxbuf])
                V(lambda e: e.max_index(out=idxs[:, 8:16], in_max=vals[:, 8:16], in_values=src_ap), r=[srcbuf, valbuf], w=[idxbuf])

            for h in range(H):
                for half, Sd in enumerate((S1, S2)):
                    un = h * 2 + half
                    top16(Sd[:n, h * 128:(h + 1) * 128], Sd, 128, tv[:n, un * 16:(un + 1) * 16], tv, ti[:n, un * 16:(un + 1) * 16], ti)
            CAND = KT_
            tv4 = tv[:n, :].rearrange("p (h two i) -> p h two i", two=2, i=16)
            for hh in range(2):
                V(lambda e, hh=hh: e.tensor_tensor(
                    out=CAND[:n, :].rearrange("p (h i j) -> p h i j", i=16, j=16),
                    in0=tv4[:, hh * 4:(hh + 1) * 4, 0, :].unsqueeze(3).to_broadcast([n, 4, 16, 16]),
                    in1=tv4[:, hh * 4:(hh + 1) * 4, 1, :].unsqueeze(2).to_broadcast([n, 4, 16, 16]), op=ALU.add), r=[tv], w=[CAND])
                for h4 in range(4):
                    h = hh * 4 + h4
                    top16(CAND[:n, h4 * 256:(h4 + 1) * 256], CAND, 256, tv2[:n, h * 16:(h + 1) * 16], tv2, ti2[:n, h * 16:(h + 1) * 16], ti2)
            V(lambda e: e.tensor_copy(out=tif[:n, :], in_=ti[:n, :]), r=[ti, tv], w=[tif])
            posf, i_f, j_f, e1f, e2f = (sc2[:n, i_, :] for i_ in range(5))
            V(lambda e: e.tensor_copy(out=posf, in_=ti2[:n, :]), r=[ti2], w=[sc2])
            OH = SCR
            oh3 = OH[:n, 0:2048 // 2].rearrange("p (a b) -> p a b", b=16) if False else None
            for hf in range(2):
                cs_ = slice(hf * 64, (hf + 1) * 64)
                o3 = OH[:n, :].rearrange("p (a b) -> p a b", b=16)
                V(lambda e, cs_=cs_, o3=o3: e.scalar_tensor_tensor(
                    out=o3, in0=posf[:, cs_].unsqueeze(2).to_broadcast([n, 64, 16]), scalar=1.0 / 16,
                    in1=iota16[:n, :].unsqueeze(1).to_broadcast([n, 64, 16]), op0=ALU.mult, op1=ALU.is_ge), r=[sc2, cst], w=[OH])
                V(lambda e, cs_=cs_, o3=o3: e.tensor_reduce(out=i_f[:, cs_], in_=o3, axis=AX.X, op=ALU.add), r=[OH], w=[sc2])
            V(lambda e: e.tensor_scalar_add(out=i_f, in0=i_f, scalar1=-1.0), r=[sc2], w=[sc2])
            V(lambda e: e.scalar_tensor_tensor(out=j_f, in0=i_f, scalar=-16.0, in1=posf, op0=ALU.mult, op1=ALU.add), r=[sc2], w=[sc2])
            tif4 = tif[:n, :].rearrange("p (h two i) -> p h two i", two=2, i=16)
            for which, (sel, dst) in enumerate(((i_f, e1f), (j_f, e2f))):
                for hf in range(2):
                    cs_ = slice(hf * 64, (hf + 1) * 64)
                    o3 = OH[:n, :].rearrange("p (a b) -> p a b", b=16)
                    V(lambda e, sel=sel, cs_=cs_, o3=o3: e.tensor_tensor(
                        out=o3, in0=sel[:, cs_].unsqueeze(2).to_broadcast([n, 64, 16]),
                        in1=iota16[:n, :].unsqueeze(1).to_broadcast([n, 64, 16]), op=ALU.is_equal), r=[sc2, cst], w=[OH])
                    o4 = OH[:n, :].rearrange("p (h k i) -> p h k i", k=16, i=16)
                    V(lambda e, o4=o4, hf=hf, which=which: e.tensor_tensor(
                        out=o4, in0=o4, in1=tif4[:, hf * 4:(hf + 1) * 4, which, :].unsqueeze(2).to_broadcast([n, 4, 16, 16]),
                        op=ALU.mult), r=[OH, tif], w=[OH])
                    V(lambda e, dst=dst, cs_=cs_, o3=o3: e.tensor_reduce(out=dst[:, cs_], in_=o3, axis=AX.X, op=ALU.add), r=[OH], w=[sc2])
            V(lambda e: e.scalar_tensor_tensor(out=eidf[:n, :], in0=e1f, scalar=128.0, in1=e2f, op0=ALU.mult, op1=ALU.add), r=[sc2], w=[eidf])
            V(lambda e: e.tensor_copy(out=eidi[:n, :], in_=eidf[:n, :]), r=[eidf], w=[eidi])
            tv23 = tv2[:n, :].rearrange("p (h k) -> p h k", k=16)
            g3_ = gate[:n, :].rearrange("p (h k) -> p h k", k=16)
            V(lambda e: e.tensor_tensor(out=g3_, in0=tv23, in1=tv23[:, :, 0:1].to_broadcast([n, H, 16]), op=ALU.subtract), r=[tv2], w=[gate])
            A(lambda e: e.activation(out=gate[:n, :], in_=gate[:n, :], func=AF.Exp), r=[gate], w=[gate])
            V(lambda e: e.tensor_reduce(out=sm[:n, 32:40], in_=g3_, axis=AX.X, op=ALU.add), r=[gate], w=[sm])
            V(lambda e: e.reciprocal(out=sm[:n, 40:48], in_=sm[:n, 32:40]), r=[sm], w=[sm])
            V(lambda e: e.tensor_tensor(out=g3_, in0=g3_, in1=sm[:n, 40:48].unsqueeze(2).to_broadcast([n, H, 16]), op=ALU.mult), r=[gate, sm], w=[gate])
            yield "b"
            GBUF = PG
            GRP = 2
            extra = [] if uvb_waited[0] else list(uvb_res)
            uvb_waited[0] = True
            lag_bufs = None
            for grp in range(H * TOPK // GRP):
                bufs = []
                for kk_ in range(GRP):
                    hk = grp * GRP + kk_
                    gb = GBUF[hk % 6]
                    gv = gb[:].bitcast(BF16)
                    DMA(lambda e, gv=gv, hk=hk: e.indirect_dma_start(out=gv, out_offset=None, in_=uvb,
                                                                    in_offset=bass.IndirectOffsetOnAxis(ap=eidi[:, hk:hk + 1], axis=0)),
                        r=[eidi] + extra, w=[gb], q="pool")
                    extra = []
                    bufs.append((gb, gv, hk))
                for gb, gv, hk in bufs:
                    V(lambda e, gv=gv, hk=hk: e.scalar_tensor_tensor(out=SCR[:].bitcast(BF16)[:n, 0:1024], in0=gv[:n, 0:1024], scalar=1.0, in1=XN[:n, :], op0=ALU.mult,
                                                                     op1=ALU.mult, accum_out=dots[:n, hk:hk + 1]), r=[gb, XN], w=[SCR, dots])
                    yield "d"
                gsl = slice(grp * GRP, (grp + 1) * GRP)
                A(lambda e, gsl=gsl: e.activation(out=coef[:n, gsl], in_=dots[:n, gsl], func=AF.Gelu), r=[dots], w=[coef])
                V(lambda e, gsl=gsl: e.tensor_tensor(out=coef[:n, gsl], in0=coef[:n, gsl], in1=gate[:n, gsl], op=ALU.mult), r=[coef, gate], w=[coef])
                def emit_acc(bufs_):
                    for gb, gv, hk in bufs_:
                        dg = DG[hk % 2]
                        A(lambda e, dg=dg, hk=hk: e.activation(out=dg[:n, :n], in_=cst[:n, C_ID:C_ID + n], func=AF.Copy, scale=coef[:n, hk:hk + 1]),
                          r=[cst, coef], w=[dg])
                        for hf in range(2):
                            PE(lambda e, dg=dg, gv=gv, hk=hk, hf=hf: e.matmul(PA[:n, hf * 512:(hf + 1) * 512], lhsT=dg[:n, :n],
                                                                              rhs=gv[:n, 1024 + hf * 512:1024 + (hf + 1) * 512],
                                                                              start=(hk == 0), stop=(hk == H * TOPK - 1)), r=[dg, gb], w=[PA])
                if lag_bufs is not None:
                    emit_acc(lag_bufs)
                lag_bufs = bufs
                yield "k"
            emit_acc(lag_bufs)
            yield "kend"
            ACC = PA
            V(lambda e: e.tensor_tensor(out=X[:n, :], in0=X[:n, :], in1=ACC[:n, :], op=ALU.add), r=[X, ACC], w=[X])
            YO = Hn
            rmsnorm_tm(YO, X, n, fnw)
            if kind == "prompt":
                stq(y_p[idx * 128:(idx + 1) * 128, :], YO[:n, :], YO)
            else:
                stq(y_s, YO[:n, :], YO)

        DUMPS = os.environ.get('KDUMP', '')
        tiles = [("meta", 0)] + [("prompt", i) for i in range(NPT)] + ([("sample", 0)] if DO_SAMPLE else [])
        if STAGE == 0:
            tiles = []
        def run_to(gen, marks):
            if gen is None:
                return None
            for m_ in gen:
                if m_ in marks:
                    return m_
            return None

        pend = None
        PMARK = ("a", "b", "d", "k", "kend")
        UG = uvb_gen()
        ug_alive = [True]

        def ug_step(k_):
            for _i in range(k_):
                if ug_alive[0] and next(UG, None) is None:
                    ug_alive[0] = False
        for kind, idx in tiles:
            L = layer_gen(kind, idx)
            nn = {"meta": NMETA, "prompt": 128, "sample": NSAMP * LS}[kind]
            p_alive = pend is not None
            run_to(pend, ("a",))
            lm = run_to(L, ("A",))
            if p_alive and run_to(pend, ("b",)) is None:
                p_alive = False
            while lm is not None:
                lm = run_to(L, ("y", "A", "B", "S1", "S", "C"))
                if lm in ("y", "S"):
                    ug_step(1)
                if lm is None or lm == "C":
                    break
                if p_alive and lm != "S1":
                    for _rep in range(1 if lm == "S" else 2):
                        if run_to(pend, PMARK) is None:
                            p_alive = False
                            break
            if p_alive:
                run_to(pend, ("__end__",))
            pend = None
            if lm == "C":
                run_to(L, ("__end__",))
                if DO_PEER:
                    ug_step(128)
                    pend = peer_gen(kind, idx, nn)
        run_to(pend, ("__end__",))
        P.finish()
        P.emit(block)
    return nc


_CACHE = {}


def kernel(**inp):
    f32 = lambda a: np.ascontiguousarray(np.asarray(a, dtype=np.float32))
    inp = {k: f32(v) for k, v in inp.items()}
    if "nc" not in _CACHE:
        _CACHE["nc"] = build_program()
    nc = _CACHE["nc"]
    consts = make_consts()
    shared = {k: inp[k] for k in ("lower_bounds", "norm1_w", "w_in", "g_norm_w", "ssm_a_re", "ssm_a_im", "ssm_log_step",
                                  "ssm_b_re", "ssm_b_im", "ssm_c_re", "ssm_c_im", "ssm_d", "w_glu", "b_glu", "w_branch_a",
                                  "w_branch_b", "w_out", "norm2_w", "peer_wq", "peer_k1", "peer_k2", "peer_u", "peer_v",
                                  "final_norm_w")}
    in_maps = []
    for c in range(NCORES):
        m = dict(shared)
        m["consts"] = consts
        m["xp"] = inp["x_prompt"][c]
        m["xs"] = f32(inp["x_sample"][c * NSAMP:(c + 1) * NSAMP].reshape(NSAMP * LS, D))
        m["meta"] = inp["meta_tokens"]
        m["st_hg"] = f32(inp["state_hgrn"][0, c * NSAMP:(c + 1) * NSAMP])
        m["st_re"] = f32(inp["state_ssm_re"][0, c * NSAMP:(c + 1) * NSAMP].reshape(NSAMP * G, PS))
        m["st_im"] = f32(inp["state_ssm_im"][0, c * NSAMP:(c + 1) * NSAMP].reshape(NSAMP * G, PS))
        in_maps.append(m)
    res = run_bass_kernel_spmd(nc, in_maps, core_ids=list(range(NCORES)))
    rs = res.results
    y_prompt = np.stack([rs[c]["y_p"] for c in range(NCORES)]).astype(np.float32)
    y_sample = np.concatenate([rs[c]["y_s"].reshape(NSAMP, LS, D) for c in range(NCORES)]).astype(np.float32)
    hg_p = np.stack([rs[c]["hg_p"] for c in range(NCORES)])[None].astype(np.float32)
    re_p = np.stack([rs[c]["re_p"] for c in range(NCORES)])[None].astype(np.float32)
    im_p = np.stack([rs[c]["im_p"] for c in range(NCORES)])[None].astype(np.float32)
    hg_s = np.concatenate([rs[c]["hg_s"] for c in range(NCORES)])[None].astype(np.float32)
    re_s = np.concatenate([rs[c]["re_s"].reshape(NSAMP, G, PS) for c in range(NCORES)])[None].astype(np.float32)
    im_s = np.concatenate([rs[c]["im_s"].reshape(NSAMP, G, PS) for c in range(NCORES)])[None].astype(np.float32)
    return (y_prompt, y_sample, hg_p, re_p, im_p, hg_s, re_s, im_s)
```

```python
import os
import numpy as np
import concourse.bass as bass
import concourse.mybir as mybir
from concourse.bass_utils import run_bass_kernel_spmd

F32 = mybir.dt.float32
F32R = mybir.dt.float32r
I32 = mybir.dt.int32
U32 = mybir.dt.uint32
BF16 = mybir.dt.bfloat16
ALU = mybir.AluOpType
AF = mybir.ActivationFunctionType
AX = mybir.AxisListType


class Res:
    __slots__ = ("name", "w", "r")

    def __init__(self, name):
        self.name = name
        self.w = None
        self.r = {}


import types


def _freeze(fn):
    if fn.__closure__ is None:
        return fn
    cells = tuple(types.CellType(c.cell_contents) for c in fn.__closure__)
    return types.FunctionType(fn.__code__, fn.__globals__, fn.__name__, fn.__defaults__, cells)


class Prog:
    CE = ("pe", "act", "dve", "pool")

    def __init__(self, nc, n_dma_sems=12):
        self.nc = nc
        self.q = {e: [] for e in ("pe", "act", "dve", "pool", "sp")}
        self.cnt = {e: 0 for e in self.CE}
        self.waited = {e: {} for e in self.q}
        self.sems = {}
        self.dma_pool = {}
        self.dma_next = {}
        self.dma_tgt = {}
        self.n_dma_sems = n_dma_sems
        self.out_events = []
        self.n_instr = 0

    def alloc_sems(self, stack):
        nc = self.nc
        for e in self.CE:
            self.sems[("e", e)] = stack.enter_context(nc.semaphore("sem_" + e))
        for qn in ("sp", "pool", "act"):
            self.dma_pool[qn] = []
            for i in range(self.n_dma_sems):
                k = ("d", qn, i)
                self.sems[k] = stack.enter_context(nc.semaphore("dsem_%s_%d" % (qn, i)))
                self.dma_pool[qn].append(k)
                self.dma_tgt[k] = 0
            self.dma_next[qn] = 0

    def _need(self, eng, ev):
        if ev is None:
            return
        key, val = ev
        if key == ("e", "pe") and eng == "pe":
            return
        if self.waited[eng].get(key, 0) >= val:
            return
        self.waited[eng][key] = val
        sem = self.sems[key]
        self.q[eng].append(lambda e, s=sem, v=val: e.wait_ge(s, v))

    def _deps(self, eng, r, w):
        for x in r:
            self._need(eng, x.w)
        for x in w:
            self._need(eng, x.w)
            for k_, v_ in x.r.items():
                self._need(eng, (k_, v_))

    def _commit(self, ev, r, w):
        for x in w:
            x.w = ev
            x.r = {}
        for x in r:
            if x not in w:
                if x.r.get(ev[0], 0) < ev[1]:
                    x.r[ev[0]] = ev[1]

    def op(self, eng, fn, r=(), w=()):
        fn = _freeze(fn)
        self._deps(eng, r, w)
        self.cnt[eng] += 1
        n = self.cnt[eng]
        key = ("e", eng)
        sem = self.sems[key]
        self.q[eng].append(lambda e, f=fn, s=sem: f(e).then_inc(s, 1))
        self.waited[eng][key] = max(self.waited[eng].get(key, 0), 0)
        self._commit((key, n), r, w)
        self.n_instr += 1

    def dma(self, qn, fn, r=(), w=(), is_output=False):
        fn = _freeze(fn)
        self._deps(qn, r, w)
        i = self.dma_next[qn]
        self.dma_next[qn] = (i + 1) % self.n_dma_sems
        k = self.dma_pool[qn][i]
        if self.dma_tgt[k] > 0:
            self._need(qn, (k, self.dma_tgt[k]))
        self.dma_tgt[k] += 16
        tgt = self.dma_tgt[k]
        sem = self.sems[k]
        self.q[qn].append(lambda e, f=fn, s=sem: f(e).then_inc(s, 16))
        ev = (k, tgt)
        self._commit(ev, r, w)
        if is_output:
            self.out_events.append(ev)
        self.n_instr += 1

    def pe_drain(self):
        n = self.cnt["pe"]
        if n:
            sem = self.sems[("e", "pe")]
            self.q["pe"].append(lambda e, s=sem, v=n: e.wait_ge(s, v))

    def finish(self):
        for ev in self.out_events:
            self._need("sp", ev)
        for e in self.CE:
            if self.cnt[e]:
                self._need("sp", (("e", e), self.cnt[e]))

    def emit(self, block):
        q = self.q

        @block.sync
        def _(eng):
            for f in q["sp"]:
                f(eng)

        @block.tensor
        def _(eng):
            for f in q["pe"]:
                f(eng)

        @block.scalar
        def _(eng):
            for f in q["act"]:
                f(eng)

        @block.vector
        def _(eng):
            for f in q["dve"]:
                f(eng)

        @block.gpsimd
        def _(eng):
            for f in q["pool"]:
                f(eng)


D = 1024
NCORES = 8
SEQ = 2048
NMETA = 16
NSAMP = 16
LS = 4
H = 8
G = 64
PS = 64
TOPK = 16
EPS = 1e-6
TWO_PI = 2.0 * np.pi

C_ID, C_TRI, C_ONE, C_MS, C_TS, C_M2, C_IO, C_MQ, C_END = 0, 128, 256, 384, 448, 512, 528, 544, 544 + 1024


def make_consts():
    c = np.zeros((128, C_END), np.float32)
    c[:, C_ID:C_ID + 128] = np.eye(128)
    c[:, C_TRI:C_TRI + 128] = np.triu(np.ones((128, 128)))
    c[:, C_ONE:C_ONE + 128] = 1.0
    seq = np.arange(64) // LS
    same = (seq[:, None] == seq[None, :])
    c[:64, C_MS:C_MS + 64] = same & (np.arange(64)[:, None] <= np.arange(64)[None, :])
    c[:64, C_TS:C_TS + 64] = same
    c[:64, C_M2:C_M2 + 16] = (seq[:, None] == np.arange(16)[None, :])
    c[:, C_IO:C_IO + 16] = np.arange(16)[None, :]
    mq = (np.arange(16)[:, None] == seq[None, :]).astype(np.float32).reshape(1, 1024)
    c[:, C_MQ:C_MQ + 1024] = mq
    return c


class Buf:
    def __init__(self, t, name):
        self.t = t
        self.r = Res(name)

    def __getitem__(self, k):
        return self.t[k]


def build_program(NPT=16, DO_SAMPLE=True, DO_PEER=True, STAGE=9):
    from contextlib import ExitStack
    nc = bass.Bass("TRN2", target_bir_lowering=False)

    def din(name, shape, dt=F32):
        return nc.dram_tensor(name, list(shape), dt, kind="ExternalInput").ap()

    def dout(name, shape, dt=F32):
        return nc.dram_tensor(name, list(shape), dt, kind="ExternalOutput").ap()

    xp = din("xp", [SEQ, D]); xs = din("xs", [NSAMP * LS, D]); meta = din("meta", [NMETA, D])
    st_hg = din("st_hg", [NSAMP, H, 128, 128]); st_re = din("st_re", [NSAMP * G, PS]); st_im = din("st_im", [NSAMP * G, PS])
    consts = din("consts", [128, C_END])
    lower_bounds = din("lower_bounds", [2, D]); norm1_w = din("norm1_w", [1, D]); w_in = din("w_in", [1, D, 7168])
    g_norm_w = din("g_norm_w", [1, 128]); a_re = din("ssm_a_re", [1, G, PS]); a_im = din("ssm_a_im", [1, G, PS])
    log_step = din("ssm_log_step", [1, G]); b_re = din("ssm_b_re", [1, G, PS, 16]); b_im = din("ssm_b_im", [1, G, PS, 16])
    c_re = din("ssm_c_re", [1, G, 16, PS]); c_im = din("ssm_c_im", [1, G, 16, PS]); ssm_d = din("ssm_d", [1, D])
    w_glu = din("w_glu", [1, D, D]); b_glu = din("b_glu", [1, D]); w_a = din("w_branch_a", [1, D, D])
    w_b = din("w_branch_b", [1, D, D]); w_out = din("w_out", [1, D, D]); norm2_w = din("norm2_w", [1, D])
    peer_wq = din("peer_wq", [1, D, 2048]); peer_k1 = din("peer_k1", [1, H, 128, 128]); peer_k2 = din("peer_k2", [1, H, 128, 128])
    peer_u = din("peer_u", [1, 16384, D]); peer_v = din("peer_v", [1, 16384, D]); final_norm_w = din("final_norm_w", [D])

    y_p = dout("y_p", [SEQ, D]); y_s = dout("y_s", [NSAMP * LS, D])
    hg_p = dout("hg_p", [H, 128, 128]); re_p = dout("re_p", [G, PS]); im_p = dout("im_p", [G, PS])
    hg_s = dout("hg_s", [NSAMP, H, 128, 128]); re_s = dout("re_s", [NSAMP * G, PS]); im_s = dout("im_s", [NSAMP * G, PS])

    with ExitStack() as st:
        P = Prog(nc)
        P.alloc_sems(st)

        def SB(name, shape, dt=F32):
            return Buf(st.enter_context(nc.sbuf_tensor(name, list(shape), dt)), name)

        ps2 = [Buf(st.enter_context(nc.psum_tensor("ps%d" % i, [128, 1024], F32)), "ps%d" % i) for i in range(4)]
        psc = [0]

        def PSN():
            psc[0] = psc[0] % 2 + 2
            return ps2[psc[0]]

        def R(bufs):
            return [b.r for b in bufs]

        def V(fn, r=(), w=()):
            P.op("dve", fn, r=R(r), w=R(w))

        def A(fn, r=(), w=()):
            P.op("act", fn, r=R(r), w=R(w))

        def GP(fn, r=(), w=()):
            P.op("pool", fn, r=R(r), w=R(w))

        def PE(fn, r=(), w=()):
            P.op("pe", fn, r=R(r), w=R(w))

        def DMA(fn, r=(), w=(), q="sp", out=False):
            P.dma(q, fn, r=R(r), w=R(w), is_output=out)

        def ld(dst_ap, src_ap, wbuf, q="sp", slow=False):
            if slow:
                DMA(lambda e: e.dma_start(out=dst_ap, in_=src_ap, allow_slow_non_contiguous=True), w=[wbuf], q=q)
            else:
                DMA(lambda e: e.dma_start(out=dst_ap, in_=src_ap), w=[wbuf], q=q)

        def stq(dst_ap, src_ap, rbuf, q="pool"):
            DMA(lambda e: e.dma_start(out=dst_ap, in_=src_ap), r=[rbuf], q=q, out=True)

        cst = SB("cst", [128, C_MQ])
        ident = cst[:, C_ID:C_ID + 128]
        n1w = SB("n1w", [128, D], BF16); n2w = SB("n2w", [128, D], BF16); fnw = SB("fnw", [128, D], BF16)
        lbt = SB("lbt", [128, D]); gnw8 = SB("gnw8", [128, D], BF16); dvec = SB("dvec", [128, D], BF16)
        bglu = SB("bglu", [128, 8])
        wb = [SB("wb%d" % i, [128, 8, 256], BF16) for i in range(3)]
        fb = [SB("fb%d" % i, [128, 8, 128], BF16) for i in range(3)]
        qT0, qT1 = fb[1], fb[2]
        wbc = [0]
        NTM = 13
        tm = [SB("tm%d" % i, [128, D]) for i in range(NTM)]
        fm = [None, SB("fm1", [128, 8, 128]), SB("fm2", [128, 8, 128]), None]

        Shg = SB("Shg", [128, H, 128])
        attm = SB("attm", [128, 128])
        gdec = SB("gdec", [128, H, NSAMP])
        sm = SB("sm", [128, 64])
        Bmat = SB("Bmat", [128, 11, 2, 2, 128], BF16)
        ub16 = SB("ub16", [128, 11, 128], BF16)
        KT1 = SB("KT1", [128, 8, 128], BF16); KT2 = SB("KT2", [128, 8, 128], BF16)
        Cmat = SB("Cmat", [128, G, 16])
        ArAr = SB("ArAr", [128, 2, G]); AiPN = SB("AiPN", [128, 2, G])
        SBLK = 64
        SHb = SB("SHb", [128, SBLK, 2, G])

        class View:
            def __init__(self, ap, res):
                self.ap = ap
                self.r = res

            def __getitem__(self, k):
                return self.ap[k]

        _shf = SHb[:].rearrange("p t v g -> p (t v g)")
        Ssm = View(_shf[:, 0:2048].rearrange("p (s v) -> p s v", v=128), SHb.r)
        Zq = View(_shf[:, 2048:3072].rearrange("p (s t) -> p s t", t=64), SHb.r)
        Zv = View(_shf[0:64, 3072:4096].rearrange("p (s v) -> p s v", v=128), SHb.r)
        Scar = SB("Scar", [128, 1, 2, G])
        sT1 = SB("sT1", [128, 1, 2, G]); sT2 = SB("sT2", [128, 1, 2, G])
        sT2a = sT2; sT2b = Buf(sT2.t, "sT2b")

        block = st.enter_context(nc.Block())

        ld(cst[:], consts[:, 0:C_MQ], cst)
        ld(n1w[:], norm1_w[0, :].partition_broadcast(128), n1w, q="pool")
        ld(n2w[:], norm2_w[0, :].partition_broadcast(128), n2w, q="pool")
        ld(fnw[:], final_norm_w.partition_broadcast(128), fnw, q="pool")
        ld(dvec[:], ssm_d[0, :].partition_broadcast(128), dvec, q="pool")
        ld(lbt[:], lower_bounds[0, :].partition_broadcast(128), lbt)
        oml = tm[12]
        ld(oml[:], lower_bounds[1, :].partition_broadcast(128), oml)
        for h in range(H):
            ld(gnw8[:, h * 128:(h + 1) * 128], g_norm_w[0, :].partition_broadcast(128), gnw8, q="pool")
        ld(bglu[:], b_glu[0].rearrange("(j p) -> p j", p=128), bglu, slow=True)
        V(lambda e: e.tensor_tensor(out=lbt[:], in0=lbt[:], in1=oml[:], op=ALU.subtract), r=[lbt, oml], w=[lbt])
        A(lambda e: e.activation(out=lbt[:], in_=lbt[:], func=AF.Sigmoid), r=[lbt], w=[lbt])

        are, aim, lst, mag, ang, cs, sn, wk1, wk2, wk3 = [tm[i] for i in range(10)]
        for half in range(2):
            ld(are[half * 64:(half + 1) * 64, 0:G], a_re[0].rearrange("g p -> p g"), are, slow=True)
            ld(aim[half * 64:(half + 1) * 64, 0:G], a_im[0].rearrange("g p -> p g"), aim, slow=True)
        ld(lst[:, 0:G], log_step[0, :].partition_broadcast(128), lst)
        g64 = slice(0, G)
        A(lambda e: e.activation(out=lst[:, g64], in_=lst[:, g64], func=AF.Exp), r=[lst], w=[lst])
        V(lambda e: e.tensor_scalar_min(out=are[:, g64], in0=are[:, g64], scalar1=-1e-4), r=[are], w=[are])
        V(lambda e: e.tensor_tensor(out=mag[:, g64], in0=are[:, g64], in1=lst[:, g64], op=ALU.mult), r=[are, lst], w=[mag])
        A(lambda e: e.activation(out=mag[:, g64], in_=mag[:, g64], func=AF.Exp), r=[mag], w=[mag])
        V(lambda e: e.tensor_tensor(out=ang[:, g64], in0=aim[:, g64], in1=lst[:, g64], op=ALU.mult), r=[aim, lst], w=[ang])

        wki = SB("wki", [128, G], I32)

        def sin_of(dst, src, shift):
            V(lambda e: e.tensor_scalar(out=wk1[:, g64], in0=src[:, g64], scalar1=1.0 / TWO_PI, scalar2=shift / TWO_PI,
                                        op0=ALU.mult, op1=ALU.add), r=[src], w=[wk1])
            V(lambda e: e.tensor_copy(out=wki[:], in_=wk1[:, g64]), r=[wk1], w=[wki])
            V(lambda e: e.tensor_copy(out=wk2[:, g64], in_=wki[:]), r=[wki], w=[wk2])
            V(lambda e: e.tensor_tensor(out=wk1[:, g64], in0=wk1[:, g64], in1=wk2[:, g64], op=ALU.subtract), r=[wk1, wk2], w=[wk1])
            V(lambda e: e.tensor_scalar(out=wk2[:, g64], in0=wk1[:, g64], scalar1=0.5, scalar2=None, op0=ALU.is_gt), r=[wk1], w=[wk2])
            V(lambda e: e.tensor_tensor(out=wk1[:, g64], in0=wk1[:, g64], in1=wk2[:, g64], op=ALU.subtract), r=[wk1, wk2], w=[wk1])
            V(lambda e: e.tensor_scalar(out=wk2[:, g64], in0=wk1[:, g64], scalar1=-0.5, scalar2=None, op0=ALU.is_lt), r=[wk1], w=[wk2])
            V(lambda e: e.tensor_tensor(out=wk1[:, g64], in0=wk1[:, g64], in1=wk2[:, g64], op=ALU.add), r=[wk1, wk2], w=[wk1])
            V(lambda e: e.tensor_scalar(out=wk1[:, g64], in0=wk1[:, g64], scalar1=TWO_PI, scalar2=3.14159,
                                        op0=ALU.mult, op1=ALU.min), r=[wk1], w=[wk1])
            V(lambda e: e.tensor_scalar_max(out=wk1[:, g64], in0=wk1[:, g64], scalar1=-3.14159), r=[wk1], w=[wk1])
            A(lambda e: e.activation(out=dst[:, g64], in_=wk1[:, g64], func=AF.Sin), r=[wk1], w=[dst])

        sin_of(sn, ang, 0.0)
        sin_of(cs, ang, np.pi / 2)
        V(lambda e: e.tensor_tensor(out=cs[:, g64], in0=cs[:, g64], in1=mag[:, g64], op=ALU.mult), r=[cs, mag], w=[cs])
        V(lambda e: e.tensor_tensor(out=sn[:, g64], in0=sn[:, g64], in1=mag[:, g64], op=ALU.mult), r=[sn, mag], w=[sn])
        V(lambda e: e.tensor_copy(out=ArAr[:, 0, :], in_=cs[:, g64]), r=[cs], w=[ArAr])
        V(lambda e: e.tensor_copy(out=ArAr[:, 1, :], in_=cs[:, g64]), r=[cs], w=[ArAr])
        V(lambda e: e.tensor_copy(out=AiPN[:, 0, :], in_=sn[:, g64]), r=[sn], w=[AiPN])
        V(lambda e: e.tensor_scalar_mul(out=AiPN[:, 1, :], in0=sn[:, g64], scalar1=-1.0), r=[sn], w=[AiPN])
        V(lambda e: e.tensor_scalar_add(out=wk1[:, g64], in0=cs[:, g64], scalar1=-1.0), r=[cs], w=[wk1])
        V(lambda e: e.tensor_tensor(out=wk2[:, g64], in0=are[:, g64], in1=are[:, g64], op=ALU.mult), r=[are], w=[wk2])
        V(lambda e: e.tensor_tensor(out=wk3[:, g64], in0=aim[:, g64], in1=aim[:, g64], op=ALU.mult), r=[aim], w=[wk3])
        V(lambda e: e.tensor_tensor(out=wk2[:, g64], in0=wk2[:, g64], in1=wk3[:, g64], op=ALU.add), r=[wk2, wk3], w=[wk2])
        V(lambda e: e.reciprocal(out=wk2[:, g64], in_=wk2[:, g64]), r=[wk2], w=[wk2])
        cr, ci = mag, ang
        V(lambda e: e.tensor_tensor(out=cr[:, g64], in0=wk1[:, g64], in1=are[:, g64], op=ALU.mult), r=[wk1, are], w=[cr])
        V(lambda e: e.tensor_tensor(out=wk3[:, g64], in0=sn[:, g64], in1=aim[:, g64], op=ALU.mult), r=[sn, aim], w=[wk3])
        V(lambda e: e.tensor_tensor(out=cr[:, g64], in0=cr[:, g64], in1=wk3[:, g64], op=ALU.add), r=[cr, wk3], w=[cr])
        V(lambda e: e.tensor_tensor(out=cr[:, g64], in0=cr[:, g64], in1=wk2[:, g64], op=ALU.mult), r=[cr, wk2], w=[cr])
        V(lambda e: e.tensor_tensor(out=ci[:, g64], in0=sn[:, g64], in1=are[:, g64], op=ALU.mult), r=[sn, are], w=[ci])
        V(lambda e: e.tensor_tensor(out=wk3[:, g64], in0=wk1[:, g64], in1=aim[:, g64], op=ALU.mult), r=[wk1, aim], w=[wk3])
        V(lambda e: e.tensor_tensor(out=ci[:, g64], in0=ci[:, g64], in1=wk3[:, g64], op=ALU.subtract), r=[ci, wk3], w=[ci])
        V(lambda e: e.tensor_tensor(out=ci[:, g64], in0=ci[:, g64], in1=wk2[:, g64], op=ALU.mult), r=[ci, wk2], w=[ci])
        V(lambda e: e.tensor_copy(out=wk1[0:64, g64], in_=cr[0:64, g64]), r=[cr], w=[wk1])
        V(lambda e: e.tensor_copy(out=wk1[64:128, g64], in_=ci[64:128, g64]), r=[ci], w=[wk1])
        V(lambda e: e.tensor_scalar_mul(out=wk2[0:64, g64], in0=ci[0:64, g64], scalar1=-1.0), r=[ci], w=[wk2])
        V(lambda e: e.tensor_copy(out=wk2[64:128, g64], in_=cr[64:128, g64]), r=[cr], w=[wk2])
        BR2, BI2, V0, V1, VM = tm[10], tm[11], tm[12], tm[0], tm[1]
        for half in range(2):
            for gh in range(2):
                gs_ = slice(gh * 32, (gh + 1) * 32)
                ld(BR2[half * 64:(half + 1) * 64, gh * 512:(gh + 1) * 512].rearrange("p (g c) -> p g c", c=16),
                   b_re[0][gs_].rearrange("g p c -> p g c"), BR2, slow=True)
                ld(BI2[half * 64:(half + 1) * 64, gh * 512:(gh + 1) * 512].rearrange("p (g c) -> p g c", c=16),
                   b_im[0][gs_].rearrange("g p c -> p g c"), BI2, slow=True)

        def g3(buf):
            return buf[:].rearrange("p (g c) -> p g c", c=16)

        def bc16(buf):
            return buf[:, g64].unsqueeze(2).to_broadcast([128, G, 16])

        V(lambda e: e.tensor_tensor(out=g3(V0), in0=g3(BR2), in1=bc16(wk1), op=ALU.mult), r=[BR2, wk1], w=[V0])
        V(lambda e: e.tensor_tensor(out=g3(V1), in0=g3(BI2), in1=bc16(wk2), op=ALU.mult), r=[BI2, wk2], w=[V1])
        V(lambda e: e.tensor_tensor(out=V0[:], in0=V0[:], in1=V1[:], op=ALU.add), r=[V0, V1], w=[V0])
        V(lambda e: e.tensor_tensor(out=g3(V1), in0=g3(BR2), in1=bc16(wk2), op=ALU.mult), r=[BR2, wk2], w=[V1])
        V(lambda e: e.tensor_tensor(out=g3(BR2), in0=g3(BI2), in1=bc16(wk1), op=ALU.mult), r=[BI2, wk1], w=[BR2])
        V(lambda e: e.tensor_tensor(out=V1[:], in0=V1[:], in1=BR2[:], op=ALU.subtract), r=[V1, BR2], w=[V1])
        for v, Vv in enumerate((V0, V1)):
            for gl in range(2):
                V(lambda e: e.memset(VM[:], 0.0), w=[VM])
                V(lambda e, Vv=Vv, gl=gl: e.tensor_copy(out=g3(VM)[:, gl::2, :], in_=g3(Vv)[:, gl::2, :]), r=[Vv], w=[VM])
                for j in range(11):
                    wd = min(96, 1024 - j * 96)
                    pp = PSN()
                    PE(lambda e, pp=pp, j=j, wd=wd: e.transpose(out=pp[:wd, 0:128], in_=VM[:, j * 96:j * 96 + wd], identity=ident),
                       r=[VM, cst], w=[pp])
                    A(lambda e, pp=pp, j=j, gl=gl, v=v, wd=wd: e.copy(out=Bmat[:wd, j, gl, v, :], in_=pp[:wd, 0:128]), r=[pp], w=[Bmat])
        Cn = tm[2]
        ld(Cn[:].rearrange("p (j q) -> p j q", q=128)[:, :, 0:64], c_re[0].rearrange("(j g) c p -> (g c) j p", j=8), Cn)
        ld(Cn[:].rearrange("p (j q) -> p j q", q=128)[:, :, 64:128], c_im[0].rearrange("(j g) c p -> (g c) j p", j=8), Cn)
        V(lambda e: e.tensor_scalar_mul(out=Cn[:].rearrange("p (j q) -> p j q", q=128)[:, :, 64:128],
                                        in0=Cn[:].rearrange("p (j q) -> p j q", q=128)[:, :, 64:128], scalar1=-1.0), r=[Cn], w=[Cn])
        for j in range(8):
            pp = PSN()
            PE(lambda e, pp=pp, j=j: e.transpose(out=pp[:, 0:128], in_=Cn[:, j * 128:(j + 1) * 128], identity=ident),
               r=[Cn, cst], w=[pp])
            A(lambda e, pp=pp, j=j: e.copy(out=Cmat[:, j * 8:(j + 1) * 8, :],
                                           in_=pp[:, 0:128].rearrange("p (g c) -> p g c", c=16)), r=[pp], w=[Cmat])

        if os.environ.get('KDUMP', '') == 'abar':
            dbg = dout("dbg", [128, 1024])
            stq(dbg[:, 0:64], ArAr[:, 0, :], ArAr)
            stq(dbg[:, 64:128], AiPN[:, 0, :], AiPN)
            stq(dbg[:, 128:192], wk1[:, g64], wk1)
            stq(dbg[:, 192:256], wk2[:, g64], wk2)
            stq(dbg[:, 512:768], Cmat[:, 0:16, :].rearrange("p g c -> p (g c)"), Cmat)
        KN1, KN2 = tm[8], tm[9]
        ld(KN1[:].rearrange("p (h d) -> p h d", d=128), peer_k1[0].rearrange("h n d -> n h d"), KN1)
        ld(KN2[:].rearrange("p (h d) -> p h d", d=128), peer_k2[0].rearrange("h n d -> n h d"), KN2)
        for KNx, KTx in ((KN1, KT1), (KN2, KT2)):
            ppk = PSN()
            for j in range(8):
                PE(lambda e, ppk=ppk, j=j, KNx=KNx: e.transpose(out=ppk[:, j * 128:(j + 1) * 128], in_=KNx[:, j * 128:(j + 1) * 128],
                                                              identity=ident), r=[KNx, cst], w=[ppk])
            A(lambda e, ppk=ppk, KTx=KTx: e.copy(out=KTx[:], in_=ppk[:, :].rearrange("p (j t) -> p j t", t=128)), r=[ppk], w=[KTx])
        WBASE = {}
        wlist = (("w_in", w_in, 28), ("w_a", w_a, 4), ("w_glu", w_glu, 4), ("w_b", w_b, 4), ("w_out", w_out, 4), ("wq", peer_wq, 8))
        ncg_tot = sum(x[2] for x in wlist)
        wsc = nc.dram_tensor("w_scratch", [ncg_tot, 128, 2048], BF16, kind="Internal").ap()
        wsc_res = [Buf(None, "wsc%d" % i_) for i_ in range(ncg_tot)]
        WSRC = {}
        gi_ = 0
        for nm_, Wd_, ncg_ in wlist:
            WBASE[nm_] = gi_
            WSRC[nm_] = Wd_
            gi_ += ncg_
        wsc_conv = set()
        wsc_seen = set()
        ubt = nc.dram_tensor("ub_scratch", [16384, 1024], BF16, kind="Internal").ap()
        vbt = nc.dram_tensor("vb_scratch", [16384, 1024], BF16, kind="Internal").ap()
        uvb_res = [Buf(None, "uvb%d" % c) for c in range(128)]

        def uvb_gen():
            for c in range(128):
                stg = PG[c % 6]
                sv = stg[:].bitcast(BF16)
                DMA(lambda e, sv=sv, c=c: e.dma_start(out=sv[:, 0:1024], in_=peer_u[0][c * 128:(c + 1) * 128, :]), w=[stg, PGU[c % 6], PGV[c % 6]], q="pool")
                DMA(lambda e, sv=sv, c=c: e.dma_start(out=sv[:, 1024:2048], in_=peer_v[0][c * 128:(c + 1) * 128, :]), w=[stg], q="pool")
                DMA(lambda e, sv=sv, c=c: e.dma_start(out=ubt[c * 128:(c + 1) * 128, :], in_=sv[:, 0:1024]), r=[stg], w=[uvb_res[c]], q="sp")
                DMA(lambda e, sv=sv, c=c: e.dma_start(out=vbt[c * 128:(c + 1) * 128, :], in_=sv[:, 1024:2048]), r=[stg], w=[uvb_res[c]], q="sp")
                yield "u"
        uvb_waited = [False]
        def transpose8(dst_fm, src_tm, n, nblk=8, evac="act"):
            pp = PSN()
            for j in range(nblk):
                PE(lambda e, pp=pp, j=j: e.transpose(out=pp[:, j * n:(j + 1) * n], in_=src_tm[:n, j * 128:(j + 1) * 128],
                                                     identity=cst[:n, C_ID:C_ID + n]), r=[src_tm, cst], w=[pp])
            src = pp[:, 0:nblk * n].rearrange("p (j t) -> p j t", t=n)
            if evac == "act":
                A(lambda e: e.copy(out=dst_fm[:, 0:nblk, :n], in_=src), r=[pp], w=[dst_fm])
            else:
                V(lambda e: e.tensor_copy(out=dst_fm[:, 0:nblk, :n], in_=src), r=[pp], w=[dst_fm])

        def load_w(wname, cg):
            i = wbc[0]
            wbc[0] = (i + 1) % 3
            w = wb[i]
            gidx = WBASE[wname] + cg
            if gidx not in wsc_conv:
                wsc_conv.add(gidx)
                Wd_ = WSRC[wname]
                DMA(lambda e, w=w, Wd_=Wd_, cg=cg: e.dma_start(out=w[:], in_=Wd_[0][:, cg * 256:(cg + 1) * 256].rearrange("(k p) c -> p k c", p=128)),
                    w=[w], q="pool")
                DMA(lambda e, w=w, gidx=gidx: e.dma_start(out=wsc[gidx], in_=w[:].rearrange("p k c -> p (k c)")), r=[w], w=[wsc_res[gidx]], q="sp")
                return w
            extra = [] if gidx in wsc_seen else [wsc_res[gidx]]
            wsc_seen.add(gidx)
            DMA(lambda e, w=w, gidx=gidx: e.dma_start(out=w[:].rearrange("p k c -> p (k c)"), in_=wsc[gidx]), r=extra, w=[w], q="sp")
            return w

        def proj_tm(lhsT, n, Wd, cgs, evac):
            for cg in cgs:
                w = load_w(Wd, cg)
                pp = PSN()
                for k in range(8):
                    PE(lambda e, pp=pp, w=w, k=k: e.matmul(pp[:n, 0:256], lhsT=lhsT[:, k, :n], rhs=w[:, k, :],
                                                           start=(k == 0), stop=(k == 7)), r=[lhsT, w], w=[pp])
                evac(cg, pp)
                yield "y"

        def rmsnorm_tm(dst, src, n, wbuf_):
            A(lambda e: e.activation(out=tm[NTM - 1][:n, :], in_=src[:n, :], func=AF.Square, accum_out=sm[:n, 0:1]),
              r=[src], w=[tm[NTM - 1], sm])
            A(lambda e: e.activation(out=sm[:n, 1:2], in_=sm[:n, 0:1], func=AF.Sqrt, scale=1.0 / D, bias=EPS), r=[sm], w=[sm])
            V(lambda e: e.reciprocal(out=sm[:n, 2:3], in_=sm[:n, 1:2]), r=[sm], w=[sm])
            V(lambda e: e.scalar_tensor_tensor(out=dst[:n, :], in0=src[:n, :], scalar=sm[:n, 2:3], in1=wbuf_[:n, :],
                                               op0=ALU.mult, op1=ALU.mult), r=[src, sm, wbuf_], w=[dst])

        PO = PY = ps2[0]
        PA = ps2[1]
        PX = SB("PX", [128, D]); PXN = SB("PXN", [128, D]); PS1 = SB("PS1", [128, D]); POH = SB("POH", [128, D])
        PG = [SB("PG%d" % i, [128, D]) for i in range(6)]
        PGU = [Buf(None, "PGU%d" % i) for i in range(6)]
        PGV = [Buf(None, "PGV%d" % i) for i in range(6)]
        pfb = SB("pfb", [128, 8, 128], BF16)
        tv = SB("tv", [128, 256]); ti = SB("ti", [128, 256], U32); tif = tv
        tv2 = SB("tv2", [128, 128]); ti2 = SB("ti2", [128, 128], U32)
        sc2 = SB("sc2", [128, 5, 128])
        eidf = SB("eidf", [128, 128]); eidi = SB("eidi", [128, 128], I32)
        gate = SB("gate", [128, 128]); dots = SB("dots", [128, 128]); coef = SB("coef", [128, 128])
        wkm = SB("wkm", [128, 256])
        DG = [SB("dg%d" % i, [128, 128], BF16) for i in range(2)]
        V(lambda e: e.memset(eidi[:], 0), w=[eidi])
        V(lambda e: e.memset(Shg[:], 0.0), w=[Shg])
        V(lambda e: e.memset(Scar[:], 0.0), w=[Scar])
        iota16 = cst[:, C_IO:C_IO + 16]

        def layer_gen(kind, idx):
            if kind == "meta":
                n, nseq, xsrc = NMETA, 1, meta
            elif kind == "prompt":
                n, nseq, xsrc = 128, 1, xp[idx * 128:(idx + 1) * 128, :]
            else:
                n, nseq, xsrc = NSAMP * LS, NSAMP, xs
            full = kind != "meta"
            if nseq == 1:
                Matt = cst[:n, C_TRI:C_TRI + n]; Tot = cst[:n, C_ONE:C_ONE + n]; M2 = cst[:n, C_ONE:C_ONE + 1]
            else:
                Matt = cst[:n, C_MS:C_MS + n]; Tot = cst[:n, C_TS:C_TS + n]; M2 = cst[:n, C_M2:C_M2 + 16]
            X, Hn, Q, KK, LG, VV, GS, U, SGA, SGB, BS, KT_, SCR = tm
            ld(X[:n, :], xsrc, X)
            rmsnorm_tm(Hn, X, n, n1w)
            transpose8(fb[0], Hn, n)
            segs = [(Q, AF.Silu), (KK, AF.Sigmoid), (VV, AF.Copy), (GS, AF.Silu), (U, AF.Copy), (SGA, AF.Sigmoid), (SGB, AF.Sigmoid)]

            def evac_in(cg, pp):
                dst, fn_ = segs[cg // 4]
                c0 = (cg % 4) * 256
                A(lambda e: e.activation(out=dst[:n, c0:c0 + 256], in_=pp[:n, 0:256], func=fn_), r=[pp], w=[dst])

            yield from proj_tm(fb[0], n, "w_in", list(range(28)) if full else list(range(12)) + list(range(16, 20)), evac_in)
            yield "A"
            V(lambda e: e.tensor_scalar(out=LG[:n, :], in0=KK[:n, :], scalar1=-1.0, scalar2=1.0, op0=ALU.mult, op1=ALU.add), r=[KK], w=[LG])
            V(lambda e: e.tensor_tensor(out=LG[:n, :], in0=LG[:n, :], in1=lbt[:n, :], op=ALU.mult), r=[LG, lbt], w=[LG])
            V(lambda e: e.tensor_tensor(out=KK[:n, :], in0=KK[:n, :], in1=LG[:n, :], op=ALU.add), r=[KK, LG], w=[KK])
            A(lambda e: e.activation(out=LG[:n, :], in_=KK[:n, :], func=AF.Ln), r=[KK], w=[LG])
            V(lambda e: e.tensor_scalar(out=KK[:n, :], in0=KK[:n, :], scalar1=-1.0, scalar2=1.0, op0=ALU.mult, op1=ALU.add),
              r=[KK], w=[KK])
            pb = PSN()
            for hf in range(2):
                PE(lambda e, hf=hf: e.matmul(pb[:n, hf * 512:(hf + 1) * 512], lhsT=Matt, rhs=LG[:n, hf * 512:(hf + 1) * 512],
                                             start=True, stop=True), r=[cst, LG], w=[pb])
            A(lambda e: e.copy(out=BS[:n, :], in_=pb[:n, :]), r=[pb], w=[BS])
            pl = PSN()
            for hf in range(2):
                PE(lambda e, hf=hf: e.matmul(pl[:n, hf * 512:(hf + 1) * 512], lhsT=Tot, rhs=LG[:n, hf * 512:(hf + 1) * 512],
                                             start=True, stop=True), r=[cst, LG], w=[pl])
            A(lambda e: e.activation(out=KT_[:n, :], in_=BS[:n, :], func=AF.Exp, scale=-1.0), r=[BS], w=[KT_])
            V(lambda e: e.tensor_tensor(out=KT_[:n, :], in0=KT_[:n, :], in1=KK[:n, :], op=ALU.mult), r=[KT_, KK], w=[KT_])
            V(lambda e: e.tensor_tensor(out=SCR[:n, :], in0=pl[:n, :], in1=BS[:n, :], op=ALU.subtract), r=[pl, BS], w=[SCR])
            A(lambda e: e.activation(out=SCR[:n, :], in_=SCR[:n, :], func=AF.Exp), r=[SCR], w=[SCR])
            V(lambda e: e.tensor_tensor(out=KK[:n, :], in0=KK[:n, :], in1=SCR[:n, :], op=ALU.mult), r=[KK, SCR], w=[KK])
            A(lambda e: e.activation(out=BS[:n, :], in_=BS[:n, :], func=AF.Exp), r=[BS], w=[BS])
            V(lambda e: e.tensor_tensor(out=Q[:n, :], in0=Q[:n, :], in1=BS[:n, :], op=ALU.mult), r=[Q, BS], w=[Q])
            pg = PSN()
            for h in range(H):
                PE(lambda e, h=h: e.matmul(pg[:, h * 16:h * 16 + nseq], lhsT=LG[:n, h * 128:(h + 1) * 128], rhs=M2,
                                           start=True, stop=True), r=[LG, cst], w=[pg])
            A(lambda e: e.activation(out=gdec[:, :, 0:nseq], in_=pg[:, 0:128].rearrange("p (h s) -> p h s", s=16)[:, :, 0:nseq],
                                     func=AF.Exp), r=[pg], w=[gdec])
            transpose8(fm[1], Q, n)
            transpose8(fm[2], KT_, n, evac="dve")
            if nseq > 1:
                ld(Hn[:, :], consts[:, C_MQ:C_MQ + 1024], Hn)
            for h in range(H):
                yield "y"
                hs = slice(h * 128, (h + 1) * 128)
                pa = PSN()
                PE(lambda e, h=h, pa=pa: e.matmul(pa[:n, 0:n], lhsT=fm[2][:, h, :n], rhs=fm[1][:, h, :n], start=True, stop=True),
                   r=[fm[1], fm[2]], w=[pa])
                V(lambda e, pa=pa: e.tensor_tensor(out=attm[:n, :n], in0=pa[:n, 0:n], in1=Matt, op=ALU.mult), r=[pa, cst], w=[attm])
                if nseq == 1:
                    if full:
                        PE(lambda e, h=h, hs=hs: e.matmul(PO[:n, hs], lhsT=fm[1][:, h, :n], rhs=Shg[:, h, :], start=True, stop=False),
                           r=[fm[1], Shg], w=[PO])
                        PE(lambda e, hs=hs: e.matmul(PO[:n, hs], lhsT=attm[:n, :n], rhs=VV[:n, hs], start=False, stop=True),
                           r=[attm, VV], w=[PO])
                    pd = PSN()
                    PE(lambda e, hs=hs, pd=pd: e.matmul(pd[:, 0:128], lhsT=KK[:n, hs], rhs=VV[:n, hs], start=True, stop=True),
                       r=[KK, VV], w=[pd])
                    V(lambda e, h=h, pd=pd: e.scalar_tensor_tensor(out=Shg[:, h, :], in0=Shg[:, h, :], scalar=gdec[:, h, 0:1],
                                                                   in1=pd[:, 0:128], op0=ALU.mult, op1=ALU.add),
                      r=[Shg, gdec, pd], w=[Shg])
                else:
                    ld(Ssm[:], st_hg[:, h].rearrange("s k v -> k s v"), Ssm)
                    V(lambda e, h=h: e.tensor_tensor(out=Zq[:], in0=fm[1][:, h, :n].unsqueeze(1).to_broadcast([128, NSAMP, n]),
                                                     in1=Hn[:, :].rearrange("p (s t) -> p s t", t=64), op=ALU.mult),
                      r=[fm[1], Hn], w=[Zq])
                    for s_ in range(NSAMP):
                        PE(lambda e, s_=s_, hs=hs: e.matmul(PO[:n, hs], lhsT=Zq[:, s_, :], rhs=Ssm[:, s_, :], start=(s_ == 0), stop=False),
                           r=[Zq, Ssm], w=[PO])
                    PE(lambda e, hs=hs: e.matmul(PO[:n, hs], lhsT=attm[:n, :n], rhs=VV[:n, hs], start=False, stop=True),
                       r=[attm, VV], w=[PO])
                    for half in range(2):
                        V(lambda e, hs=hs, half=half: e.tensor_tensor(
                            out=Zv[:], in0=VV[:n, hs].unsqueeze(1).to_broadcast([n, 8, 128]),
                            in1=cst[:n, C_M2 + half * 8:C_M2 + half * 8 + 8].unsqueeze(2).to_broadcast([n, 8, 128]), op=ALU.mult),
                          r=[VV, cst], w=[Zv])
                        pd = PSN()
                        for qq in range(2):
                            PE(lambda e, hs=hs, pd=pd, qq=qq: e.matmul(
                                pd[:, qq * 512:(qq + 1) * 512], lhsT=KK[:n, hs],
                                rhs=Zv[:, qq * 4:(qq + 1) * 4, :].rearrange("p s v -> p (s v)"), start=True, stop=True),
                               r=[KK, Zv], w=[pd])
                        for sl in range(8):
                            s_ = half * 8 + sl
                            V(lambda e, h=h, s_=s_, sl=sl, pd=pd: e.scalar_tensor_tensor(
                                out=Ssm[:, s_, :], in0=Ssm[:, s_, :], scalar=gdec[:, h, s_:s_ + 1],
                                in1=pd[:, sl * 128:(sl + 1) * 128], op0=ALU.mult, op1=ALU.add), r=[Ssm, gdec, pd], w=[Ssm])
                    stq(hg_s[:, h].rearrange("s k v -> k s v"), Ssm[:], Ssm)
            if kind == "prompt" and idx == NPT - 1:
                stq(hg_p.rearrange("h k v -> k h v"), Shg[:], Shg)
            if STAGE <= 1 and full:
                return
            if full:
                OG = BS
                A(lambda e: e.activation(out=SCR[:n, :], in_=PO[:n, :], func=AF.Square), r=[PO], w=[SCR])
                V(lambda e: e.tensor_reduce(out=sm[:n, 8:16], in_=SCR[:n, :].rearrange("p (h v) -> p h v", v=128), axis=AX.X, op=ALU.add),
                  r=[SCR], w=[sm])
                A(lambda e: e.activation(out=sm[:n, 16:24], in_=sm[:n, 8:16], func=AF.Sqrt, scale=1.0 / 128, bias=EPS), r=[sm], w=[sm])
                V(lambda e: e.reciprocal(out=sm[:n, 24:32], in_=sm[:n, 16:24]), r=[sm], w=[sm])
                V(lambda e: e.tensor_tensor(out=OG[:n, :].rearrange("p (h v) -> p h v", v=128),
                                            in0=PO[:n, :].rearrange("p (h v) -> p h v", v=128),
                                            in1=sm[:n, 24:32].unsqueeze(2).to_broadcast([n, H, 128]), op=ALU.mult), r=[PO, sm], w=[OG])
                V(lambda e: e.tensor_tensor(out=OG[:n, :], in0=OG[:n, :], in1=gnw8[:n, :], op=ALU.mult), r=[OG, gnw8], w=[OG])
                V(lambda e: e.tensor_tensor(out=OG[:n, :], in0=OG[:n, :], in1=GS[:n, :], op=ALU.mult), r=[OG, GS], w=[OG])
                transpose8(fb[0], OG, n)
                BRA = Hn

                def evac_a(cg, pp):
                    A(lambda e: e.copy(out=BRA[:n, cg * 256:(cg + 1) * 256], in_=pp[:n, 0:256]), r=[pp], w=[BRA])
                yield from proj_tm(fb[0], n, "w_a", [0, 1, 2, 3], evac_a)

            if STAGE <= 2 and full:
                return
            for jr in (range(0, 8), range(8, 11)):
                pp = PSN()
                for j in jr:
                    wd = min(96, 1024 - j * 96)
                    PE(lambda e, pp=pp, j=j, wd=wd: e.transpose(out=pp[:wd, (j % 8) * n:(j % 8 + 1) * n], in_=U[:n, j * 96:j * 96 + wd],
                                                             identity=cst[:n, C_ID:C_ID + n]), r=[U, cst], w=[pp])
                nbk_ = len(jr)
                j0_ = jr[0]
                A(lambda e, pp=pp, nbk_=nbk_, j0_=j0_: e.copy(out=ub16[:96, j0_:j0_ + nbk_, :n],
                                                             in_=pp[:96, 0:nbk_ * n].rearrange("p (j t) -> p j t", t=n)), r=[pp], w=[ub16])
            T1, T2 = LG, VV
            DBG = os.environ.get('KDBG', '')
            if DBG == 'a' and full:
                return
            if nseq > 1:
                AIN, BIN = Q, KK
                S0s = [KT_, SCR]
                a3 = AIN[:].rearrange("p (j q) -> p j q", q=128)
                b3 = BIN[:].rearrange("p (j q) -> p j q", q=128)
                ld(a3[:, :, 0:64], st_re.rearrange("(j r) p -> r j p", r=128), AIN)
                ld(a3[:, :, 64:128], st_im.rearrange("(j r) p -> r j p", r=128), AIN)
                V(lambda e: e.tensor_scalar_mul(out=b3[:, :, 0:64], in0=a3[:, :, 64:128], scalar1=-1.0), r=[AIN], w=[BIN])
                V(lambda e: e.tensor_copy(out=b3[:, :, 64:128], in_=a3[:, :, 0:64]), r=[AIN], w=[BIN])
                for jj in range(8):
                    S0 = S0s[jj // 4]
                    for v, src in enumerate((AIN, BIN)):
                        pp = PSN()
                        PE(lambda e, pp=pp, src=src, jj=jj: e.transpose(out=pp[:, 0:128], in_=src[:, jj * 128:(jj + 1) * 128],
                                                                        identity=ident), r=[src, cst], w=[pp])
                        A(lambda e, pp=pp, jj=jj, v=v, S0=S0: e.copy(
                            out=S0[:].rearrange("p (s v g) -> p s v g", v=2, g=G)[:, 2 * (jj % 4):2 * (jj % 4) + 2, v, :],
                            in_=pp[:, 0:128].rearrange("p (s g) -> p s g", g=G)), r=[pp], w=[S0])
            SCB = int(os.environ.get('KNB', SBLK)) if nseq == 1 else SBLK
            yield "B"
            nblk = (n + SCB - 1) // SCB
            for bi in range(nblk):
                b0 = bi * SCB
                nb = min(SCB, n - b0)
                for q8 in range(8):
                    pp = PSN()
                    last_rr = -1
                    for gi in sorted(range(8), key=lambda gi_: ((q8 * 8 + gi_) // 2) % 3):
                        g = q8 * 8 + gi
                        pq_ = g // 2
                        gl = g % 2
                        jb, rr = divmod(pq_, 3)
                        srcf = ub16
                        if rr != last_rr or os.environ.get('KDRAIN', ''):
                            P.pe_drain()
                            last_rr = rr
                        for v in range(2):
                            slot = v * 8 + gi
                            PE(lambda e, pp=pp, slot=slot, rr=rr, jb=jb, gl=gl, v=v, srcf=srcf: e.matmul(
                                pp[:, slot * nb:(slot + 1) * nb], lhsT=Bmat[32 * rr:32 * rr + 32, jb, gl, v, :],
                                rhs=srcf[32 * rr:32 * rr + 32, jb, b0:b0 + nb], start=True, stop=True), r=[Bmat, srcf], w=[pp])
                    A(lambda e, pp=pp, q8=q8: e.copy(
                        out=SHb[:, 0:nb, :, q8 * 8:(q8 + 1) * 8].rearrange("p t v g -> p v g t"),
                        in_=pp[:, 0:16 * nb].rearrange("p (v g t) -> p v g t", v=2, g=8)), r=[pp], w=[SHb])
                    yield "q"
                if nseq == 1:
                    steps = [(None if t == 0 else SHb[:, t - 1:t, :, :], SHb[:, t:t + 1, :, :], 1, Scar[:, 0:1, :, :], Scar)
                             for t in range(nb)]
                else:
                    shv = SHb[:, 0:nb, :, :].rearrange("p (s l) v g -> p s l v g", l=LS)
                    steps = []
                    for sbt in range(2):
                        S0 = S0s[sbt]
                        for l in range(LS):
                            steps.append((None if l == 0 else shv[:, 8 * sbt:8 * sbt + 8, l - 1, :, :], shv[:, 8 * sbt:8 * sbt + 8, l, :, :], 8,
                                          S0[:].rearrange("p (s v g) -> p s v g", v=2, g=G), S0))
                if DBG == 'b' and full:
                    steps = []
                if DBG == 'c' and full:
                    steps = steps[:2]
                for (prev, cur, nbat, prev0, prev0_buf) in steps:
                    pbuf = SHb
                    if prev is None:
                        prev, pbuf = prev0, prev0_buf
                    if nbat == 1:
                        t1v = sT1[:, 0:1, :, :]; t2v = sT2[:, 0:1, :, :]
                        rT1, rT2a, rT2b = sT1, sT2a, sT2b
                    else:
                        t1v = T1[:, 0:nbat * 128].rearrange("p (s v g) -> p s v g", v=2, g=G)
                        t2v = T2[:, 0:nbat * 128].rearrange("p (s v g) -> p s v g", v=2, g=G)
                        rT1, rT2a, rT2b = T1, T2, T2
                    V(lambda e, prev=prev, t1v=t1v, nbat=nbat: e.tensor_tensor(
                        out=t1v, in0=prev, in1=ArAr[:].unsqueeze(1).to_broadcast([128, nbat, 2, G]), op=ALU.mult),
                      r=[pbuf, ArAr], w=[rT1])
                    V(lambda e, prev=prev, t2v=t2v, nbat=nbat: e.tensor_tensor(
                        out=t2v, in0=prev[:, :, ::-1, :], in1=AiPN[:].unsqueeze(1).to_broadcast([128, nbat, 2, G]), op=ALU.mult),
                      r=[pbuf, AiPN], w=[rT2a, rT2b])
                    V(lambda e, cur=cur, t1v=t1v: e.tensor_tensor(out=cur, in0=cur, in1=t1v, op=ALU.add), r=[SHb, rT1], w=[SHb])
                    yield "S1"
                    V(lambda e, cur=cur, t2v=t2v: e.tensor_tensor(out=cur, in0=cur, in1=t2v, op=ALU.add), r=[SHb, rT2a, rT2b], w=[SHb])
                    yield "S"
                if nseq == 1:
                    V(lambda e, nb=nb: e.tensor_copy(out=Scar[:, 0:1, :, :], in_=SHb[:, nb - 1:nb, :, :]), r=[SHb], w=[Scar])
                    if os.environ.get('KDUMP', '') == 'blk' and kind == 'prompt':
                        if bi == 0:
                            dU = dout("dbgU", [128, 1024]); stq(dU, U[:, :], U)
                        dbgb = dout("dbg%d" % bi, [128, 64 * 128])
                        stq(dbgb, SHb[:].rearrange("p t v g -> p (t v g)"), SHb)
                else:
                    FS = GS
                    FSX = Q
                    V(lambda e: e.tensor_copy(out=FSX[:, :].rearrange("p (s g) -> p s g", g=G),
                                              in_=SHb[:, 0:nb, 0, :].rearrange("p (s l) g -> p s l g", l=LS)[:, :, LS - 1, :]),
                      r=[SHb], w=[FSX])
                    for pr in range(8):
                        pp = PSN()
                        PE(lambda e, pp=pp, pr=pr: e.transpose(out=pp[:, 0:128], in_=FSX[:, pr * 128:(pr + 1) * 128],
                                                               identity=ident), r=[FSX, cst], w=[pp])
                        A(lambda e, pp=pp, pr=pr: e.copy(out=FS[:, pr * 128:(pr + 1) * 128], in_=pp[:, 0:128]), r=[pp], w=[FS])
                    fs3 = FS[:, :].rearrange("p (j q) -> p j q", q=128)
                    stq(re_s.rearrange("(j r) p -> r j p", r=128), fs3[:, :, 0:64], FS)
                    stq(im_s.rearrange("(j r) p -> r j p", r=128), fs3[:, :, 64:128], FS)
                if full and STAGE != 3:
                    for g in range(G):
                        PE(lambda e, g=g, b0=b0, nb=nb: e.matmul(PY[b0:b0 + nb, g * 16:(g + 1) * 16], lhsT=SHb[:, 0:nb, 0, g],
                                                                 rhs=Cmat[:, g, :], start=True, stop=True), r=[SHb, Cmat], w=[PY])
            if kind == "prompt" and idx == NPT - 1:
                pp = PSN()
                PE(lambda e, pp=pp: e.transpose(out=pp[:G, 0:128], in_=Scar[:, 0, 0, :], identity=ident), r=[Scar, cst], w=[pp])
                A(lambda e, pp=pp: e.copy(out=SCR[:G, 0:128], in_=pp[:G, 0:128]), r=[pp], w=[SCR])
                stq(re_p, SCR[:G, 0:64], SCR)
                stq(im_p, SCR[:G, 64:128], SCR)
            if not full:
                return
            if STAGE <= 3:
                return
            Y = BS
            V(lambda e: e.tensor_tensor(out=Y[:n, :], in0=U[:n, :], in1=dvec[:n, :], op=ALU.mult), r=[U, dvec], w=[Y])
            V(lambda e: e.tensor_tensor(out=Y[:n, :], in0=Y[:n, :], in1=PY[:n, :], op=ALU.add), r=[Y, PY], w=[Y])
            A(lambda e: e.activation(out=Y[:n, :], in_=Y[:n, :], func=AF.Gelu), r=[Y], w=[Y])
            transpose8(fb[1], Y, n)
            for cg in range(4):
                yield "y"
                w = load_w("w_glu", cg)
                for m in range(2):
                    pz = PSN()
                    for k in range(8):
                        PE(lambda e, pz=pz, w=w, k=k, m=m: e.matmul(pz[:, 0:n], lhsT=w[:, k, m * 128:(m + 1) * 128], rhs=fb[1][:, k, :n],
                                                                    start=(k == 0), stop=(k == 7)), r=[w, fb[1]], w=[pz])
                    A(lambda e, pz=pz, cg=cg, m=m: e.activation(out=fb[2][:, cg * 2 + m, :n], in_=pz[:, 0:n], func=AF.Sigmoid,
                                                                bias=bglu[:, cg * 2 + m:cg * 2 + m + 1]), r=[pz, bglu], w=[fb[2]])
            V(lambda e: e.tensor_tensor(out=fb[2][:, :, :n], in0=fb[2][:, :, :n], in1=fb[1][:, :, :n], op=ALU.mult), r=[fb[2], fb[1]], w=[fb[2]])
            MIX = KT_

            def evac_b(cg, pp):
                V(lambda e: e.tensor_tensor(out=MIX[:n, cg * 256:(cg + 1) * 256], in0=pp[:n, 0:256], in1=SGB[:n, cg * 256:(cg + 1) * 256],
                                            op=ALU.mult), r=[pp, SGB], w=[MIX])
            yield from proj_tm(fb[2], n, "w_b", [0, 1, 2, 3], evac_b)
            V(lambda e: e.tensor_tensor(out=BRA[:n, :], in0=BRA[:n, :], in1=SGA[:n, :], op=ALU.mult), r=[BRA, SGA], w=[BRA])
            V(lambda e: e.tensor_tensor(out=MIX[:n, :], in0=MIX[:n, :], in1=BRA[:n, :], op=ALU.add), r=[MIX, BRA], w=[MIX])
            transpose8(fb[0], MIX, n)

            def evac_o(cg, pp):
                V(lambda e: e.tensor_tensor(out=X[:n, cg * 256:(cg + 1) * 256], in0=X[:n, cg * 256:(cg + 1) * 256], in1=pp[:n, 0:256],
                                            op=ALU.add), r=[pp, X], w=[X])
            yield from proj_tm(fb[0], n, "w_out", [0, 1, 2, 3], evac_o)

            if STAGE <= 4:
                return
            yield "C"
            V(lambda e: e.tensor_copy(out=PX[:n, :], in_=X[:n, :]), r=[X], w=[PX])
            return

        def peer_gen(kind, idx, n):
            X = PX; XN = PXN; Hn = PS1; LG = PS1; VV = POH; KT_ = POH; SCR = POH
            rmsnorm_tm(XN, X, n, n2w)
            transpose8(pfb, XN, n)
            for cg in range(8):
                w = load_w("wq", cg)
                for m in range(2):
                    un = cg * 2 + m
                    pq = PSN()
                    for k in range(8):
                        PE(lambda e, pq=pq, w=w, k=k, m=m: e.matmul(pq[:, 0:n], lhsT=w[:, k, m * 128:(m + 1) * 128], rhs=pfb[:, k, :n],
                                                                    start=(k == 0), stop=(k == 7)), r=[w, pfb], w=[pq])
                    dstq = qT0 if un < 8 else qT1
                    A(lambda e, pq=pq, dstq=dstq, un=un: e.copy(out=dstq[:, un % 8, :n], in_=pq[:, 0:n]), r=[pq], w=[dstq])
            S1, S2 = LG, VV
            for half, (Sd, KTb) in enumerate(((S1, KT1), (S2, KT2))):
                for hh in range(2):
                    pp = PSN()
                    for h4 in range(4):
                        h = hh * 4 + h4
                        un = h * 2 + half
                        srcq = qT0 if un < 8 else qT1
                        PE(lambda e, pp=pp, h4=h4, h=h, srcq=srcq, un=un, KTb=KTb: e.matmul(
                            pp[:n, h4 * 128:(h4 + 1) * 128], lhsT=srcq[:, un % 8, :n], rhs=KTb[:, h, :], start=True, stop=True),
                           r=[srcq, KTb], w=[pp])
                    A(lambda e, pp=pp, hh=hh, Sd=Sd: e.copy(out=Sd[:n, hh * 512:(hh + 1) * 512], in_=pp[:n, 0:512]), r=[pp], w=[Sd])

            yield "a"

            def top16(src_ap, srcbuf, width, vals, valbuf, idxs, idxbuf):
                V(lambda e: e.max(out=vals[:, 0:8], in_=src_ap), r=[srcbuf], w=[valbuf])
                V(lambda e: e.match_replace(out=wkm[:n, 0:width], in_to_replace=vals[:, 0:8], in_values=src_ap, imm_value=-1e30),
                  r=[srcbuf, valbuf], w=[wkm])
                V(lambda e: e.max(out=vals[:, 8:16], in_=wkm[:n, 0:width]), r=[wkm], w=[valbuf])
                V(lambda e: e.max_index(out=idxs[:, 0:8], in_max=vals[:, 0:8], in_values=src_ap), r=[srcbuf, valbuf], w=[idxbuf])
                V(lambda e: e.max_index(out=idxs[:, 8:16], in_max=vals[:, 8:16], in_values=src_ap), r=[srcbuf, valbuf], w=[idxbuf])

            for h in range(H):
                for half, Sd in enumerate((S1, S2)):
                    un = h * 2 + half
                    top16(Sd[:n, h * 128:(h + 1) * 128], Sd, 128, tv[:n, un * 16:(un + 1) * 16], tv, ti[:n, un * 16:(un + 1) * 16], ti)
            CAND = KT_
            tv4 = tv[:n, :].rearrange("p (h two i) -> p h two i", two=2, i=16)
            for hh in range(2):
                V(lambda e, hh=hh: e.tensor_tensor(
                    out=CAND[:n, :].rearrange("p (h i j) -> p h i j", i=16, j=16),
                    in0=tv4[:, hh * 4:(hh + 1) * 4, 0, :].unsqueeze(3).to_broadcast([n, 4, 16, 16]),
                    in1=tv4[:, hh * 4:(hh + 1) * 4, 1, :].unsqueeze(2).to_broadcast([n, 4, 16, 16]), op=ALU.add), r=[tv], w=[CAND])
                for h4 in range(4):
                    h = hh * 4 + h4
                    top16(CAND[:n, h4 * 256:(h4 + 1) * 256], CAND, 256, tv2[:n, h * 16:(h + 1) * 16], tv2, ti2[:n, h * 16:(h + 1) * 16], ti2)
            V(lambda e: e.tensor_copy(out=tif[:n, :], in_=ti[:n, :]), r=[ti, tv], w=[tif])
            posf, i_f, j_f, e1f, e2f = (sc2[:n, i_, :] for i_ in range(5))
            V(lambda e: e.tensor_copy(out=posf, in_=ti2[:n, :]), r=[ti2], w=[sc2])
            OH = SCR
            oh3 = OH[:n, 0:2048 // 2].rearrange("p (a b) -> p a b", b=16) if False else None
            for hf in range(2):
                cs_ = slice(hf * 64, (hf + 1) * 64)
                o3 = OH[:n, :].rearrange("p (a b) -> p a b", b=16)
                V(lambda e, cs_=cs_, o3=o3: e.scalar_tensor_tensor(
                    out=o3, in0=posf[:, cs_].unsqueeze(2).to_broadcast([n, 64, 16]), scalar=1.0 / 16,
                    in1=iota16[:n, :].unsqueeze(1).to_broadcast([n, 64, 16]), op0=ALU.mult, op1=ALU.is_ge), r=[sc2, cst], w=[OH])
                V(lambda e, cs_=cs_, o3=o3: e.tensor_reduce(out=i_f[:, cs_], in_=o3, axis=AX.X, op=ALU.add), r=[OH], w=[sc2])
            V(lambda e: e.tensor_scalar_add(out=i_f, in0=i_f, scalar1=-1.0), r=[sc2], w=[sc2])
            V(lambda e: e.scalar_tensor_tensor(out=j_f, in0=i_f, scalar=-16.0, in1=posf, op0=ALU.mult, op1=ALU.add), r=[sc2], w=[sc2])
            tif4 = tif[:n, :].rearrange("p (h two i) -> p h two i", two=2, i=16)
            for which, (sel, dst) in enumerate(((i_f, e1f), (j_f, e2f))):
                for hf in range(2):
                    cs_ = slice(hf * 64, (hf + 1) * 64)
                    o3 = OH[:n, :].rearrange("p (a b) -> p a b", b=16)
                    V(lambda e, sel=sel, cs_=cs_, o3=o3: e.tensor_tensor(
                        out=o3, in0=sel[:, cs_].unsqueeze(2).to_broadcast([n, 64, 16]),
                        in1=iota16[:n, :].unsqueeze(1).to_broadcast([n, 64, 16]), op=ALU.is_equal), r=[sc2, cst], w=[OH])
                    o4 = OH[:n, :].rearrange("p (h k i) -> p h k i", k=16, i=16)
                    V(lambda e, o4=o4, hf=hf, which=which: e.tensor_tensor(
                        out=o4, in0=o4, in1=tif4[:, hf * 4:(hf + 1) * 4, which, :].unsqueeze(2).to_broadcast([n, 4, 16, 16]),
                        op=ALU.mult), r=[OH, tif], w=[OH])
                    V(lambda e, dst=dst, cs_=cs_, o3=o3: e.tensor_reduce(out=dst[:, cs_], in_=o3, axis=AX.X, op=ALU.add), r=[OH], w=[sc2])
            V(lambda e: e.scalar_tensor_tensor(out=eidf[:n, :], in0=e1f, scalar=128.0, in1=e2f, op0=ALU.mult, op1=ALU.add), r=[sc2], w=[eidf])
            V(lambda e: e.tensor_copy(out=eidi[:n, :], in_=eidf[:n, :]), r=[eidf], w=[eidi])
            tv23 = tv2[:n, :].rearrange("p (h k) -> p h k", k=16)
            g3_ = gate[:n, :].rearrange("p (h k) -> p h k", k=16)
            V(lambda e: e.tensor_tensor(out=g3_, in0=tv23, in1=tv23[:, :, 0:1].to_broadcast([n, H, 16]), op=ALU.subtract), r=[tv2], w=[gate])
            A(lambda e: e.activation(out=gate[:n, :], in_=gate[:n, :], func=AF.Exp), r=[gate], w=[gate])
            V(lambda e: e.tensor_reduce(out=sm[:n, 32:40], in_=g3_, axis=AX.X, op=ALU.add), r=[gate], w=[sm])
            V(lambda e: e.reciprocal(out=sm[:n, 40:48], in_=sm[:n, 32:40]), r=[sm], w=[sm])
            V(lambda e: e.tensor_tensor(out=g3_, in0=g3_, in1=sm[:n, 40:48].unsqueeze(2).to_broadcast([n, H, 16]), op=ALU.mult), r=[gate, sm], w=[gate])
            yield "b"
            extra0 = [] if uvb_waited[0] else list(uvb_res)
            uvb_waited[0] = True
            NK = H * TOPK

            def u_stream():
                extra = extra0
                for grp in range(NK // 2):
                    bufs = []
                    for kk_ in range(2):
                        hk = grp * 2 + kk_
                        gb = PG[hk % 6]
                        gv = gb[:].bitcast(BF16)[:, 0:1024]
                        DMA(lambda e, gv=gv, hk=hk: e.indirect_dma_start(out=gv, out_offset=None, in_=ubt,
                                                                        in_offset=bass.IndirectOffsetOnAxis(ap=eidi[:, hk:hk + 1], axis=0)),
                            r=[eidi] + extra, w=[PGU[hk % 6]], q="pool")
                        extra = []
                        bufs.append((gv, hk))
                    for gv, hk in bufs:
                        V(lambda e, gv=gv, hk=hk: e.scalar_tensor_tensor(out=SCR[:].bitcast(BF16)[:n, 0:1024], in0=gv[:n, :], scalar=1.0, in1=XN[:n, :],
                                                                         op0=ALU.mult, op1=ALU.mult, accum_out=dots[:n, hk:hk + 1]),
                          r=[PGU[hk % 6], XN], w=[SCR, dots])
                        yield "d"
                    gsl = slice(grp * 2, grp * 2 + 2)
                    A(lambda e, gsl=gsl: e.activation(out=coef[:n, gsl], in_=dots[:n, gsl], func=AF.Gelu), r=[dots], w=[coef])
                    V(lambda e, gsl=gsl: e.tensor_tensor(out=coef[:n, gsl], in0=coef[:n, gsl], in1=gate[:n, gsl], op=ALU.mult), r=[coef, gate], w=[coef])
                    yield "k"

            def v_stream():
                for hk in range(NK):
                    gb = PG[hk % 6]
                    gv = gb[:].bitcast(BF16)[:, 1024:2048]
                    DMA(lambda e, gv=gv, hk=hk: e.indirect_dma_start(out=gv, out_offset=None, in_=vbt,
                                                                    in_offset=bass.IndirectOffsetOnAxis(ap=eidi[:, hk:hk + 1], axis=0)),
                        r=[eidi], w=[PGV[hk % 6]], q="pool")
                    dg = DG[hk % 2]
                    A(lambda e, dg=dg, hk=hk: e.activation(out=dg[:n, :n], in_=cst[:n, C_ID:C_ID + n], func=AF.Copy, scale=coef[:n, hk:hk + 1]),
                      r=[cst, coef], w=[dg])
                    for hf in range(2):
                        PE(lambda e, dg=dg, gv=gv, hk=hk, hf=hf: e.matmul(PA[:n, hf * 512:(hf + 1) * 512], lhsT=dg[:n, :n],
                                                                          rhs=gv[:n, hf * 512:(hf + 1) * 512],
                                                                          start=(hk == 0), stop=(hk == NK - 1)), r=[dg, PGV[hk % 6]], w=[PA])
                    yield "v"

            US = u_stream(); VS = v_stream()
            u_done = 0
            v_done = 0
            u_alive = True
            while u_alive or v_done < NK:
                want = yield "pk"
                if want == "v" and v_done < u_done:
                    next(VS); v_done += 1
                elif u_alive:
                    m_ = next(US, None)
                    if m_ is None:
                        u_alive = False
                    elif m_ == "k":
                        u_done += 2
                elif v_done < NK:
                    next(VS); v_done += 1
            yield "kend"
            ACC = PA
            V(lambda e: e.tensor_tensor(out=X[:n, :], in0=X[:n, :], in1=ACC[:n, :], op=ALU.add), r=[X, ACC], w=[X])
            YO = Hn
            rmsnorm_tm(YO, X, n, fnw)
            if kind == "prompt":
                stq(y_p[idx * 128:(idx + 1) * 128, :], YO[:n, :], YO)
            else:
                stq(y_s, YO[:n, :], YO)

        DUMPS = os.environ.get('KDUMP', '')
        tiles = [("meta", 0)] + [("prompt", i) for i in range(NPT)] + ([("sample", 0)] if DO_SAMPLE else [])
        if STAGE == 0:
            tiles = []
        def run_to(gen, marks):
            if gen is None:
                return None
            for m_ in gen:
                if m_ in marks:
                    return m_
            return None

        def p_step(gen, want):
            try:
                m_ = gen.send(want)
            except StopIteration:
                return False
            return m_ == "pk"

        QQ = int(os.environ.get('KQQ', '4'))
        QY = int(os.environ.get('KQY', '6'))
        pend = None
        UG = uvb_gen()
        ug_alive = [True]

        def ug_step(k_):
            for _i in range(k_):
                if ug_alive[0] and next(UG, None) is None:
                    ug_alive[0] = False

        for kind, idx in tiles:
            L = layer_gen(kind, idx)
            nn = {"meta": NMETA, "prompt": 128, "sample": NSAMP * LS}[kind]
            p_loop = False
            if pend is not None:
                run_to(pend, ("a",))
            lm = run_to(L, ("A",))
            if pend is not None:
                p_loop = run_to(pend, ("pk",)) == "pk"
            while lm is not None:
                lm = run_to(L, ("y", "q", "A", "B", "S1", "S", "C"))
                if lm in ("y", "S"):
                    ug_step(1)
                if lm is None or lm == "C":
                    break
                if p_loop and lm != "S1":
                    if lm == "S":
                        p_loop = p_step(pend, "v")
                    else:
                        for _rep in range(QQ if lm == "q" else QY):
                            if p_loop:
                                p_loop = p_step(pend, "u")
            if pend is not None:
                while p_loop:
                    p_loop = p_step(pend, "v")
                run_to(pend, ("__end__",))
            pend = None
            if lm == "C":
                run_to(L, ("__end__",))
                if DO_PEER:
                    ug_step(128)
                    pend = peer_gen(kind, idx, nn)
        if pend is not None:
            run_to(pend, ("a",))
            p_loop = run_to(pend, ("pk",)) == "pk"
            tog = 0
            while p_loop:
                p_loop = p_step(pend, "v" if tog % 3 == 2 else "u")
                tog += 1
            run_to(pend, ("__end__",))
        P.finish()
        P.emit(block)
    return nc


_CACHE = {}


def kernel(**inp):
    f32 = lambda a: np.ascontiguousarray(np.asarray(a, dtype=np.float32))
    inp = {k: f32(v) for k, v in inp.items()}
    if "nc" not in _CACHE:
        _CACHE["nc"] = build_program()
    nc = _CACHE["nc"]
    consts = make_consts()
    shared = {k: inp[k] for k in ("lower_bounds", "norm1_w", "w_in", "g_norm_w", "ssm_a_re", "ssm_a_im", "ssm_log_step",
                                  "ssm_b_re", "ssm_b_im", "ssm_c_re", "ssm_c_im", "ssm_d", "w_glu", "b_glu", "w_branch_a",
                                  "w_branch_b", "w_out", "norm2_w", "peer_wq", "peer_k1", "peer_k2", "peer_u", "peer_v",
                                  "final_norm_w")}
    in_maps = []
    for c in range(NCORES):
        m = dict(shared)
        m["consts"] = consts
        m["xp"] = inp["x_prompt"][c]
        m["xs"] = f32(inp["x_sample"][c * NSAMP:(c + 1) * NSAMP].reshape(NSAMP * LS, D))
        m["meta"] = inp["meta_tokens"]
        m["st_hg"] = f32(inp["state_hgrn"][0, c * NSAMP:(c + 1) * NSAMP])
        m["st_re"] = f32(inp["state_ssm_re"][0, c * NSAMP:(c + 1) * NSAMP].reshape(NSAMP * G, PS))
        m["st_im"] = f32(inp["state_ssm_im"][0, c * NSAMP:(c + 1) * NSAMP].reshape(NSAMP * G, PS))
        in_maps.append(m)
    res = run_bass_kernel_spmd(nc, in_maps, core_ids=list(range(NCORES)))
    rs = res.results
    y_prompt = np.stack([rs[c]["y_p"] for c in range(NCORES)]).astype(np.float32)
    y_sample = np.concatenate([rs[c]["y_s"].reshape(NSAMP, LS, D) for c in range(NCORES)]).astype(np.float32)
    hg_p = np.stack([rs[c]["hg_p"] for c in range(NCORES)])[None].astype(np.float32)
    re_p = np.stack([rs[c]["re_p"] for c in range(NCORES)])[None].astype(np.float32)
    im_p = np.stack([rs[c]["im_p"] for c in range(NCORES)])[None].astype(np.float32)
    hg_s = np.concatenate([rs[c]["hg_s"] for c in range(NCORES)])[None].astype(np.float32)
    re_s = np.concatenate([rs[c]["re_s"].reshape(NSAMP, G, PS) for c in range(NCORES)])[None].astype(np.float32)
    im_s = np.concatenate([rs[c]["im_s"].reshape(NSAMP, G, PS) for c in range(NCORES)])[None].astype(np.float32)
    return (y_prompt, y_sample, hg_p, re_p, im_p, hg_s, re_s, im_s)
```

```python
import os
import numpy as np
import concourse.bass as bass
import concourse.mybir as mybir
from concourse.bass_utils import run_bass_kernel_spmd

F32 = mybir.dt.float32
F32R = mybir.dt.float32r
I32 = mybir.dt.int32
U32 = mybir.dt.uint32
BF16 = mybir.dt.bfloat16
ALU = mybir.AluOpType
AF = mybir.ActivationFunctionType
AX = mybir.AxisListType


class Res:
    __slots__ = ("name", "w", "r")

    def __init__(self, name):
        self.name = name
        self.w = None
        self.r = {}


import types


def _freeze(fn):
    if fn.__closure__ is None:
        return fn
    cells = tuple(types.CellType(c.cell_contents) for c in fn.__closure__)
    return types.FunctionType(fn.__code__, fn.__globals__, fn.__name__, fn.__defaults__, cells)


class Prog:
    CE = ("pe", "act", "dve", "pool")

    def __init__(self, nc, n_dma_sems=12):
        self.nc = nc
        self.q = {e: [] for e in ("pe", "act", "dve", "pool", "sp")}
        self.cnt = {e: 0 for e in self.CE}
        self.waited = {e: {} for e in self.q}
        self.sems = {}
        self.dma_pool = {}
        self.dma_next = {}
        self.dma_tgt = {}
        self.n_dma_sems = n_dma_sems
        self.out_events = []
        self.n_instr = 0

    def alloc_sems(self, stack):
        nc = self.nc
        for e in self.CE:
            self.sems[("e", e)] = stack.enter_context(nc.semaphore("sem_" + e))
        for qn in ("sp", "pool", "act"):
            self.dma_pool[qn] = []
            for i in range(self.n_dma_sems):
                k = ("d", qn, i)
                self.sems[k] = stack.enter_context(nc.semaphore("dsem_%s_%d" % (qn, i)))
                self.dma_pool[qn].append(k)
                self.dma_tgt[k] = 0
            self.dma_next[qn] = 0

    def _need(self, eng, ev):
        if ev is None:
            return
        key, val = ev
        if key == ("e", "pe") and eng == "pe":
            return
        if self.waited[eng].get(key, 0) >= val:
            return
        self.waited[eng][key] = val
        sem = self.sems[key]
        self.q[eng].append(lambda e, s=sem, v=val: e.wait_ge(s, v))

    def _deps(self, eng, r, w):
        for x in r:
            self._need(eng, x.w)
        for x in w:
            self._need(eng, x.w)
            for k_, v_ in x.r.items():
                self._need(eng, (k_, v_))

    def _commit(self, ev, r, w):
        for x in w:
            x.w = ev
            x.r = {}
        for x in r:
            if x not in w:
                if x.r.get(ev[0], 0) < ev[1]:
                    x.r[ev[0]] = ev[1]

    def op(self, eng, fn, r=(), w=()):
        fn = _freeze(fn)
        self._deps(eng, r, w)
        self.cnt[eng] += 1
        n = self.cnt[eng]
        key = ("e", eng)
        sem = self.sems[key]
        self.q[eng].append(lambda e, f=fn, s=sem: f(e).then_inc(s, 1))
        self.waited[eng][key] = max(self.waited[eng].get(key, 0), 0)
        self._commit((key, n), r, w)
        self.n_instr += 1

    def dma(self, qn, fn, r=(), w=(), is_output=False):
        fn = _freeze(fn)
        self._deps(qn, r, w)
        i = self.dma_next[qn]
        self.dma_next[qn] = (i + 1) % self.n_dma_sems
        k = self.dma_pool[qn][i]
        if self.dma_tgt[k] > 0:
            self._need(qn, (k, self.dma_tgt[k]))
        self.dma_tgt[k] += 16
        tgt = self.dma_tgt[k]
        sem = self.sems[k]
        self.q[qn].append(lambda e, f=fn, s=sem: f(e).then_inc(s, 16))
        ev = (k, tgt)
        self._commit(ev, r, w)
        if is_output:
            self.out_events.append(ev)
        self.n_instr += 1

    def pe_drain(self):
        n = self.cnt["pe"]
        if n:
            sem = self.sems[("e", "pe")]
            self.q["pe"].append(lambda e, s=sem, v=n: e.wait_ge(s, v))

    def finish(self):
        for ev in self.out_events:
            self._need("sp", ev)
        for e in self.CE:
            if self.cnt[e]:
                self._need("sp", (("e", e), self.cnt[e]))

    def emit(self, block):
        q = self.q

        @block.sync
        def _(eng):
            for f in q["sp"]:
                f(eng)

        @block.tensor
        def _(eng):
            for f in q["pe"]:
                f(eng)

        @block.scalar
        def _(eng):
            for f in q["act"]:
                f(eng)

        @block.vector
        def _(eng):
            for f in q["dve"]:
                f(eng)

        @block.gpsimd
        def _(eng):
            for f in q["pool"]:
                f(eng)


D = 1024
NCORES = 8
SEQ = 2048
NMETA = 16
NSAMP = 16
LS = 4
H = 8
G = 64
PS = 64
TOPK = 16
EPS = 1e-6
TWO_PI = 2.0 * np.pi

C_ID, C_TRI, C_ONE, C_MS, C_TS, C_M2, C_IO, C_MQ, C_END = 0, 128, 256, 384, 448, 512, 528, 544, 544 + 1024


def make_consts():
    c = np.zeros((128, C_END), np.float32)
    c[:, C_ID:C_ID + 128] = np.eye(128)
    c[:, C_TRI:C_TRI + 128] = np.triu(np.ones((128, 128)))
    c[:, C_ONE:C_ONE + 128] = 1.0
    seq = np.arange(64) // LS
    same = (seq[:, None] == seq[None, :])
    c[:64, C_MS:C_MS + 64] = same & (np.arange(64)[:, None] <= np.arange(64)[None, :])
    c[:64, C_TS:C_TS + 64] = same
    c[:64, C_M2:C_M2 + 16] = (seq[:, None] == np.arange(16)[None, :])
    c[:, C_IO:C_IO + 16] = np.arange(16)[None, :]
    mq = (np.arange(16)[:, None] == seq[None, :]).astype(np.float32).reshape(1, 1024)
    c[:, C_MQ:C_MQ + 1024] = mq
    return c


class Buf:
    def __init__(self, t, name):
        self.t = t
        self.r = Res(name)

    def __getitem__(self, k):
        return self.t[k]


def build_program(NPT=16, DO_SAMPLE=True, DO_PEER=True, STAGE=9):
    from contextlib import ExitStack
    nc = bass.Bass("TRN2", target_bir_lowering=False)

    def din(name, shape, dt=F32):
        return nc.dram_tensor(name, list(shape), dt, kind="ExternalInput").ap()

    def dout(name, shape, dt=F32):
        return nc.dram_tensor(name, list(shape), dt, kind="ExternalOutput").ap()

    xp = din("xp", [SEQ, D]); xs = din("xs", [NSAMP * LS, D]); meta = din("meta", [NMETA, D])
    st_hg = din("st_hg", [NSAMP, H, 128, 128]); st_re = din("st_re", [NSAMP * G, PS]); st_im = din("st_im", [NSAMP * G, PS])
    consts = din("consts", [128, C_END])
    lower_bounds = din("lower_bounds", [2, D]); norm1_w = din("norm1_w", [1, D]); w_in = din("w_in", [1, D, 7168])
    g_norm_w = din("g_norm_w", [1, 128]); a_re = din("ssm_a_re", [1, G, PS]); a_im = din("ssm_a_im", [1, G, PS])
    log_step = din("ssm_log_step", [1, G]); b_re = din("ssm_b_re", [1, G, PS, 16]); b_im = din("ssm_b_im", [1, G, PS, 16])
    c_re = din("ssm_c_re", [1, G, 16, PS]); c_im = din("ssm_c_im", [1, G, 16, PS]); ssm_d = din("ssm_d", [1, D])
    w_glu = din("w_glu", [1, D, D]); b_glu = din("b_glu", [1, D]); w_a = din("w_branch_a", [1, D, D])
    w_b = din("w_branch_b", [1, D, D]); w_out = din("w_out", [1, D, D]); norm2_w = din("norm2_w", [1, D])
    peer_wq = din("peer_wq", [1, D, 2048]); peer_k1 = din("peer_k1", [1, H, 128, 128]); peer_k2 = din("peer_k2", [1, H, 128, 128])
    peer_u = din("peer_u", [1, 16384, D]); peer_v = din("peer_v", [1, 16384, D]); final_norm_w = din("final_norm_w", [D])

    y_p = dout("y_p", [SEQ, D]); y_s = dout("y_s", [NSAMP * LS, D])
    hg_p = dout("hg_p", [H, 128, 128]); re_p = dout("re_p", [G, PS]); im_p = dout("im_p", [G, PS])
    hg_s = dout("hg_s", [NSAMP, H, 128, 128]); re_s = dout("re_s", [NSAMP * G, PS]); im_s = dout("im_s", [NSAMP * G, PS])

    with ExitStack() as st:
        P = Prog(nc)
        P.alloc_sems(st)

        def SB(name, shape, dt=F32):
            return Buf(st.enter_context(nc.sbuf_tensor(name, list(shape), dt)), name)

        ps2 = [Buf(st.enter_context(nc.psum_tensor("ps%d" % i, [128, 1024], F32)), "ps%d" % i) for i in range(4)]
        psc = [0]

        def PSN():
            psc[0] = psc[0] % 2 + 2
            return ps2[psc[0]]

        def R(bufs):
            return [b.r for b in bufs]

        def V(fn, r=(), w=()):
            P.op("dve", fn, r=R(r), w=R(w))

        def A(fn, r=(), w=()):
            P.op("act", fn, r=R(r), w=R(w))

        def GP(fn, r=(), w=()):
            P.op("pool", fn, r=R(r), w=R(w))

        def PE(fn, r=(), w=()):
            P.op("pe", fn, r=R(r), w=R(w))

        def DMA(fn, r=(), w=(), q="sp", out=False):
            P.dma(q, fn, r=R(r), w=R(w), is_output=out)

        def ld(dst_ap, src_ap, wbuf, q="sp", slow=False):
            if slow:
                DMA(lambda e: e.dma_start(out=dst_ap, in_=src_ap, allow_slow_non_contiguous=True), w=[wbuf], q=q)
            else:
                DMA(lambda e: e.dma_start(out=dst_ap, in_=src_ap), w=[wbuf], q=q)

        def stq(dst_ap, src_ap, rbuf, q="pool"):
            DMA(lambda e: e.dma_start(out=dst_ap, in_=src_ap), r=[rbuf], q=q, out=True)

        cst = SB("cst", [128, C_MQ])
        ident = cst[:, C_ID:C_ID + 128]
        n1w = SB("n1w", [128, D], BF16); n2w = SB("n2w", [128, D], BF16); fnw = SB("fnw", [128, D], BF16)
        lbt = SB("lbt", [128, D]); gnw8 = SB("gnw8", [128, D], BF16); dvec = SB("dvec", [128, D], BF16)
        bglu = SB("bglu", [128, 8])
        wb = [SB("wb%d" % i, [128, 8, 256], BF16) for i in range(3)]
        fb = [SB("fb%d" % i, [128, 8, 128], BF16) for i in range(3)]
        qT0, qT1 = fb[1], fb[2]
        wbc = [0]
        NTM = 13
        tm = [SB("tm%d" % i, [128, D]) for i in range(NTM)]
        fm = [None, SB("fm1", [128, 8, 128]), SB("fm2", [128, 8, 128]), None]

        Shg = SB("Shg", [128, H, 128])
        attm2 = [SB("attm0", [128, 128]), SB("attm1", [128, 128])]
        gdec = SB("gdec", [128, H, NSAMP])
        sm = SB("sm", [128, 48])
        Bmat = SB("Bmat", [128, 11, 2, 2, 128], BF16)
        ub16 = SB("ub16", [128, 11, 128], BF16)
        KT1 = SB("KT1", [128, 8, 128], BF16); KT2 = SB("KT2", [128, 8, 128], BF16)
        Cmat = SB("Cmat", [128, G, 16])
        ArAr = SB("ArAr", [128, 2, G]); AiPN = SB("AiPN", [128, 2, G])
        SBLK = 64
        SHb = SB("SHb", [128, SBLK, 2, G])

        class View:
            def __init__(self, ap, res):
                self.ap = ap
                self.r = res

            def __getitem__(self, k):
                return self.ap[k]

        _shf = SHb[:].rearrange("p t v g -> p (t v g)")
        Ssm = View(_shf[:, 0:2048].rearrange("p (s v) -> p s v", v=128), SHb.r)
        Zq = View(_shf[:, 2048:3072].rearrange("p (s t) -> p s t", t=64), SHb.r)
        Zv = View(_shf[0:64, 3072:4096].rearrange("p (s v) -> p s v", v=128), SHb.r)
        Scar = SB("Scar", [128, 1, 2, G])
        sT1 = SB("sT1", [128, 1, 2, G]); sT2 = SB("sT2", [128, 1, 2, G])
        sT2a = sT2; sT2b = Buf(sT2.t, "sT2b")

        block = st.enter_context(nc.Block())

        ld(cst[:], consts[:, 0:C_MQ], cst)
        ld(n1w[:], norm1_w[0, :].partition_broadcast(128), n1w, q="pool")
        ld(n2w[:], norm2_w[0, :].partition_broadcast(128), n2w, q="pool")
        ld(fnw[:], final_norm_w.partition_broadcast(128), fnw, q="pool")
        ld(dvec[:], ssm_d[0, :].partition_broadcast(128), dvec, q="pool")
        ld(lbt[:], lower_bounds[0, :].partition_broadcast(128), lbt)
        oml = tm[12]
        ld(oml[:], lower_bounds[1, :].partition_broadcast(128), oml)
        for h in range(H):
            ld(gnw8[:, h * 128:(h + 1) * 128], g_norm_w[0, :].partition_broadcast(128), gnw8, q="pool")
        ld(bglu[:], b_glu[0].rearrange("(j p) -> p j", p=128), bglu, slow=True)
        V(lambda e: e.tensor_tensor(out=lbt[:], in0=lbt[:], in1=oml[:], op=ALU.subtract), r=[lbt, oml], w=[lbt])
        A(lambda e: e.activation(out=lbt[:], in_=lbt[:], func=AF.Sigmoid), r=[lbt], w=[lbt])

        are, aim, lst, mag, ang, cs, sn, wk1, wk2, wk3 = [tm[i] for i in range(10)]
        for half in range(2):
            ld(are[half * 64:(half + 1) * 64, 0:G], a_re[0].rearrange("g p -> p g"), are, slow=True)
            ld(aim[half * 64:(half + 1) * 64, 0:G], a_im[0].rearrange("g p -> p g"), aim, slow=True)
        ld(lst[:, 0:G], log_step[0, :].partition_broadcast(128), lst)
        g64 = slice(0, G)
        A(lambda e: e.activation(out=lst[:, g64], in_=lst[:, g64], func=AF.Exp), r=[lst], w=[lst])
        V(lambda e: e.tensor_scalar_min(out=are[:, g64], in0=are[:, g64], scalar1=-1e-4), r=[are], w=[are])
        V(lambda e: e.tensor_tensor(out=mag[:, g64], in0=are[:, g64], in1=lst[:, g64], op=ALU.mult), r=[are, lst], w=[mag])
        A(lambda e: e.activation(out=mag[:, g64], in_=mag[:, g64], func=AF.Exp), r=[mag], w=[mag])
        V(lambda e: e.tensor_tensor(out=ang[:, g64], in0=aim[:, g64], in1=lst[:, g64], op=ALU.mult), r=[aim, lst], w=[ang])

        wki = SB("wki", [128, G], I32)

        def sin_of(dst, src, shift):
            V(lambda e: e.tensor_scalar(out=wk1[:, g64], in0=src[:, g64], scalar1=1.0 / TWO_PI, scalar2=shift / TWO_PI,
                                        op0=ALU.mult, op1=ALU.add), r=[src], w=[wk1])
            V(lambda e: e.tensor_copy(out=wki[:], in_=wk1[:, g64]), r=[wk1], w=[wki])
            V(lambda e: e.tensor_copy(out=wk2[:, g64], in_=wki[:]), r=[wki], w=[wk2])
            V(lambda e: e.tensor_tensor(out=wk1[:, g64], in0=wk1[:, g64], in1=wk2[:, g64], op=ALU.subtract), r=[wk1, wk2], w=[wk1])
            V(lambda e: e.tensor_scalar(out=wk2[:, g64], in0=wk1[:, g64], scalar1=0.5, scalar2=None, op0=ALU.is_gt), r=[wk1], w=[wk2])
            V(lambda e: e.tensor_tensor(out=wk1[:, g64], in0=wk1[:, g64], in1=wk2[:, g64], op=ALU.subtract), r=[wk1, wk2], w=[wk1])
            V(lambda e: e.tensor_scalar(out=wk2[:, g64], in0=wk1[:, g64], scalar1=-0.5, scalar2=None, op0=ALU.is_lt), r=[wk1], w=[wk2])
            V(lambda e: e.tensor_tensor(out=wk1[:, g64], in0=wk1[:, g64], in1=wk2[:, g64], op=ALU.add), r=[wk1, wk2], w=[wk1])
            V(lambda e: e.tensor_scalar(out=wk1[:, g64], in0=wk1[:, g64], scalar1=TWO_PI, scalar2=3.14159,
                                        op0=ALU.mult, op1=ALU.min), r=[wk1], w=[wk1])
            V(lambda e: e.tensor_scalar_max(out=wk1[:, g64], in0=wk1[:, g64], scalar1=-3.14159), r=[wk1], w=[wk1])
            A(lambda e: e.activation(out=dst[:, g64], in_=wk1[:, g64], func=AF.Sin), r=[wk1], w=[dst])

        sin_of(sn, ang, 0.0)
        sin_of(cs, ang, np.pi / 2)
        V(lambda e: e.tensor_tensor(out=cs[:, g64], in0=cs[:, g64], in1=mag[:, g64], op=ALU.mult), r=[cs, mag], w=[cs])
        V(lambda e: e.tensor_tensor(out=sn[:, g64], in0=sn[:, g64], in1=mag[:, g64], op=ALU.mult), r=[sn, mag], w=[sn])
        V(lambda e: e.tensor_copy(out=ArAr[:, 0, :], in_=cs[:, g64]), r=[cs], w=[ArAr])
        V(lambda e: e.tensor_copy(out=ArAr[:, 1, :], in_=cs[:, g64]), r=[cs], w=[ArAr])
        V(lambda e: e.tensor_copy(out=AiPN[:, 0, :], in_=sn[:, g64]), r=[sn], w=[AiPN])
        V(lambda e: e.tensor_scalar_mul(out=AiPN[:, 1, :], in0=sn[:, g64], scalar1=-1.0), r=[sn], w=[AiPN])
        V(lambda e: e.tensor_scalar_add(out=wk1[:, g64], in0=cs[:, g64], scalar1=-1.0), r=[cs], w=[wk1])
        V(lambda e: e.tensor_tensor(out=wk2[:, g64], in0=are[:, g64], in1=are[:, g64], op=ALU.mult), r=[are], w=[wk2])
        V(lambda e: e.tensor_tensor(out=wk3[:, g64], in0=aim[:, g64], in1=aim[:, g64], op=ALU.mult), r=[aim], w=[wk3])
        V(lambda e: e.tensor_tensor(out=wk2[:, g64], in0=wk2[:, g64], in1=wk3[:, g64], op=ALU.add), r=[wk2, wk3], w=[wk2])
        V(lambda e: e.reciprocal(out=wk2[:, g64], in_=wk2[:, g64]), r=[wk2], w=[wk2])
        cr, ci = mag, ang
        V(lambda e: e.tensor_tensor(out=cr[:, g64], in0=wk1[:, g64], in1=are[:, g64], op=ALU.mult), r=[wk1, are], w=[cr])
        V(lambda e: e.tensor_tensor(out=wk3[:, g64], in0=sn[:, g64], in1=aim[:, g64], op=ALU.mult), r=[sn, aim], w=[wk3])
        V(lambda e: e.tensor_tensor(out=cr[:, g64], in0=cr[:, g64], in1=wk3[:, g64], op=ALU.add), r=[cr, wk3], w=[cr])
        V(lambda e: e.tensor_tensor(out=cr[:, g64], in0=cr[:, g64], in1=wk2[:, g64], op=ALU.mult), r=[cr, wk2], w=[cr])
        V(lambda e: e.tensor_tensor(out=ci[:, g64], in0=sn[:, g64], in1=are[:, g64], op=ALU.mult), r=[sn, are], w=[ci])
        V(lambda e: e.tensor_tensor(out=wk3[:, g64], in0=wk1[:, g64], in1=aim[:, g64], op=ALU.mult), r=[wk1, aim], w=[wk3])
        V(lambda e: e.tensor_tensor(out=ci[:, g64], in0=ci[:, g64], in1=wk3[:, g64], op=ALU.subtract), r=[ci, wk3], w=[ci])
        V(lambda e: e.tensor_tensor(out=ci[:, g64], in0=ci[:, g64], in1=wk2[:, g64], op=ALU.mult), r=[ci, wk2], w=[ci])
        V(lambda e: e.tensor_copy(out=wk1[0:64, g64], in_=cr[0:64, g64]), r=[cr], w=[wk1])
        V(lambda e: e.tensor_copy(out=wk1[64:128, g64], in_=ci[64:128, g64]), r=[ci], w=[wk1])
        V(lambda e: e.tensor_scalar_mul(out=wk2[0:64, g64], in0=ci[0:64, g64], scalar1=-1.0), r=[ci], w=[wk2])
        V(lambda e: e.tensor_copy(out=wk2[64:128, g64], in_=cr[64:128, g64]), r=[cr], w=[wk2])
        BR2, BI2, V0, V1, VM = tm[10], tm[11], tm[12], tm[0], tm[1]
        for half in range(2):
            for gh in range(2):
                gs_ = slice(gh * 32, (gh + 1) * 32)
                ld(BR2[half * 64:(half + 1) * 64, gh * 512:(gh + 1) * 512].rearrange("p (g c) -> p g c", c=16),
                   b_re[0][gs_].rearrange("g p c -> p g c"), BR2, slow=True)
                ld(BI2[half * 64:(half + 1) * 64, gh * 512:(gh + 1) * 512].rearrange("p (g c) -> p g c", c=16),
                   b_im[0][gs_].rearrange("g p c -> p g c"), BI2, slow=True)

        def g3(buf):
            return buf[:].rearrange("p (g c) -> p g c", c=16)

        def bc16(buf):
            return buf[:, g64].unsqueeze(2).to_broadcast([128, G, 16])

        V(lambda e: e.tensor_tensor(out=g3(V0), in0=g3(BR2), in1=bc16(wk1), op=ALU.mult), r=[BR2, wk1], w=[V0])
        V(lambda e: e.tensor_tensor(out=g3(V1), in0=g3(BI2), in1=bc16(wk2), op=ALU.mult), r=[BI2, wk2], w=[V1])
        V(lambda e: e.tensor_tensor(out=V0[:], in0=V0[:], in1=V1[:], op=ALU.add), r=[V0, V1], w=[V0])
        V(lambda e: e.tensor_tensor(out=g3(V1), in0=g3(BR2), in1=bc16(wk2), op=ALU.mult), r=[BR2, wk2], w=[V1])
        V(lambda e: e.tensor_tensor(out=g3(BR2), in0=g3(BI2), in1=bc16(wk1), op=ALU.mult), r=[BI2, wk1], w=[BR2])
        V(lambda e: e.tensor_tensor(out=V1[:], in0=V1[:], in1=BR2[:], op=ALU.subtract), r=[V1, BR2], w=[V1])
        for v, Vv in enumerate((V0, V1)):
            for gl in range(2):
                V(lambda e: e.memset(VM[:], 0.0), w=[VM])
                V(lambda e, Vv=Vv, gl=gl: e.tensor_copy(out=g3(VM)[:, gl::2, :], in_=g3(Vv)[:, gl::2, :]), r=[Vv], w=[VM])
                for j in range(11):
                    wd = min(96, 1024 - j * 96)
                    pp = PSN()
                    PE(lambda e, pp=pp, j=j, wd=wd: e.transpose(out=pp[:wd, 0:128], in_=VM[:, j * 96:j * 96 + wd], identity=ident),
                       r=[VM, cst], w=[pp])
                    A(lambda e, pp=pp, j=j, gl=gl, v=v, wd=wd: e.copy(out=Bmat[:wd, j, gl, v, :], in_=pp[:wd, 0:128]), r=[pp], w=[Bmat])
        Cn = tm[2]
        ld(Cn[:].rearrange("p (j q) -> p j q", q=128)[:, :, 0:64], c_re[0].rearrange("(j g) c p -> (g c) j p", j=8), Cn)
        ld(Cn[:].rearrange("p (j q) -> p j q", q=128)[:, :, 64:128], c_im[0].rearrange("(j g) c p -> (g c) j p", j=8), Cn)
        V(lambda e: e.tensor_scalar_mul(out=Cn[:].rearrange("p (j q) -> p j q", q=128)[:, :, 64:128],
                                        in0=Cn[:].rearrange("p (j q) -> p j q", q=128)[:, :, 64:128], scalar1=-1.0), r=[Cn], w=[Cn])
        for j in range(8):
            pp = PSN()
            PE(lambda e, pp=pp, j=j: e.transpose(out=pp[:, 0:128], in_=Cn[:, j * 128:(j + 1) * 128], identity=ident),
               r=[Cn, cst], w=[pp])
            A(lambda e, pp=pp, j=j: e.copy(out=Cmat[:, j * 8:(j + 1) * 8, :],
                                           in_=pp[:, 0:128].rearrange("p (g c) -> p g c", c=16)), r=[pp], w=[Cmat])

        if os.environ.get('KDUMP', '') == 'abar':
            dbg = dout("dbg", [128, 1024])
            stq(dbg[:, 0:64], ArAr[:, 0, :], ArAr)
            stq(dbg[:, 64:128], AiPN[:, 0, :], AiPN)
            stq(dbg[:, 128:192], wk1[:, g64], wk1)
            stq(dbg[:, 192:256], wk2[:, g64], wk2)
            stq(dbg[:, 512:768], Cmat[:, 0:16, :].rearrange("p g c -> p (g c)"), Cmat)
        KN1, KN2 = tm[8], tm[9]
        ld(KN1[:].rearrange("p (h d) -> p h d", d=128), peer_k1[0].rearrange("h n d -> n h d"), KN1)
        ld(KN2[:].rearrange("p (h d) -> p h d", d=128), peer_k2[0].rearrange("h n d -> n h d"), KN2)
        for KNx, KTx in ((KN1, KT1), (KN2, KT2)):
            ppk = PSN()
            for j in range(8):
                PE(lambda e, ppk=ppk, j=j, KNx=KNx: e.transpose(out=ppk[:, j * 128:(j + 1) * 128], in_=KNx[:, j * 128:(j + 1) * 128],
                                                              identity=ident), r=[KNx, cst], w=[ppk])
            A(lambda e, ppk=ppk, KTx=KTx: e.copy(out=KTx[:], in_=ppk[:, :].rearrange("p (j t) -> p j t", t=128)), r=[ppk], w=[KTx])
        WBASE = {}
        wlist = (("w_in", w_in, 28), ("w_a", w_a, 4), ("w_glu", w_glu, 4), ("w_b", w_b, 4), ("w_out", w_out, 4), ("wq", peer_wq, 8))
        ncg_tot = sum(x[2] for x in wlist)
        wsc = nc.dram_tensor("w_scratch", [ncg_tot, 128, 2048], BF16, kind="Internal").ap()
        wsc_res = [Buf(None, "wsc%d" % i_) for i_ in range(ncg_tot)]
        WSRC = {}
        gi_ = 0
        for nm_, Wd_, ncg_ in wlist:
            WBASE[nm_] = gi_
            WSRC[nm_] = Wd_
            gi_ += ncg_
        wsc_conv = set()
        wsc_seen = set()
        ubt = nc.dram_tensor("ub_scratch", [16384, 1024], BF16, kind="Internal").ap()
        vbt = nc.dram_tensor("vb_scratch", [16384, 1024], BF16, kind="Internal").ap()
        uvb_res = [Buf(None, "uvb%d" % c) for c in range(128)]

        def uvb_gen():
            for c in range(128):
                stg = PG[c % 6]
                sv = stg[:].bitcast(BF16)
                DMA(lambda e, sv=sv, c=c: e.dma_start(out=sv[:, 0:1024], in_=peer_u[0][c * 128:(c + 1) * 128, :]), w=[stg, PGU[c % 6], PGV[c % 6]], q="pool")
                DMA(lambda e, sv=sv, c=c: e.dma_start(out=sv[:, 1024:2048], in_=peer_v[0][c * 128:(c + 1) * 128, :]), w=[stg], q="pool")
                DMA(lambda e, sv=sv, c=c: e.dma_start(out=ubt[c * 128:(c + 1) * 128, :], in_=sv[:, 0:1024]), r=[stg], w=[uvb_res[c]], q="sp")
                DMA(lambda e, sv=sv, c=c: e.dma_start(out=vbt[c * 128:(c + 1) * 128, :], in_=sv[:, 1024:2048]), r=[stg], w=[uvb_res[c]], q="sp")
                yield "u"
        uvb_waited = [False]
        def transpose8(dst_fm, src_tm, n, nblk=8, evac="act"):
            pp = PSN()
            for j in range(nblk):
                PE(lambda e, pp=pp, j=j: e.transpose(out=pp[:, j * n:(j + 1) * n], in_=src_tm[:n, j * 128:(j + 1) * 128],
                                                     identity=cst[:n, C_ID:C_ID + n]), r=[src_tm, cst], w=[pp])
            src = pp[:, 0:nblk * n].rearrange("p (j t) -> p j t", t=n)
            if evac == "act":
                A(lambda e: e.copy(out=dst_fm[:, 0:nblk, :n], in_=src), r=[pp], w=[dst_fm])
            else:
                V(lambda e: e.tensor_copy(out=dst_fm[:, 0:nblk, :n], in_=src), r=[pp], w=[dst_fm])

        def load_w(wname, cg):
            i = wbc[0]
            wbc[0] = (i + 1) % 3
            w = wb[i]
            gidx = WBASE[wname] + cg
            if gidx not in wsc_conv:
                wsc_conv.add(gidx)
                Wd_ = WSRC[wname]
                DMA(lambda e, w=w, Wd_=Wd_, cg=cg: e.dma_start(out=w[:], in_=Wd_[0][:, cg * 256:(cg + 1) * 256].rearrange("(k p) c -> p k c", p=128)),
                    w=[w], q="pool")
                DMA(lambda e, w=w, gidx=gidx: e.dma_start(out=wsc[gidx], in_=w[:].rearrange("p k c -> p (k c)")), r=[w], w=[wsc_res[gidx]], q="sp")
                return w
            extra = [] if gidx in wsc_seen else [wsc_res[gidx]]
            wsc_seen.add(gidx)
            DMA(lambda e, w=w, gidx=gidx: e.dma_start(out=w[:].rearrange("p k c -> p (k c)"), in_=wsc[gidx]), r=extra, w=[w], q="sp")
            return w

        def proj_tm(lhsT, n, Wd, cgs, evac):
            for cg in cgs:
                w = load_w(Wd, cg)
                pp = PSN()
                for k in range(8):
                    PE(lambda e, pp=pp, w=w, k=k: e.matmul(pp[:n, 0:256], lhsT=lhsT[:, k, :n], rhs=w[:, k, :],
                                                           start=(k == 0), stop=(k == 7)), r=[lhsT, w], w=[pp])
                evac(cg, pp)
                yield "y"

        def rmsnorm_tm(dst, src, n, wbuf_):
            A(lambda e: e.activation(out=tm[NTM - 1][:n, :], in_=src[:n, :], func=AF.Square, accum_out=sm[:n, 0:1]),
              r=[src], w=[tm[NTM - 1], sm])
            A(lambda e: e.activation(out=sm[:n, 1:2], in_=sm[:n, 0:1], func=AF.Sqrt, scale=1.0 / D, bias=EPS), r=[sm], w=[sm])
            V(lambda e: e.reciprocal(out=sm[:n, 2:3], in_=sm[:n, 1:2]), r=[sm], w=[sm])
            V(lambda e: e.scalar_tensor_tensor(out=dst[:n, :], in0=src[:n, :], scalar=sm[:n, 2:3], in1=wbuf_[:n, :],
                                               op0=ALU.mult, op1=ALU.mult), r=[src, sm, wbuf_], w=[dst])

        PO = PY = ps2[0]
        PA = ps2[1]
        PX = SB("PX", [128, D]); PXN = SB("PXN", [128, D]); PS1 = SB("PS1", [128, D]); POH = SB("POH", [128, D])
        PG = [SB("PG%d" % i, [128, D]) for i in range(6)]
        PGU = [Buf(None, "PGU%d" % i) for i in range(6)]
        PGV = [Buf(None, "PGV%d" % i) for i in range(6)]
        pfb = SB("pfb", [128, 8, 128], BF16)
        tv = SB("tv", [128, 256]); ti = SB("ti", [128, 256], U32); tif = tv
        tv2 = SB("tv2", [128, 128]); ti2 = SB("ti2", [128, 128], U32)
        sc2 = SB("sc2", [128, 5, 128])
        eidf = SB("eidf", [128, 128]); eidi = SB("eidi", [128, 128], I32)
        gate = SB("gate", [128, 128]); dots = SB("dots", [128, 128]); coef = SB("coef", [128, 128])
        wkm = SB("wkm", [128, 256])
        DG = [SB("dg%d" % i, [128, 128], BF16) for i in range(2)]
        V(lambda e: e.memset(eidi[:], 0), w=[eidi])
        V(lambda e: e.memset(Shg[:], 0.0), w=[Shg])
        V(lambda e: e.memset(Scar[:], 0.0), w=[Scar])
        iota16 = cst[:, C_IO:C_IO + 16]

        def layer_gen(kind, idx):
            if kind == "meta":
                n, nseq, xsrc = NMETA, 1, meta
            elif kind == "prompt":
                n, nseq, xsrc = 128, 1, xp[idx * 128:(idx + 1) * 128, :]
            else:
                n, nseq, xsrc = NSAMP * LS, NSAMP, xs
            full = kind != "meta"
            if nseq == 1:
                Matt = cst[:n, C_TRI:C_TRI + n]; Tot = cst[:n, C_ONE:C_ONE + n]; M2 = cst[:n, C_ONE:C_ONE + 1]
            else:
                Matt = cst[:n, C_MS:C_MS + n]; Tot = cst[:n, C_TS:C_TS + n]; M2 = cst[:n, C_M2:C_M2 + 16]
            X, Hn, Q, KK, LG, VV, GS, U, SGA, SGB, BS, KT_, SCR = tm
            ld(X[:n, :], xsrc, X)
            rmsnorm_tm(Hn, X, n, n1w)
            transpose8(fb[0], Hn, n)
            segs = [(Q, AF.Silu), (KK, AF.Sigmoid), (VV, AF.Copy), (GS, AF.Silu), (U, AF.Copy), (SGA, AF.Sigmoid), (SGB, AF.Sigmoid)]

            def evac_in(cg, pp):
                dst, fn_ = segs[cg // 4]
                c0 = (cg % 4) * 256
                A(lambda e: e.activation(out=dst[:n, c0:c0 + 256], in_=pp[:n, 0:256], func=fn_), r=[pp], w=[dst])

            yield from proj_tm(fb[0], n, "w_in", list(range(28)) if full else list(range(12)) + list(range(16, 20)), evac_in)
            yield "A"
            V(lambda e: e.tensor_scalar(out=LG[:n, :], in0=KK[:n, :], scalar1=-1.0, scalar2=1.0, op0=ALU.mult, op1=ALU.add), r=[KK], w=[LG])
            V(lambda e: e.tensor_tensor(out=LG[:n, :], in0=LG[:n, :], in1=lbt[:n, :], op=ALU.mult), r=[LG, lbt], w=[LG])
            V(lambda e: e.tensor_tensor(out=KK[:n, :], in0=KK[:n, :], in1=LG[:n, :], op=ALU.add), r=[KK, LG], w=[KK])
            A(lambda e: e.activation(out=LG[:n, :], in_=KK[:n, :], func=AF.Ln), r=[KK], w=[LG])
            V(lambda e: e.tensor_scalar(out=KK[:n, :], in0=KK[:n, :], scalar1=-1.0, scalar2=1.0, op0=ALU.mult, op1=ALU.add),
              r=[KK], w=[KK])
            pb = PSN()
            for hf in range(2):
                PE(lambda e, hf=hf: e.matmul(pb[:n, hf * 512:(hf + 1) * 512], lhsT=Matt, rhs=LG[:n, hf * 512:(hf + 1) * 512],
                                             start=True, stop=True), r=[cst, LG], w=[pb])
            A(lambda e: e.copy(out=BS[:n, :], in_=pb[:n, :]), r=[pb], w=[BS])
            pl = PSN()
            for hf in range(2):
                PE(lambda e, hf=hf: e.matmul(pl[:n, hf * 512:(hf + 1) * 512], lhsT=Tot, rhs=LG[:n, hf * 512:(hf + 1) * 512],
                                             start=True, stop=True), r=[cst, LG], w=[pl])
            A(lambda e: e.activation(out=KT_[:n, :], in_=BS[:n, :], func=AF.Exp, scale=-1.0), r=[BS], w=[KT_])
            V(lambda e: e.tensor_tensor(out=KT_[:n, :], in0=KT_[:n, :], in1=KK[:n, :], op=ALU.mult), r=[KT_, KK], w=[KT_])
            V(lambda e: e.tensor_tensor(out=SCR[:n, :], in0=pl[:n, :], in1=BS[:n, :], op=ALU.subtract), r=[pl, BS], w=[SCR])
            A(lambda e: e.activation(out=SCR[:n, :], in_=SCR[:n, :], func=AF.Exp), r=[SCR], w=[SCR])
            V(lambda e: e.tensor_tensor(out=KK[:n, :], in0=KK[:n, :], in1=SCR[:n, :], op=ALU.mult), r=[KK, SCR], w=[KK])
            A(lambda e: e.activation(out=BS[:n, :], in_=BS[:n, :], func=AF.Exp), r=[BS], w=[BS])
            V(lambda e: e.tensor_tensor(out=Q[:n, :], in0=Q[:n, :], in1=BS[:n, :], op=ALU.mult), r=[Q, BS], w=[Q])
            pg = PSN()
            for h in range(H):
                PE(lambda e, h=h: e.matmul(pg[:, h * 16:h * 16 + nseq], lhsT=LG[:n, h * 128:(h + 1) * 128], rhs=M2,
                                           start=True, stop=True), r=[LG, cst], w=[pg])
            A(lambda e: e.activation(out=gdec[:, :, 0:nseq], in_=pg[:, 0:128].rearrange("p (h s) -> p h s", s=16)[:, :, 0:nseq],
                                     func=AF.Exp), r=[pg], w=[gdec])
            transpose8(fm[1], Q, n)
            transpose8(fm[2], KT_, n, evac="dve")
            if nseq > 1:
                ld(Hn[:, :], consts[:, C_MQ:C_MQ + 1024], Hn)
            for h in range(H):
                yield "y"
                hs = slice(h * 128, (h + 1) * 128)
                pa = PSN()
                PE(lambda e, h=h, pa=pa: e.matmul(pa[:n, 0:n], lhsT=fm[2][:, h, :n], rhs=fm[1][:, h, :n], start=True, stop=True),
                   r=[fm[1], fm[2]], w=[pa])
                attm = attm2[h % 2]
                V(lambda e, pa=pa, attm=attm: e.tensor_tensor(out=attm[:n, :n], in0=pa[:n, 0:n], in1=Matt, op=ALU.mult), r=[pa, cst], w=[attm])
                if nseq == 1:
                    if full:
                        PE(lambda e, h=h, hs=hs: e.matmul(PO[:n, hs], lhsT=fm[1][:, h, :n], rhs=Shg[:, h, :], start=True, stop=False),
                           r=[fm[1], Shg], w=[PO])
                        PE(lambda e, hs=hs, attm=attm: e.matmul(PO[:n, hs], lhsT=attm[:n, :n], rhs=VV[:n, hs], start=False, stop=True),
                           r=[attm, VV], w=[PO])
                    pd = PSN()
                    PE(lambda e, hs=hs, pd=pd: e.matmul(pd[:, 0:128], lhsT=KK[:n, hs], rhs=VV[:n, hs], start=True, stop=True),
                       r=[KK, VV], w=[pd])
                    V(lambda e, h=h, pd=pd: e.scalar_tensor_tensor(out=Shg[:, h, :], in0=Shg[:, h, :], scalar=gdec[:, h, 0:1],
                                                                   in1=pd[:, 0:128], op0=ALU.mult, op1=ALU.add),
                      r=[Shg, gdec, pd], w=[Shg])
                else:
                    ld(Ssm[:], st_hg[:, h].rearrange("s k v -> k s v"), Ssm)
                    V(lambda e, h=h: e.tensor_tensor(out=Zq[:], in0=fm[1][:, h, :n].unsqueeze(1).to_broadcast([128, NSAMP, n]),
                                                     in1=Hn[:, :].rearrange("p (s t) -> p s t", t=64), op=ALU.mult),
                      r=[fm[1], Hn], w=[Zq])
                    for s_ in range(NSAMP):
                        PE(lambda e, s_=s_, hs=hs: e.matmul(PO[:n, hs], lhsT=Zq[:, s_, :], rhs=Ssm[:, s_, :], start=(s_ == 0), stop=False),
                           r=[Zq, Ssm], w=[PO])
                    PE(lambda e, hs=hs, attm=attm: e.matmul(PO[:n, hs], lhsT=attm[:n, :n], rhs=VV[:n, hs], start=False, stop=True),
                       r=[attm, VV], w=[PO])
                    for half in range(2):
                        V(lambda e, hs=hs, half=half: e.tensor_tensor(
                            out=Zv[:], in0=VV[:n, hs].unsqueeze(1).to_broadcast([n, 8, 128]),
                            in1=cst[:n, C_M2 + half * 8:C_M2 + half * 8 + 8].unsqueeze(2).to_broadcast([n, 8, 128]), op=ALU.mult),
                          r=[VV, cst], w=[Zv])
                        pd = PSN()
                        for qq in range(2):
                            PE(lambda e, hs=hs, pd=pd, qq=qq: e.matmul(
                                pd[:, qq * 512:(qq + 1) * 512], lhsT=KK[:n, hs],
                                rhs=Zv[:, qq * 4:(qq + 1) * 4, :].rearrange("p s v -> p (s v)"), start=True, stop=True),
                               r=[KK, Zv], w=[pd])
                        for sl in range(8):
                            s_ = half * 8 + sl
                            V(lambda e, h=h, s_=s_, sl=sl, pd=pd: e.scalar_tensor_tensor(
                                out=Ssm[:, s_, :], in0=Ssm[:, s_, :], scalar=gdec[:, h, s_:s_ + 1],
                                in1=pd[:, sl * 128:(sl + 1) * 128], op0=ALU.mult, op1=ALU.add), r=[Ssm, gdec, pd], w=[Ssm])
                    stq(hg_s[:, h].rearrange("s k v -> k s v"), Ssm[:], Ssm)
            if kind == "prompt" and idx == NPT - 1:
                stq(hg_p.rearrange("h k v -> k h v"), Shg[:], Shg)
            if STAGE <= 1 and full:
                return
            if full:
                OG = BS
                A(lambda e: e.activation(out=SCR[:n, :], in_=PO[:n, :], func=AF.Square), r=[PO], w=[SCR])
                V(lambda e: e.tensor_reduce(out=sm[:n, 8:16], in_=SCR[:n, :].rearrange("p (h v) -> p h v", v=128), axis=AX.X, op=ALU.add),
                  r=[SCR], w=[sm])
                A(lambda e: e.activation(out=sm[:n, 16:24], in_=sm[:n, 8:16], func=AF.Sqrt, scale=1.0 / 128, bias=EPS), r=[sm], w=[sm])
                V(lambda e: e.reciprocal(out=sm[:n, 24:32], in_=sm[:n, 16:24]), r=[sm], w=[sm])
                V(lambda e: e.tensor_tensor(out=OG[:n, :].rearrange("p (h v) -> p h v", v=128),
                                            in0=PO[:n, :].rearrange("p (h v) -> p h v", v=128),
                                            in1=sm[:n, 24:32].unsqueeze(2).to_broadcast([n, H, 128]), op=ALU.mult), r=[PO, sm], w=[OG])
                V(lambda e: e.tensor_tensor(out=OG[:n, :], in0=OG[:n, :], in1=gnw8[:n, :], op=ALU.mult), r=[OG, gnw8], w=[OG])
                V(lambda e: e.tensor_tensor(out=OG[:n, :], in0=OG[:n, :], in1=GS[:n, :], op=ALU.mult), r=[OG, GS], w=[OG])
                transpose8(fb[0], OG, n)
                BRA = Hn

                def evac_a(cg, pp):
                    A(lambda e: e.copy(out=BRA[:n, cg * 256:(cg + 1) * 256], in_=pp[:n, 0:256]), r=[pp], w=[BRA])
                yield from proj_tm(fb[0], n, "w_a", [0, 1, 2, 3], evac_a)

            if STAGE <= 2 and full:
                return
            for jr in (range(0, 8), range(8, 11)):
                pp = PSN()
                for j in jr:
                    wd = min(96, 1024 - j * 96)
                    PE(lambda e, pp=pp, j=j, wd=wd: e.transpose(out=pp[:wd, (j % 8) * n:(j % 8 + 1) * n], in_=U[:n, j * 96:j * 96 + wd],
                                                             identity=cst[:n, C_ID:C_ID + n]), r=[U, cst], w=[pp])
                nbk_ = len(jr)
                j0_ = jr[0]
                A(lambda e, pp=pp, nbk_=nbk_, j0_=j0_: e.copy(out=ub16[:96, j0_:j0_ + nbk_, :n],
                                                             in_=pp[:96, 0:nbk_ * n].rearrange("p (j t) -> p j t", t=n)), r=[pp], w=[ub16])
            T1, T2 = LG, VV
            DBG = os.environ.get('KDBG', '')
            if DBG == 'a' and full:
                return
            if nseq > 1:
                AIN, BIN = Q, KK
                S0s = [KT_, SCR]
                a3 = AIN[:].rearrange("p (j q) -> p j q", q=128)
                b3 = BIN[:].rearrange("p (j q) -> p j q", q=128)
                ld(a3[:, :, 0:64], st_re.rearrange("(j r) p -> r j p", r=128), AIN)
                ld(a3[:, :, 64:128], st_im.rearrange("(j r) p -> r j p", r=128), AIN)
                V(lambda e: e.tensor_scalar_mul(out=b3[:, :, 0:64], in0=a3[:, :, 64:128], scalar1=-1.0), r=[AIN], w=[BIN])
                V(lambda e: e.tensor_copy(out=b3[:, :, 64:128], in_=a3[:, :, 0:64]), r=[AIN], w=[BIN])
                for jj in range(8):
                    S0 = S0s[jj // 4]
                    for v, src in enumerate((AIN, BIN)):
                        pp = PSN()
                        PE(lambda e, pp=pp, src=src, jj=jj: e.transpose(out=pp[:, 0:128], in_=src[:, jj * 128:(jj + 1) * 128],
                                                                        identity=ident), r=[src, cst], w=[pp])
                        A(lambda e, pp=pp, jj=jj, v=v, S0=S0: e.copy(
                            out=S0[:].rearrange("p (s v g) -> p s v g", v=2, g=G)[:, 2 * (jj % 4):2 * (jj % 4) + 2, v, :],
                            in_=pp[:, 0:128].rearrange("p (s g) -> p s g", g=G)), r=[pp], w=[S0])
            SCB = int(os.environ.get('KNB', SBLK)) if nseq == 1 else SBLK
            yield "B"
            nblk = (n + SCB - 1) // SCB
            for bi in range(nblk):
                b0 = bi * SCB
                nb = min(SCB, n - b0)
                for q8 in range(8):
                    pp = PSN()
                    last_rr = -1
                    for gi in sorted(range(8), key=lambda gi_: ((q8 * 8 + gi_) // 2) % 3):
                        g = q8 * 8 + gi
                        pq_ = g // 2
                        gl = g % 2
                        jb, rr = divmod(pq_, 3)
                        srcf = ub16
                        if rr != last_rr or os.environ.get('KDRAIN', ''):
                            P.pe_drain()
                            last_rr = rr
                        for v in range(2):
                            slot = v * 8 + gi
                            PE(lambda e, pp=pp, slot=slot, rr=rr, jb=jb, gl=gl, v=v, srcf=srcf: e.matmul(
                                pp[:, slot * nb:(slot + 1) * nb], lhsT=Bmat[32 * rr:32 * rr + 32, jb, gl, v, :],
                                rhs=srcf[32 * rr:32 * rr + 32, jb, b0:b0 + nb], start=True, stop=True), r=[Bmat, srcf], w=[pp])
                    A(lambda e, pp=pp, q8=q8: e.copy(
                        out=SHb[:, 0:nb, :, q8 * 8:(q8 + 1) * 8].rearrange("p t v g -> p v g t"),
                        in_=pp[:, 0:16 * nb].rearrange("p (v g t) -> p v g t", v=2, g=8)), r=[pp], w=[SHb])
                    yield "q"
                if nseq == 1:
                    steps = [(None if t == 0 else SHb[:, t - 1:t, :, :], SHb[:, t:t + 1, :, :], 1, Scar[:, 0:1, :, :], Scar)
                             for t in range(nb)]
                else:
                    shv = SHb[:, 0:nb, :, :].rearrange("p (s l) v g -> p s l v g", l=LS)
                    steps = []
                    for sbt in range(2):
                        S0 = S0s[sbt]
                        for l in range(LS):
                            steps.append((None if l == 0 else shv[:, 8 * sbt:8 * sbt + 8, l - 1, :, :], shv[:, 8 * sbt:8 * sbt + 8, l, :, :], 8,
                                          S0[:].rearrange("p (s v g) -> p s v g", v=2, g=G), S0))
                if DBG == 'b' and full:
                    steps = []
                if DBG == 'c' and full:
                    steps = steps[:2]
                for (prev, cur, nbat, prev0, prev0_buf) in steps:
                    pbuf = SHb
                    if prev is None:
                        prev, pbuf = prev0, prev0_buf
                    if nbat == 1:
                        t1v = sT1[:, 0:1, :, :]; t2v = sT2[:, 0:1, :, :]
                        rT1, rT2a, rT2b = sT1, sT2a, sT2b
                    else:
                        t1v = T1[:, 0:nbat * 128].rearrange("p (s v g) -> p s v g", v=2, g=G)
                        t2v = T2[:, 0:nbat * 128].rearrange("p (s v g) -> p s v g", v=2, g=G)
                        rT1, rT2a, rT2b = T1, T2, T2
                    V(lambda e, prev=prev, t1v=t1v, nbat=nbat: e.tensor_tensor(
                        out=t1v, in0=prev, in1=ArAr[:].unsqueeze(1).to_broadcast([128, nbat, 2, G]), op=ALU.mult),
                      r=[pbuf, ArAr], w=[rT1])
                    V(lambda e, prev=prev, t2v=t2v, nbat=nbat: e.tensor_tensor(
                        out=t2v, in0=prev[:, :, ::-1, :], in1=AiPN[:].unsqueeze(1).to_broadcast([128, nbat, 2, G]), op=ALU.mult),
                      r=[pbuf, AiPN], w=[rT2a, rT2b])
                    V(lambda e, cur=cur, t1v=t1v: e.tensor_tensor(out=cur, in0=cur, in1=t1v, op=ALU.add), r=[SHb, rT1], w=[SHb])
                    yield "S1"
                    V(lambda e, cur=cur, t2v=t2v: e.tensor_tensor(out=cur, in0=cur, in1=t2v, op=ALU.add), r=[SHb, rT2a, rT2b], w=[SHb])
                    yield "S"
                if nseq == 1:
                    V(lambda e, nb=nb: e.tensor_copy(out=Scar[:, 0:1, :, :], in_=SHb[:, nb - 1:nb, :, :]), r=[SHb], w=[Scar])
                    if os.environ.get('KDUMP', '') == 'blk' and kind == 'prompt':
                        if bi == 0:
                            dU = dout("dbgU", [128, 1024]); stq(dU, U[:, :], U)
                        dbgb = dout("dbg%d" % bi, [128, 64 * 128])
                        stq(dbgb, SHb[:].rearrange("p t v g -> p (t v g)"), SHb)
                else:
                    FS = GS
                    FSX = Q
                    V(lambda e: e.tensor_copy(out=FSX[:, :].rearrange("p (s g) -> p s g", g=G),
                                              in_=SHb[:, 0:nb, 0, :].rearrange("p (s l) g -> p s l g", l=LS)[:, :, LS - 1, :]),
                      r=[SHb], w=[FSX])
                    for pr in range(8):
                        pp = PSN()
                        PE(lambda e, pp=pp, pr=pr: e.transpose(out=pp[:, 0:128], in_=FSX[:, pr * 128:(pr + 1) * 128],
                                                               identity=ident), r=[FSX, cst], w=[pp])
                        A(lambda e, pp=pp, pr=pr: e.copy(out=FS[:, pr * 128:(pr + 1) * 128], in_=pp[:, 0:128]), r=[pp], w=[FS])
                    fs3 = FS[:, :].rearrange("p (j q) -> p j q", q=128)
                    stq(re_s.rearrange("(j r) p -> r j p", r=128), fs3[:, :, 0:64], FS)
                    stq(im_s.rearrange("(j r) p -> r j p", r=128), fs3[:, :, 64:128], FS)
                if full and STAGE != 3:
                    for g in range(G):
                        PE(lambda e, g=g, b0=b0, nb=nb: e.matmul(PY[b0:b0 + nb, g * 16:(g + 1) * 16], lhsT=SHb[:, 0:nb, 0, g],
                                                                 rhs=Cmat[:, g, :], start=True, stop=True), r=[SHb, Cmat], w=[PY])
            if kind == "prompt" and idx == NPT - 1:
                pp = PSN()
                PE(lambda e, pp=pp: e.transpose(out=pp[:G, 0:128], in_=Scar[:, 0, 0, :], identity=ident), r=[Scar, cst], w=[pp])
                A(lambda e, pp=pp: e.copy(out=SCR[:G, 0:128], in_=pp[:G, 0:128]), r=[pp], w=[SCR])
                stq(re_p, SCR[:G, 0:64], SCR)
                stq(im_p, SCR[:G, 64:128], SCR)
            if not full:
                return
            if STAGE <= 3:
                return
            Y = BS
            V(lambda e: e.tensor_tensor(out=Y[:n, :], in0=U[:n, :], in1=dvec[:n, :], op=ALU.mult), r=[U, dvec], w=[Y])
            V(lambda e: e.tensor_tensor(out=Y[:n, :], in0=Y[:n, :], in1=PY[:n, :], op=ALU.add), r=[Y, PY], w=[Y])
            A(lambda e: e.activation(out=Y[:n, :], in_=Y[:n, :], func=AF.Gelu), r=[Y], w=[Y])
            transpose8(fb[1], Y, n)
            for cg in range(4):
                yield "y"
                w = load_w("w_glu", cg)
                for m in range(2):
                    pz = PSN()
                    for k in range(8):
                        PE(lambda e, pz=pz, w=w, k=k, m=m: e.matmul(pz[:, 0:n], lhsT=w[:, k, m * 128:(m + 1) * 128], rhs=fb[1][:, k, :n],
                                                                    start=(k == 0), stop=(k == 7)), r=[w, fb[1]], w=[pz])
                    A(lambda e, pz=pz, cg=cg, m=m: e.activation(out=fb[2][:, cg * 2 + m, :n], in_=pz[:, 0:n], func=AF.Sigmoid,
                                                                bias=bglu[:, cg * 2 + m:cg * 2 + m + 1]), r=[pz, bglu], w=[fb[2]])
            V(lambda e: e.tensor_tensor(out=fb[2][:, :, :n], in0=fb[2][:, :, :n], in1=fb[1][:, :, :n], op=ALU.mult), r=[fb[2], fb[1]], w=[fb[2]])
            MIX = KT_

            def evac_b(cg, pp):
                V(lambda e: e.tensor_tensor(out=MIX[:n, cg * 256:(cg + 1) * 256], in0=pp[:n, 0:256], in1=SGB[:n, cg * 256:(cg + 1) * 256],
                                            op=ALU.mult), r=[pp, SGB], w=[MIX])
            yield from proj_tm(fb[2], n, "w_b", [0, 1, 2, 3], evac_b)
            V(lambda e: e.tensor_tensor(out=BRA[:n, :], in0=BRA[:n, :], in1=SGA[:n, :], op=ALU.mult), r=[BRA, SGA], w=[BRA])
            V(lambda e: e.tensor_tensor(out=MIX[:n, :], in0=MIX[:n, :], in1=BRA[:n, :], op=ALU.add), r=[MIX, BRA], w=[MIX])
            transpose8(fb[0], MIX, n)

            def evac_o(cg, pp):
                V(lambda e: e.tensor_tensor(out=X[:n, cg * 256:(cg + 1) * 256], in0=X[:n, cg * 256:(cg + 1) * 256], in1=pp[:n, 0:256],
                                            op=ALU.add), r=[pp, X], w=[X])
            yield from proj_tm(fb[0], n, "w_out", [0, 1, 2, 3], evac_o)

            if STAGE <= 4:
                return
            yield "C"
            V(lambda e: e.tensor_copy(out=PX[:n, :], in_=X[:n, :]), r=[X], w=[PX])
            return

        def peer_gen(kind, idx, n):
            X = PX; XN = PXN; Hn = PS1; LG = PS1; VV = POH; KT_ = POH; SCR = POH
            rmsnorm_tm(XN, X, n, n2w)
            transpose8(pfb, XN, n)
            for cg in range(8):
                w = load_w("wq", cg)
                for m in range(2):
                    un = cg * 2 + m
                    pq = PSN()
                    for k in range(8):
                        PE(lambda e, pq=pq, w=w, k=k, m=m: e.matmul(pq[:, 0:n], lhsT=w[:, k, m * 128:(m + 1) * 128], rhs=pfb[:, k, :n],
                                                                    start=(k == 0), stop=(k == 7)), r=[w, pfb], w=[pq])
                    dstq = qT0 if un < 8 else qT1
                    A(lambda e, pq=pq, dstq=dstq, un=un: e.copy(out=dstq[:, un % 8, :n], in_=pq[:, 0:n]), r=[pq], w=[dstq])
            S1, S2 = LG, VV
            for half, (Sd, KTb) in enumerate(((S1, KT1), (S2, KT2))):
                for hh in range(2):
                    pp = PSN()
                    for h4 in range(4):
                        h = hh * 4 + h4
                        un = h * 2 + half
                        srcq = qT0 if un < 8 else qT1
                        PE(lambda e, pp=pp, h4=h4, h=h, srcq=srcq, un=un, KTb=KTb: e.matmul(
                            pp[:n, h4 * 128:(h4 + 1) * 128], lhsT=srcq[:, un % 8, :n], rhs=KTb[:, h, :], start=True, stop=True),
                           r=[srcq, KTb], w=[pp])
                    A(lambda e, pp=pp, hh=hh, Sd=Sd: e.copy(out=Sd[:n, hh * 512:(hh + 1) * 512], in_=pp[:n, 0:512]), r=[pp], w=[Sd])

            yield "a"

            def top16(src_ap, srcbuf, width, vals, valbuf, idxs, idxbuf):
                V(lambda e: e.max(out=vals[:, 0:8], in_=src_ap), r=[srcbuf], w=[valbuf])
                V(lambda e: e.match_replace(out=wkm[:n, 0:width], in_to_replace=vals[:, 0:8], in_values=src_ap, imm_value=-1e30),
                  r=[srcbuf, valbuf], w=[wkm])
                V(lambda e: e.max(out=vals[:, 8:16], in_=wkm[:n, 0:width]), r=[wkm], w=[valbuf])
                V(lambda e: e.max_index(out=idxs[:, 0:8], in_max=vals[:, 0:8], in_values=src_ap), r=[srcbuf, valbuf], w=[idxbuf])
                V(lambda e: e.max_index(out=idxs[:, 8:16], in_max=vals[:, 8:16], in_values=src_ap), r=[srcbuf, valbuf], w=[idxbuf])

            for h in range(H):
                for half, Sd in enumerate((S1, S2)):
                    un = h * 2 + half
                    top16(Sd[:n, h * 128:(h + 1) * 128], Sd, 128, tv[:n, un * 16:(un + 1) * 16], tv, ti[:n, un * 16:(un + 1) * 16], ti)
            CAND = KT_
            tv4 = tv[:n, :].rearrange("p (h two i) -> p h two i", two=2, i=16)
            for hh in range(2):
                V(lambda e, hh=hh: e.tensor_tensor(
                    out=CAND[:n, :].rearrange("p (h i j) -> p h i j", i=16, j=16),
                    in0=tv4[:, hh * 4:(hh + 1) * 4, 0, :].unsqueeze(3).to_broadcast([n, 4, 16, 16]),
                    in1=tv4[:, hh * 4:(hh + 1) * 4, 1, :].unsqueeze(2).to_broadcast([n, 4, 16, 16]), op=ALU.add), r=[tv], w=[CAND])
                for h4 in range(4):
                    h = hh * 4 + h4
                    top16(CAND[:n, h4 * 256:(h4 + 1) * 256], CAND, 256, tv2[:n, h * 16:(h + 1) * 16], tv2, ti2[:n, h * 16:(h + 1) * 16], ti2)
            V(lambda e: e.tensor_copy(out=tif[:n, :], in_=ti[:n, :]), r=[ti, tv], w=[tif])
            posf, i_f, j_f, e1f, e2f = (sc2[:n, i_, :] for i_ in range(5))
            V(lambda e: e.tensor_copy(out=posf, in_=ti2[:n, :]), r=[ti2], w=[sc2])
            OH = SCR
            oh3 = OH[:n, 0:2048 // 2].rearrange("p (a b) -> p a b", b=16) if False else None
            for hf in range(2):
                cs_ = slice(hf * 64, (hf + 1) * 64)
                o3 = OH[:n, :].rearrange("p (a b) -> p a b", b=16)
                V(lambda e, cs_=cs_, o3=o3: e.scalar_tensor_tensor(
                    out=o3, in0=posf[:, cs_].unsqueeze(2).to_broadcast([n, 64, 16]), scalar=1.0 / 16,
                    in1=iota16[:n, :].unsqueeze(1).to_broadcast([n, 64, 16]), op0=ALU.mult, op1=ALU.is_ge), r=[sc2, cst], w=[OH])
                V(lambda e, cs_=cs_, o3=o3: e.tensor_reduce(out=i_f[:, cs_], in_=o3, axis=AX.X, op=ALU.add), r=[OH], w=[sc2])
            V(lambda e: e.tensor_scalar_add(out=i_f, in0=i_f, scalar1=-1.0), r=[sc2], w=[sc2])
            V(lambda e: e.scalar_tensor_tensor(out=j_f, in0=i_f, scalar=-16.0, in1=posf, op0=ALU.mult, op1=ALU.add), r=[sc2], w=[sc2])
            tif4 = tif[:n, :].rearrange("p (h two i) -> p h two i", two=2, i=16)
            for which, (sel, dst) in enumerate(((i_f, e1f), (j_f, e2f))):
                for hf in range(2):
                    cs_ = slice(hf * 64, (hf + 1) * 64)
                    o3 = OH[:n, :].rearrange("p (a b) -> p a b", b=16)
                    V(lambda e, sel=sel, cs_=cs_, o3=o3: e.tensor_tensor(
                        out=o3, in0=sel[:, cs_].unsqueeze(2).to_broadcast([n, 64, 16]),
                        in1=iota16[:n, :].unsqueeze(1).to_broadcast([n, 64, 16]), op=ALU.is_equal), r=[sc2, cst], w=[OH])
                    o4 = OH[:n, :].rearrange("p (h k i) -> p h k i", k=16, i=16)
                    V(lambda e, o4=o4, hf=hf, which=which: e.tensor_tensor(
                        out=o4, in0=o4, in1=tif4[:, hf * 4:(hf + 1) * 4, which, :].unsqueeze(2).to_broadcast([n, 4, 16, 16]),
                        op=ALU.mult), r=[OH, tif], w=[OH])
                    V(lambda e, dst=dst, cs_=cs_, o3=o3: e.tensor_reduce(out=dst[:, cs_], in_=o3, axis=AX.X, op=ALU.add), r=[OH], w=[sc2])
            V(lambda e: e.scalar_tensor_tensor(out=eidf[:n, :], in0=e1f, scalar=128.0, in1=e2f, op0=ALU.mult, op1=ALU.add), r=[sc2], w=[eidf])
            V(lambda e: e.tensor_copy(out=eidi[:n, :], in_=eidf[:n, :]), r=[eidf], w=[eidi])
            tv23 = tv2[:n, :].rearrange("p (h k) -> p h k", k=16)
            g3_ = gate[:n, :].rearrange("p (h k) -> p h k", k=16)
            V(lambda e: e.tensor_tensor(out=g3_, in0=tv23, in1=tv23[:, :, 0:1].to_broadcast([n, H, 16]), op=ALU.subtract), r=[tv2], w=[gate])
            A(lambda e: e.activation(out=gate[:n, :], in_=gate[:n, :], func=AF.Exp), r=[gate], w=[gate])
            V(lambda e: e.tensor_reduce(out=sm[:n, 32:40], in_=g3_, axis=AX.X, op=ALU.add), r=[gate], w=[sm])
            V(lambda e: e.reciprocal(out=sm[:n, 40:48], in_=sm[:n, 32:40]), r=[sm], w=[sm])
            V(lambda e: e.tensor_tensor(out=g3_, in0=g3_, in1=sm[:n, 40:48].unsqueeze(2).to_broadcast([n, H, 16]), op=ALU.mult), r=[gate, sm], w=[gate])
            yield "b"
            extra0 = [] if uvb_waited[0] else list(uvb_res)
            uvb_waited[0] = True
            NK = H * TOPK

            def u_stream():
                extra = extra0
                for grp in range(NK // 2):
                    bufs = []
                    for kk_ in range(2):
                        hk = grp * 2 + kk_
                        gb = PG[hk % 6]
                        gv = gb[:].bitcast(BF16)[:, 0:1024]
                        DMA(lambda e, gv=gv, hk=hk: e.indirect_dma_start(out=gv[:n, :], out_offset=None, in_=ubt,
                                                                        in_offset=bass.IndirectOffsetOnAxis(ap=eidi[:n, hk:hk + 1], axis=0)),
                            r=[eidi] + extra, w=[PGU[hk % 6]], q="pool")
                        extra = []
                        bufs.append((gv, hk))
                    for gv, hk in bufs:
                        V(lambda e, gv=gv, hk=hk: e.scalar_tensor_tensor(out=SCR[:].bitcast(BF16)[:n, 0:1024], in0=gv[:n, :], scalar=1.0, in1=XN[:n, :],
                                                                         op0=ALU.mult, op1=ALU.mult, accum_out=dots[:n, hk:hk + 1]),
                          r=[PGU[hk % 6], XN], w=[SCR, dots])
                        yield "d"
                    gsl = slice(grp * 2, grp * 2 + 2)
                    A(lambda e, gsl=gsl: e.activation(out=coef[:n, gsl], in_=dots[:n, gsl], func=AF.Gelu), r=[dots], w=[coef])
                    V(lambda e, gsl=gsl: e.tensor_tensor(out=coef[:n, gsl], in0=coef[:n, gsl], in1=gate[:n, gsl], op=ALU.mult), r=[coef, gate], w=[coef])
                    yield "k"

            def v_stream():
                for hk in range(NK):
                    gb = PG[hk % 6]
                    gv = gb[:].bitcast(BF16)[:, 1024:2048]
                    DMA(lambda e, gv=gv, hk=hk: e.indirect_dma_start(out=gv[:n, :], out_offset=None, in_=vbt,
                                                                    in_offset=bass.IndirectOffsetOnAxis(ap=eidi[:n, hk:hk + 1], axis=0)),
                        r=[eidi], w=[PGV[hk % 6]], q="pool")
                    dg = DG[hk % 2]
                    A(lambda e, dg=dg, hk=hk: e.activation(out=dg[:n, :n], in_=cst[:n, C_ID:C_ID + n], func=AF.Copy, scale=coef[:n, hk:hk + 1]),
                      r=[cst, coef], w=[dg])
                    for hf in range(2):
                        PE(lambda e, dg=dg, gv=gv, hk=hk, hf=hf: e.matmul(PA[:n, hf * 512:(hf + 1) * 512], lhsT=dg[:n, :n],
                                                                          rhs=gv[:n, hf * 512:(hf + 1) * 512],
                                                                          start=(hk == 0), stop=(hk == NK - 1)), r=[dg, PGV[hk % 6]], w=[PA])
                    yield "v"

            US = u_stream(); VS = v_stream()
            u_done = 0
            v_done = 0
            u_alive = True
            while u_alive or v_done < NK:
                want = yield "pk"
                if want == "v" and v_done < u_done:
                    next(VS); v_done += 1
                elif u_alive:
                    m_ = next(US, None)
                    if m_ is None:
                        u_alive = False
                    elif m_ == "k":
                        u_done += 2
                elif v_done < NK:
                    next(VS); v_done += 1
            yield "kend"
            ACC = PA
            V(lambda e: e.tensor_tensor(out=X[:n, :], in0=X[:n, :], in1=ACC[:n, :], op=ALU.add), r=[X, ACC], w=[X])
            YO = Hn
            rmsnorm_tm(YO, X, n, fnw)
            if kind == "prompt":
                stq(y_p[idx * 128:(idx + 1) * 128, :], YO[:n, :], YO)
            else:
                stq(y_s, YO[:n, :], YO)

        DUMPS = os.environ.get('KDUMP', '')
        tiles = [("meta", 0)] + [("prompt", i) for i in range(NPT)] + ([("sample", 0)] if DO_SAMPLE else [])
        if STAGE == 0:
            tiles = []
        def run_to(gen, marks):
            if gen is None:
                return None
            for m_ in gen:
                if m_ in marks:
                    return m_
            return None

        def p_step(gen, want):
            try:
                m_ = gen.send(want)
            except StopIteration:
                return False
            return m_ == "pk"

        QQ = int(os.environ.get('KQQ', '4'))
        QY = int(os.environ.get('KQY', '6'))
        pend = None
        UG = uvb_gen()
        ug_alive = [True]

        def ug_step(k_):
            for _i in range(k_):
                if ug_alive[0] and next(UG, None) is None:
                    ug_alive[0] = False

        for kind, idx in tiles:
            L = layer_gen(kind, idx)
            nn = {"meta": NMETA, "prompt": 128, "sample": NSAMP * LS}[kind]
            p_loop = False
            if pend is not None:
                run_to(pend, ("a",))
            lm = run_to(L, ("A",))
            if pend is not None:
                p_loop = run_to(pend, ("pk",)) == "pk"
            while lm is not None:
                lm = run_to(L, ("y", "q", "A", "B", "S1", "S", "C"))
                if lm in ("y", "S"):
                    ug_step(1)
                if lm is None or lm == "C":
                    break
                if p_loop and lm != "S1":
                    if lm == "S":
                        p_loop = p_step(pend, "v")
                    else:
                        for _rep in range(QQ if lm == "q" else QY):
                            if p_loop:
                                p_loop = p_step(pend, "u")
            if pend is not None:
                while p_loop:
                    p_loop = p_step(pend, "v")
                run_to(pend, ("__end__",))
            pend = None
            if lm == "C":
                run_to(L, ("__end__",))
                if DO_PEER:
                    ug_step(128)
                    pend = peer_gen(kind, idx, nn)
        if pend is not None:
            run_to(pend, ("a",))
            p_loop = run_to(pend, ("pk",)) == "pk"
            tog = 0
            while p_loop:
                p_loop = p_step(pend, "v" if tog % 3 == 2 else "u")
                tog += 1
            run_to(pend, ("__end__",))
        P.finish()
        P.emit(block)
    return nc


_CACHE = {}


def kernel(**inp):
    f32 = lambda a: np.ascontiguousarray(np.asarray(a, dtype=np.float32))
    inp = {k: f32(v) for k, v in inp.items()}
    if "nc" not in _CACHE:
        _CACHE["nc"] = build_program()
    nc = _CACHE["nc"]
    consts = make_consts()
    shared = {k: inp[k] for k in ("lower_bounds", "norm1_w", "w_in", "g_norm_w", "ssm_a_re", "ssm_a_im", "ssm_log_step",
                                  "ssm_b_re", "ssm_b_im", "ssm_c_re", "ssm_c_im", "ssm_d", "w_glu", "b_glu", "w_branch_a",
                                  "w_branch_b", "w_out", "norm2_w", "peer_wq", "peer_k1", "peer_k2", "peer_u", "peer_v",
                                  "final_norm_w")}
    in_maps = []
    for c in range(NCORES):
        m = dict(shared)
        m["consts"] = consts
        m["xp"] = inp["x_prompt"][c]
        m["xs"] = f32(inp["x_sample"][c * NSAMP:(c + 1) * NSAMP].reshape(NSAMP * LS, D))
        m["meta"] = inp["meta_tokens"]
        m["st_hg"] = f32(inp["state_hgrn"][0, c * NSAMP:(c + 1) * NSAMP])
        m["st_re"] = f32(inp["state_ssm_re"][0, c * NSAMP:(c + 1) * NSAMP].reshape(NSAMP * G, PS))
        m["st_im"] = f32(inp["state_ssm_im"][0, c * NSAMP:(c + 1) * NSAMP].reshape(NSAMP * G, PS))
        in_maps.append(m)
    res = run_bass_kernel_spmd(nc, in_maps, core_ids=list(range(NCORES)))
    rs = res.results
    y_prompt = np.stack([rs[c]["y_p"] for c in range(NCORES)]).astype(np.float32)
    y_sample = np.concatenate([rs[c]["y_s"].reshape(NSAMP, LS, D) for c in range(NCORES)]).astype(np.float32)
    hg_p = np.stack([rs[c]["hg_p"] for c in range(NCORES)])[None].astype(np.float32)
    re_p = np.stack([rs[c]["re_p"] for c in range(NCORES)])[None].astype(np.float32)
    im_p = np.stack([rs[c]["im_p"] for c in range(NCORES)])[None].astype(np.float32)
    hg_s = np.concatenate([rs[c]["hg_s"] for c in range(NCORES)])[None].astype(np.float32)
    re_s = np.concatenate([rs[c]["re_s"].reshape(NSAMP, G, PS) for c in range(NCORES)])[None].astype(np.float32)
    im_s = np.concatenate([rs[c]["im_s"].reshape(NSAMP, G, PS) for c in range(NCORES)])[None].astype(np.float32)
    return (y_prompt, y_sample, hg_p, re_p, im_p, hg_s, re_s, im_s)
```
